# Optimizing a Trainium2 kernel written in Bass

```python
import jax
import jax.numpy as jnp
from jax import lax
import numpy as np

D_MODEL = 2048
BATCH = 2
SEQ = 4096
DEPTH = 2

D_FF = 5632
NORM_EPS = 1e-6
CONV_W = 1024
CONV_K = 31
LN_EPS = 1e-5
RWKV_HEADS = 16
RWKV_HEAD = 64
RWKV_W = RWKV_HEADS * RWKV_HEAD
LORA_W = 96
LORA_A = 96
LORA_G = 256
DECAY_SCALE = 0.6065306597
GN_EPS = 64e-5
RWKV_IN = 3 * RWKV_W + LORA_W + LORA_A + LORA_G
NSA_HEADS = 16
NSA_KV = 4
NSA_GQA = NSA_HEADS // NSA_KV
NSA_HEAD = 64
NSA_W = NSA_HEADS * NSA_HEAD
NSA_KV_W = NSA_KV * NSA_HEAD
CMP_LEN = 32
CMP_STRIDE = 16
CMP_HIDDEN = 256
SLC_LEN = 64
N_SLC = 16
WIN = 512
WIN_Q_BLOCK = 128
SLC_Q_BLOCK = 64
N_BRANCH = 3
IN_SIZES = (2 * CONV_W, RWKV_IN, NSA_W, 6 * NSA_KV_W, 3 * NSA_HEADS, N_BRANCH * D_MODEL)
D_IN = 2 * CONV_W + RWKV_IN + NSA_W + 6 * NSA_KV_W + 3 * NSA_HEADS + N_BRANCH * D_MODEL

kernel_name = 'hybrid_conv_rwkv7_nsa_macaron'


def _offsets(sizes):
    return [int(o) for o in np.cumsum(sizes)[:-1]]


def rms_norm(x, g):
    xf = x.astype(jnp.float32)
    y = xf * lax.rsqrt(jnp.mean(xf * xf, axis=-1, keepdims=True) + NORM_EPS)
    return (y * g.astype(jnp.float32)).astype(x.dtype)


def layer_norm(x, g, b, eps):
    xf = x.astype(jnp.float32)
    mu = jnp.mean(xf, axis=-1, keepdims=True)
    xc = xf - mu
    var = jnp.mean(xc * xc, axis=-1, keepdims=True)
    y = xc * lax.rsqrt(var + eps) * g.astype(jnp.float32) + b.astype(jnp.float32)
    return y.astype(x.dtype)


def swiglu(h, w13, w2):
    a, b = jnp.split(h @ w13, 2, axis=-1)
    return (jax.nn.silu(a) * b) @ w2


def masked_softmax(s, mask):
    s = jnp.where(mask, s.astype(jnp.float32), -jnp.inf)
    m = jnp.max(s, axis=-1, keepdims=True)
    m = jnp.where(jnp.isfinite(m), m, 0.0)
    e = jnp.exp(s - m)
    d = jnp.sum(e, axis=-1, keepdims=True)
    return e / jnp.where(d > 0, d, 1.0)


def conv_module(u, conv_w, conv_b, ln_g, ln_b):
    a, gate = jnp.split(u, 2, axis=-1)
    z = a * jax.nn.sigmoid(gate)
    z = lax.conv_general_dilated(
        z, conv_w[:, None, :], window_strides=(1,), padding=[(CONV_K - 1, 0)],
        dimension_numbers=('NWC', 'WIO', 'NWC'), feature_group_count=CONV_W) + conv_b
    z = layer_norm(z, ln_g, ln_b, LN_EPS)
    return jax.nn.silu(z)


def token_shift(p, mu):
    prev = jnp.pad(p, ((0, 0), (1, 0), (0, 0)))[:, :-1]
    return p + (prev - p) * mu


def wkv7_scan(r, w, k, v, kk, a):
    B, T, H, N = r.shape

    def step(S, inp):
        r_t, w_t, k_t, v_t, kk_t, a_t = inp
        sa = jnp.einsum('bhij,bhj->bhi', S, -kk_t)
        S = (S * w_t[:, :, None, :] + sa[..., None] * (kk_t * a_t)[:, :, None, :]
             + v_t[..., None] * k_t[:, :, None, :])
        y = jnp.einsum('bhij,bhj->bhi', S, r_t)
        return S, y

    xs = tuple(jnp.moveaxis(t, 1, 0) for t in (r, w, k, v, kk, a))
    S0 = jnp.zeros((B, H, N, N), jnp.float32)
    _, ys = lax.scan(step, S0, xs)
    return jnp.moveaxis(ys, 0, 1)


def rwkv7_mixer(u, w0, w_b, a0, a_b, g_b, k_k, k_a, r_k, lnx_g, lnx_b):
    B, T, _ = u.shape
    r, k, v, wl, al, gl = jnp.split(
        u, _offsets((RWKV_W, RWKV_W, RWKV_W, LORA_W, LORA_A, LORA_G)), axis=-1)
    w = jnp.exp(-DECAY_SCALE * jax.nn.sigmoid(w0 + jnp.tanh(wl) @ w_b))
    a = jax.nn.sigmoid(a0 + al @ a_b)
    g = jax.nn.sigmoid(gl) @ g_b
    kk = k * k_k
    k = k * (1.0 + (a - 1.0) * k_a)
    shp = (B, T, RWKV_HEADS, RWKV_HEAD)
    r, w, k, v, kk, a = [t.reshape(shp).astype(jnp.float32) for t in (r, w, k, v, kk, a)]
    kk = kk * lax.rsqrt(jnp.sum(kk * kk, axis=-1, keepdims=True) + 1e-12)
    y = wkv7_scan(r, w, k, v, kk, a)
    y = layer_norm(y, lnx_g.reshape(RWKV_HEADS, RWKV_HEAD), lnx_b.reshape(RWKV_HEADS, RWKV_HEAD), GN_EPS)
    y = y + jnp.sum(r * k * r_k.astype(jnp.float32), axis=-1, keepdims=True) * v
    return y.reshape(B, T, RWKV_W).astype(u.dtype) * g


def nsa_mixer(q, kv, gate, pe_k, pe_v, ck_w1, ck_w2, cv_w1, cv_w2):
    B, T, _ = q.shape
    q = q.reshape(B, T, NSA_KV, NSA_GQA, NSA_HEAD) * (NSA_HEAD ** -0.5)
    k_c, v_c, k_s, v_s, k_w, v_w = [t.reshape(B, T, NSA_KV, NSA_HEAD) for t in jnp.split(kv, 6, axis=-1)]
    t_pos = jnp.arange(T)

    n_cmp = (T - CMP_LEN) // CMP_STRIDE + 1
    cmp_idx = np.arange(n_cmp)[:, None] * CMP_STRIDE + np.arange(CMP_LEN)[None, :]

    def compress(z, pe, w1, w2):
        blk = z[:, cmp_idx] + pe[None, None, :, None, :]
        blk = jnp.moveaxis(blk, 3, 2).reshape(B, n_cmp, NSA_KV, CMP_LEN * NSA_HEAD)
        return jax.nn.gelu(blk @ w1) @ w2

    kc = compress(k_c, pe_k, ck_w1, ck_w2)
    vc = compress(v_c, pe_v, cv_w1, cv_w2)
    cmp_start = jnp.arange(n_cmp) * CMP_STRIDE
    cmp_end = cmp_start + CMP_LEN - 1
    p_cmp = masked_softmax(jnp.einsum('btkgd,bnkd->bkgtn', q, kc), cmp_end[None, :] <= t_pos[:, None])
    o_cmp = jnp.einsum('bkgtn,bnkd->btkgd', p_cmp.astype(q.dtype), vc)

    n_slc = T // SLC_LEN
    n_sel = min(N_SLC, n_slc)
    slc_start = jnp.arange(n_slc) * SLC_LEN
    overlap = ((cmp_start[:, None] <= slc_start[None, :] + SLC_LEN - 1)
               & (cmp_end[:, None] >= slc_start[None, :])).astype(jnp.float32)
    imp = jnp.einsum('bkgtn,nj->bktj', p_cmp, overlap)
    cur = (t_pos // SLC_LEN)[:, None]
    j = jnp.arange(n_slc)[None, :]
    imp = jnp.where((j == 0) | (j == cur) | (j == cur - 1), jnp.inf, imp)
    imp = jnp.where(j > cur, -jnp.inf, imp)
    top_val, top_idx = lax.top_k(imp, n_sel)
    top_ok = top_val > -jnp.inf

    q_t = jnp.transpose(q, (0, 2, 3, 1, 4))
    k_blk = jnp.transpose(k_s, (0, 2, 1, 3)).reshape(B, NSA_KV, n_slc, SLC_LEN, NSA_HEAD)
    v_blk = jnp.transpose(v_s, (0, 2, 1, 3)).reshape(B, NSA_KV, n_slc, SLC_LEN, NSA_HEAD)
    n_qb = T // SLC_Q_BLOCK
    q_chunks = jnp.moveaxis(q_t.reshape(B, NSA_KV, NSA_GQA, n_qb, SLC_Q_BLOCK, NSA_HEAD), 3, 0)
    i_chunks = jnp.moveaxis(top_idx.reshape(B, NSA_KV, n_qb, SLC_Q_BLOCK, n_sel), 2, 0)
    ok_chunks = jnp.moveaxis(top_ok.reshape(B, NSA_KV, n_qb, SLC_Q_BLOCK, n_sel), 2, 0)
    t_chunks = t_pos.reshape(n_qb, SLC_Q_BLOCK)
    gather_blocks = jax.vmap(jax.vmap(lambda blocks, ids: blocks[ids]))

    def slc_block(args):
        qb, ib, okb, tb = args
        flat = ib.reshape(B, NSA_KV, SLC_Q_BLOCK * n_sel)
        kg = gather_blocks(k_blk, flat).reshape(B, NSA_KV, SLC_Q_BLOCK, n_sel, SLC_LEN, NSA_HEAD)
        vg = gather_blocks(v_blk, flat).reshape(B, NSA_KV, SLC_Q_BLOCK, n_sel, SLC_LEN, NSA_HEAD)
        s = jnp.einsum('bkgqd,bkqnsd->bkgqns', qb, kg).reshape(
            B, NSA_KV, NSA_GQA, SLC_Q_BLOCK, n_sel * SLC_LEN)
        pos = ib[..., None] * SLC_LEN + jnp.arange(SLC_LEN)
        mask = okb[..., None] & (pos <= tb[:, None, None])
        p = masked_softmax(s, mask.reshape(B, NSA_KV, 1, SLC_Q_BLOCK, n_sel * SLC_LEN))
        p = p.reshape(B, NSA_KV, NSA_GQA, SLC_Q_BLOCK, n_sel, SLC_LEN).astype(qb.dtype)
        return jnp.einsum('bkgqns,bkqnsd->bkgqd', p, vg)

    o_slc = lax.map(slc_block, (q_chunks, i_chunks, ok_chunks, t_chunks))

    kw = jnp.pad(jnp.transpose(k_w, (0, 2, 1, 3)), ((0, 0), (0, 0), (WIN, 0), (0, 0)))
    vw = jnp.pad(jnp.transpose(v_w, (0, 2, 1, 3)), ((0, 0), (0, 0), (WIN, 0), (0, 0)))
    n_wb = T // WIN_Q_BLOCK
    qw_chunks = jnp.moveaxis(q_t.reshape(B, NSA_KV, NSA_GQA, n_wb, WIN_Q_BLOCK, NSA_HEAD), 3, 0)

    def win_block(args):
        qb, c = args
        start = c * WIN_Q_BLOCK
        kb = lax.dynamic_slice_in_dim(kw, start, WIN_Q_BLOCK + WIN, axis=2)
        vb = lax.dynamic_slice_in_dim(vw, start, WIN_Q_BLOCK + WIN, axis=2)
        tq = start + jnp.arange(WIN_Q_BLOCK)
        sk = start - WIN + jnp.arange(WIN_Q_BLOCK + WIN)
        diff = tq[:, None] - sk[None, :]
        mask = (diff >= 0) & (diff < WIN) & (sk[None, :] >= 0)
        p = masked_softmax(jnp.einsum('bkgqd,bksd->bkgqs', qb, kb), mask)
        return jnp.einsum('bkgqs,bksd->bkgqd', p.astype(qb.dtype), vb)

    o_win = lax.map(win_block, (qw_chunks, jnp.arange(n_wb)))

    def unchunk(o):
        o = jnp.moveaxis(o, 0, 3).reshape(B, NSA_KV, NSA_GQA, T, NSA_HEAD)
        return jnp.transpose(o, (0, 3, 1, 2, 4))

    g = jax.nn.sigmoid(gate).reshape(B, T, NSA_KV, NSA_GQA, 3)
    o = g[..., 0:1] * o_cmp + g[..., 1:2] * unchunk(o_slc) + g[..., 2:3] * unchunk(o_win)
    return o.reshape(B, T, NSA_W)


def hybrid_layer(x, ffn1_norm, ffn1_w13, ffn1_w2, mix_norm, w_in,
                 conv_w, conv_b, conv_ln_g, conv_ln_b, w_conv_out,
                 rwkv_mu, rwkv_w0, rwkv_w_b, rwkv_a0, rwkv_a_b, rwkv_g_b, rwkv_k_k, rwkv_k_a,
                 rwkv_r_k, rwkv_lnx_g, rwkv_lnx_b, w_rwkv_out,
                 nsa_pe_k, nsa_pe_v, nsa_ck_w1, nsa_ck_w2, nsa_cv_w1, nsa_cv_w2, w_nsa_out,
                 w_out, ffn2_norm, ffn2_w13, ffn2_w2):
    B, T, _ = x.shape
    x = x + 0.5 * swiglu(rms_norm(x, ffn1_norm), ffn1_w13, ffn1_w2)
    h = rms_norm(x, mix_norm)
    u_conv, u_rwkv, nsa_q, nsa_kv, nsa_g, br_g = jnp.split(h @ w_in, _offsets(IN_SIZES), axis=-1)
    y_a = conv_module(u_conv, conv_w, conv_b, conv_ln_g, conv_ln_b) @ w_conv_out
    y_b = rwkv7_mixer(token_shift(u_rwkv, rwkv_mu), rwkv_w0, rwkv_w_b, rwkv_a0, rwkv_a_b, rwkv_g_b,
                      rwkv_k_k, rwkv_k_a, rwkv_r_k, rwkv_lnx_g, rwkv_lnx_b) @ w_rwkv_out
    y_c = nsa_mixer(nsa_q, nsa_kv, nsa_g, nsa_pe_k, nsa_pe_v,
                    nsa_ck_w1, nsa_ck_w2, nsa_cv_w1, nsa_cv_w2) @ w_nsa_out
    g = jax.nn.sigmoid(br_g).reshape(B, T, N_BRANCH, D_MODEL)
    merged = g[:, :, 0] * y_a + g[:, :, 1] * y_b + g[:, :, 2] * y_c
    x = x + merged @ w_out
    x = x + 0.5 * swiglu(rms_norm(x, ffn2_norm), ffn2_w13, ffn2_w2)
    return x


def setup_inputs(seed: int = 0) -> dict:
    key = jax.random.key(seed)
    keys = iter(jax.random.split(key, 64))
    L = DEPTH

    def nrm(shape, scale):
        return jax.random.normal(next(keys), shape, jnp.float32) * scale

    def gain(shape):
        return 1.0 + nrm(shape, 0.02)

    def unif(shape, lo, hi):
        return jax.random.uniform(next(keys), shape, jnp.float32, lo, hi)

    return {
        'x': nrm((BATCH, SEQ, D_MODEL), 1.0),
        'ffn1_norm': gain((L, D_MODEL)),
        'ffn1_w13': nrm((L, D_MODEL, 2 * D_FF), D_MODEL ** -0.5),
        'ffn1_w2': nrm((L, D_FF, D_MODEL), D_FF ** -0.5),
        'mix_norm': gain((L, D_MODEL)),
        'w_in': nrm((L, D_MODEL, D_IN), D_MODEL ** -0.5),
        'conv_w': nrm((L, CONV_K, CONV_W), CONV_K ** -0.5),
        'conv_b': nrm((L, CONV_W), 0.02),
        'conv_ln_g': gain((L, CONV_W)),
        'conv_ln_b': nrm((L, CONV_W), 0.02),
        'w_conv_out': nrm((L, CONV_W, D_MODEL), CONV_W ** -0.5),
        'rwkv_mu': unif((L, RWKV_IN), 0.0, 1.0),
        'rwkv_w0': unif((L, RWKV_W), -4.0, 1.0),
        'rwkv_w_b': nrm((L, LORA_W, RWKV_W), LORA_W ** -0.5),
        'rwkv_a0': nrm((L, RWKV_W), 0.5),
        'rwkv_a_b': nrm((L, LORA_A, RWKV_W), LORA_A ** -0.5),
        'rwkv_g_b': nrm((L, LORA_G, RWKV_W), LORA_G ** -0.5),
        'rwkv_k_k': 0.85 + nrm((L, RWKV_W), 0.05),
        'rwkv_k_a': 1.0 + nrm((L, RWKV_W), 0.05),
        'rwkv_r_k': nrm((L, RWKV_HEADS, RWKV_HEAD), 0.1),
        'rwkv_lnx_g': gain((L, RWKV_W)),
        'rwkv_lnx_b': nrm((L, RWKV_W), 0.02),
        'w_rwkv_out': nrm((L, RWKV_W, D_MODEL), RWKV_W ** -0.5),
        'nsa_pe_k': nrm((L, CMP_LEN, NSA_HEAD), 0.1),
        'nsa_pe_v': nrm((L, CMP_LEN, NSA_HEAD), 0.1),
        'nsa_ck_w1': nrm((L, CMP_LEN * NSA_HEAD, CMP_HIDDEN), (CMP_LEN * NSA_HEAD) ** -0.5),
        'nsa_ck_w2': nrm((L, CMP_HIDDEN, NSA_HEAD), CMP_HIDDEN ** -0.5),
        'nsa_cv_w1': nrm((L, CMP_LEN * NSA_HEAD, CMP_HIDDEN), (CMP_LEN * NSA_HEAD) ** -0.5),
        'nsa_cv_w2': nrm((L, CMP_HIDDEN, NSA_HEAD), CMP_HIDDEN ** -0.5),
        'w_nsa_out': nrm((L, NSA_W, D_MODEL), NSA_W ** -0.5),
        'w_out': nrm((L, D_MODEL, D_MODEL), D_MODEL ** -0.5),
        'ffn2_norm': gain((L, D_MODEL)),
        'ffn2_w13': nrm((L, D_MODEL, 2 * D_FF), D_MODEL ** -0.5),
        'ffn2_w2': nrm((L, D_FF, D_MODEL), D_FF ** -0.5),
        'final_norm': gain((D_MODEL,)),
    }


def reference(x, ffn1_norm, ffn1_w13, ffn1_w2, mix_norm, w_in,
              conv_w, conv_b, conv_ln_g, conv_ln_b, w_conv_out,
              rwkv_mu, rwkv_w0, rwkv_w_b, rwkv_a0, rwkv_a_b, rwkv_g_b, rwkv_k_k, rwkv_k_a,
              rwkv_r_k, rwkv_lnx_g, rwkv_lnx_b, w_rwkv_out,
              nsa_pe_k, nsa_pe_v, nsa_ck_w1, nsa_ck_w2, nsa_cv_w1, nsa_cv_w2, w_nsa_out,
              w_out, ffn2_norm, ffn2_w13, ffn2_w2, final_norm):
    for l in range(DEPTH):
        x = hybrid_layer(
            x, ffn1_norm[l], ffn1_w13[l], ffn1_w2[l], mix_norm[l], w_in[l],
            conv_w[l], conv_b[l], conv_ln_g[l], conv_ln_b[l], w_conv_out[l],
            rwkv_mu[l], rwkv_w0[l], rwkv_w_b[l], rwkv_a0[l], rwkv_a_b[l], rwkv_g_b[l],
            rwkv_k_k[l], rwkv_k_a[l], rwkv_r_k[l], rwkv_lnx_g[l], rwkv_lnx_b[l], w_rwkv_out[l],
            nsa_pe_k[l], nsa_pe_v[l], nsa_ck_w1[l], nsa_ck_w2[l], nsa_cv_w1[l], nsa_cv_w2[l],
            w_nsa_out[l], w_out[l], ffn2_norm[l], ffn2_w13[l], ffn2_w2[l])
    return rms_norm(x, final_norm)
```

```python
import numpy as np
from contextlib import ExitStack
import concourse.bass as bass
import concourse.mybir as mybir
from concourse.bass_utils import run_bass_kernel_spmd

F32 = mybir.dt.float32
BF16 = mybir.dt.bfloat16
AF = mybir.ActivationFunctionType
ALU = mybir.AluOpType
AX = mybir.AxisListType


class Res:
    __slots__ = ("name", "lw", "rd")

    def __init__(self, name=""):
        self.name = name
        self.lw = None
        self.rd = {}


class KB:
    NDMA = 6

    def __init__(self, nc, st):
        self.nc = nc
        self.st = st
        self.eng = {"pe": nc.tensor, "act": nc.scalar, "dve": nc.vector,
                    "pool": nc.gpsimd, "sp": nc.sync}
        self.sems = {}
        self.cnt = {}
        for e in self.eng:
            self.sems[e] = st.enter_context(nc.semaphore("c_" + e))
            self.cnt[e] = 0
        self.dq = {}
        for q in ("sp", "pool", "act"):
            lst = []
            for i in range(self.NDMA):
                key = "d_%s%d" % (q, i)
                self.sems[key] = st.enter_context(nc.semaphore(key))
                self.cnt[key] = 0
                lst.append(key)
            self.dq[q] = [lst, 0]
        self.known = {e: {} for e in self.eng}
        self.ninst = 0

    def sb(self, name, shape, dt):
        return self.st.enter_context(self.nc.sbuf_tensor(name, shape, dt))

    def ps(self, name, shape, dt=F32):
        return self.st.enter_context(self.nc.psum_tensor(name, shape, dt))

    def _wait(self, e, deps):
        best = {}
        for (k, v) in deps:
            if best.get(k, 0) < v:
                best[k] = v
        for k, v in best.items():
            if k == e and e == "pe":
                continue
            if self.known[e].get(k, 0) >= v:
                continue
            self.eng[e].wait_ge(self.sems[k], v)
            self.known[e][k] = v
            self.ninst += 1

    def _deps(self, r, w):
        deps = []
        for x in r:
            if x.lw is not None:
                deps.append(x.lw)
        for x in w:
            if x.lw is not None:
                deps.append(x.lw)
            deps.extend(x.rd.items())
        return deps

    def op(self, e, fn, r=(), w=()):
        self._wait(e, self._deps(r, w))
        ins = fn(self.eng[e])
        self.cnt[e] += 1
        ins.then_inc(self.sems[e], 1)
        tok = (e, self.cnt[e])
        for x in r:
            x.rd[e] = self.cnt[e]
        for x in w:
            x.lw = tok
            x.rd = {}
        self.ninst += 1
        return ins

    def dma(self, q, out, in_, r=(), w=(), **kw):
        lst, i = self.dq[q]
        key = lst[i % len(lst)]
        self.dq[q][1] = i + 1
        deps = self._deps(r, w)
        if self.cnt[key] > 0:
            deps.append((key, self.cnt[key]))
        self._wait(q, deps)
        ins = self.eng[q].dma_start(out=out, in_=in_, **kw)
        self.cnt[key] += 16
        ins.then_inc(self.sems[key], 16)
        tok = (key, self.cnt[key])
        for x in r:
            x.rd[key] = self.cnt[key]
        for x in w:
            x.lw = tok
            x.rd = {}
        self.ninst += 1
        return ins

    def wait_all(self, e, res):
        deps = []
        for x in res:
            if x.lw is not None:
                deps.append(x.lw)
            deps.extend(x.rd.items())
        self._wait(e, deps)

    def barrier(self):
        allk = [(k, v) for k, v in self.cnt.items() if v > 0]
        for e in self.eng:
            self._wait(e, allk)


D = 2048
DFF = 5632
DIN = 14320
NT = 1024
KC = D // 128
EPS = 1e-6


def build_tl(do_C, do_A, do_final, NBLK=8):
    nc = bass.Bass("TRN2", target_bir_lowering=False)

    def din(name, shape):
        return nc.dram_tensor(name, list(shape), F32, kind="ExternalInput").ap()

    def dout(name, shape):
        return nc.dram_tensor(name, list(shape), F32, kind="ExternalOutput").ap()

    xT = din("xT", [NBLK * D, NT])
    if do_C:
        mixT = din("mixT", [NBLK * 3072, NT])
        brgT = din("brgT", [NBLK * 3 * D, NT])
        w_br = [din("w_conv_out", [1024, D]), din("w_rwkv_out", [1024, D]), din("w_nsa_out", [1024, D])]
        w_out = din("w_out", [D, D])
        ffn2_norm = din("ffn2_norm", [128, KC])
        ffn2_w13 = din("ffn2_w13", [D, 2 * DFF])
        ffn2_w2 = din("ffn2_w2", [DFF, D])
    if do_A:
        ffn1_norm = din("ffn1_norm", [128, KC])
        ffn1_w13 = din("ffn1_w13", [D, 2 * DFF])
        ffn1_w2 = din("ffn1_w2", [DFF, D])
        mix_norm = din("mix_norm", [128, KC])
        w_in = din("w_in", [D, DIN])
        x1T = dout("x1T", [NBLK * D, NT])
        uT = dout("uT", [NBLK * DIN, NT])
    if do_final:
        final_norm = din("final_norm", [128, KC])
        outT = dout("outT", [NBLK * D, NT])

    with ExitStack() as st:
        k = KB(nc, st)
        X = k.sb("X", [128, KC, NT], F32)
        H = k.sb("H", [128, KC, NT], BF16)
        rX = [Res("X%d" % c) for c in range(KC)]
        rH = [Res("H%d" % c) for c in range(KC)]
        ones = k.sb("ones", [128, 128], BF16)
        r_ones = Res("ones")
        gains = k.sb("gains", [128, 4, KC], F32)
        r_gains = Res("gains")
        sq = [k.sb("sq%d" % i, [128, NT], BF16) for i in range(2)]
        r_sq = [Res("sq%d" % i) for i in range(2)]
        rstd = k.sb("rstd", [128, NT], F32)
        r_rstd = Res("rstd")
        banks = [k.ps("bank%d" % i, [128, 512], F32) for i in range(8)]
        r_bank = [Res("bank%d" % i) for i in range(8)]
        bank_i = [0]
        uid = [0]
        blk = [0]

        def nb():
            i = bank_i[0] % 8
            bank_i[0] += 1
            return banks[i], r_bank[i]

        k.op("dve", lambda e: e.memset(ones[:], 1.0), w=[r_ones])
        xvs = [xT[b_ * D:(b_ + 1) * D, :].rearrange("(c p) t -> p c t", p=128) for b_ in range(NBLK)]
        gi = 0
        gidx = {}
        for nm, flag in (("ffn2_norm", do_C), ("ffn1_norm", do_A), ("mix_norm", do_A), ("final_norm", do_final)):
            if flag:
                k.dma("sp", gains[:, gi, :], locals()[nm], w=[r_gains])
                gidx[nm] = gi
                gi += 1

        def rmsnorm(gname, to_x=False):
            g = gidx[gname]
            b0, rb0 = nb()
            b1, rb1 = nb()
            for c in range(KC):
                s, rs = sq[c % 2], r_sq[c % 2]
                k.op("act", lambda e: e.activation(out=s[:], in_=X[:, c, :], func=AF.Square), r=[rX[c]], w=[rs])
                for th, (b, rb) in enumerate(((b0, rb0), (b1, rb1))):
                    k.op("pe", lambda e: e.matmul(b[:], ones[:], s[:, th * 512:(th + 1) * 512],
                                                   start=(c == 0), stop=(c == KC - 1)), r=[rs, r_ones], w=[rb])
            for th, (b, rb) in enumerate(((b0, rb0), (b1, rb1))):
                sl = slice(th * 512, (th + 1) * 512)
                k.op("dve", lambda e: e.tensor_scalar(out=rstd[:, sl], in0=b[:], scalar1=1.0 / D, scalar2=EPS,
                                                      op0=ALU.mult, op1=ALU.add), r=[rb], w=[r_rstd])
            k.op("act", lambda e: e.activation(out=rstd[:], in_=rstd[:], func=AF.Sqrt), r=[r_rstd], w=[r_rstd])
            k.op("dve", lambda e: e.reciprocal(out=rstd[:], in_=rstd[:]), r=[r_rstd], w=[r_rstd])
            for c in range(KC):
                if to_x:
                    k.op("dve", lambda e: e.scalar_tensor_tensor(out=X[:, c, :], in0=X[:, c, :], scalar=gains[:, g, c:c + 1],
                                                                 in1=rstd[:], op0=ALU.mult, op1=ALU.mult),
                         r=[rX[c], r_rstd, r_gains], w=[rX[c]])
                else:
                    k.op("dve", lambda e: e.scalar_tensor_tensor(out=H[:, c, :], in0=X[:, c, :], scalar=gains[:, g, c:c + 1],
                                                                 in1=rstd[:], op0=ALU.mult, op1=ALU.mult),
                         r=[rX[c], r_rstd, r_gains], w=[rH[c]])

        def ffn(w13, w2):
            NG = 4
            GF = 11
            w13v = w13.rearrange("(kc p) n -> p kc n", p=128)
            w2v = w2.rearrange("(f p) n -> p f n", p=128)
            with ExitStack() as st2:
                def sb2(name, shape, dt):
                    uid[0] += 1
                    return st2.enter_context(nc.sbuf_tensor("%s_u%d" % (name, uid[0]), shape, dt))
                G = sb2("G", [128, GF, NT], BF16)
                rG = [Res("G%d" % i) for i in range(GF)]
                W1 = [sb2("W1_%d" % i, [128, KC, 128], BF16) for i in range(2)]
                W3 = [sb2("W3_%d" % i, [128, KC, 128], BF16) for i in range(2)]
                rW1 = [Res() for i in range(2)]
                rW3 = [Res() for i in range(2)]
                W2 = sb2("W2", [128, GF, D], BF16)
                rW2 = [Res() for i in range(GF)]
                sa = [sb2("sa%d" % i, [128, 512], F32) for i in range(2)]
                r_sa = [Res() for i in range(2)]

                def load13(f):
                    i = f % 2
                    k.dma("pool", W1[i][:], w13v[:, :, f * 128:(f + 1) * 128], w=[rW1[i]])
                    k.dma("pool", W3[i][:], w13v[:, :, DFF + f * 128:DFF + (f + 1) * 128], w=[rW3[i]])

                load13(0)
                it = 0
                for gidx_ in range(NG):
                    for fl in range(GF):
                        f = gidx_ * GF + fl
                        if f + 1 < NG * GF:
                            load13(f + 1)
                        if fl == 0:
                            for j in range(GF):
                                k.dma("pool", W2[:, j, :], w2v[:, gidx_ * GF + j, :], w=[rW2[j]])
                        i = f % 2
                        for th in range(2):
                            sl = slice(th * 512, (th + 1) * 512)
                            ba, rba = nb()
                            bb, rbb = nb()
                            for kc in range(KC):
                                k.op("pe", lambda e: e.matmul(ba[:], W1[i][:, kc, :], H[:, kc, sl], start=(kc == 0), stop=(kc == KC - 1)),
                                     r=[rW1[i], rH[kc]], w=[rba])
                            for kc in range(KC):
                                k.op("pe", lambda e: e.matmul(bb[:], W3[i][:, kc, :], H[:, kc, sl], start=(kc == 0), stop=(kc == KC - 1)),
                                     r=[rW3[i], rH[kc]], w=[rbb])
                            s_, rs_ = sa[it % 2], r_sa[it % 2]
                            it += 1
                            k.op("act", lambda e: e.activation(out=s_[:], in_=ba[:], func=AF.Silu), r=[rba], w=[rs_])
                            k.op("dve", lambda e: e.tensor_tensor(out=G[:, fl, sl], in0=s_[:], in1=bb[:], op=ALU.mult),
                                 r=[rs_, rbb], w=[rG[fl]])
                    for m in range(KC):
                        for th in range(2):
                            sl = slice(th * 512, (th + 1) * 512)
                            b, rb = nb()
                            for fl in range(GF):
                                k.op("pe", lambda e: e.matmul(b[:], W2[:, fl, m * 128:(m + 1) * 128], G[:, fl, sl],
                                                               start=(fl == 0), stop=(fl == GF - 1)),
                                     r=[rW2[fl], rG[fl]], w=[rb])
                            k.op("dve", lambda e: e.scalar_tensor_tensor(out=X[:, m, sl], in0=b[:], scalar=0.5, in1=X[:, m, sl],
                                                                         op0=ALU.mult, op1=ALU.add),
                                 r=[rb, rX[m]], w=[rX[m]])
                k.barrier()

        def c_phase():
            mixv = mixT[blk[0] * 3072:(blk[0] + 1) * 3072, :].rearrange("(c p) t -> p c t", p=128)
            brgv = brgT[blk[0] * 3 * D:(blk[0] + 1) * 3 * D, :].rearrange("(b m p) t -> p b m t", p=128, b=3)
            with ExitStack() as st2:
                def sb2(name, shape, dt):
                    uid[0] += 1
                    return st2.enter_context(nc.sbuf_tensor("%s_u%d" % (name, uid[0]), shape, dt))
                MIX = sb2("MIX", [128, 24, NT], BF16)
                rMIX = [Res() for i in range(24)]
                for c0 in range(0, 24, 4):
                    k.dma("pool", MIX[:, c0:c0 + 4, :], mixv[:, c0:c0 + 4, :], w=rMIX[c0:c0 + 4])
                BRG = [sb2("BRG%d" % i, [128, 3, 512], F32) for i in range(2)]
                rBRG = [Res() for i in range(2)]
                WB = [[sb2("WB%d_%d" % (b, i), [128, 8, 128], BF16) for i in range(2)] for b in range(3)]
                rWB = [[Res() for i in range(2)] for b in range(3)]
                sg = [sb2("sg%d" % i, [128, 512], F32) for i in range(3)]
                r_sg = [Res() for i in range(3)]
                tt = [sb2("tt%d" % i, [128, 512], F32) for i in range(3)]
                r_tt = [Res() for i in range(3)]
                wbv = [w.rearrange("(kc p) n -> p kc n", p=128) for w in w_br]

                def loadm(m):
                    i = m % 2
                    for b in range(3):
                        k.dma("pool", WB[b][i][:], wbv[b][:, :, m * 128:(m + 1) * 128], w=[rWB[b][i]])

                def loadbrg(it_):
                    m_, th_ = it_ // 2, it_ % 2
                    k.dma("sp", BRG[it_ % 2][:], brgv[:, :, m_, th_ * 512:(th_ + 1) * 512], w=[rBRG[it_ % 2]])

                loadm(0)
                loadbrg(0)
                for m in range(KC):
                    if m + 1 < KC:
                        loadm(m + 1)
                    i = m % 2
                    for th in range(2):
                        sl = slice(th * 512, (th + 1) * 512)
                        bi = (m * 2 + th) % 2
                        if m * 2 + th + 1 < 2 * KC:
                            loadbrg(m * 2 + th + 1)
                        pb = []
                        for b in range(3):
                            bk, rbk = nb()
                            pb.append((bk, rbk))
                            for kc in range(8):
                                k.op("pe", lambda e: e.matmul(bk[:], WB[b][i][:, kc, :], MIX[:, b * 8 + kc, sl],
                                                               start=(kc == 0), stop=(kc == 7)),
                                     r=[rWB[b][i], rMIX[b * 8 + kc]], w=[rbk])
                        for b in range(3):
                            k.op("act", lambda e: e.activation(out=sg[b][:], in_=BRG[bi][:, b, :], func=AF.Sigmoid),
                                 r=[rBRG[bi]], w=[r_sg[b]])
                            k.op("dve", lambda e: e.tensor_tensor(out=tt[b][:], in0=sg[b][:], in1=pb[b][0][:], op=ALU.mult),
                                 r=[r_sg[b], pb[b][1]], w=[r_tt[b]])
                        k.op("dve", lambda e: e.tensor_tensor(out=tt[0][:], in0=tt[0][:], in1=tt[1][:], op=ALU.add),
                             r=[r_tt[0], r_tt[1]], w=[r_tt[0]])
                        k.op("dve", lambda e: e.tensor_tensor(out=H[:, m, sl], in0=tt[0][:], in1=tt[2][:], op=ALU.add),
                             r=[r_tt[0], r_tt[2]], w=[rH[m]])
                k.barrier()
            wov = w_out.rearrange("(kc p) n -> p kc n", p=128)
            with ExitStack() as st2:
                WO = [st2.enter_context(nc.sbuf_tensor("WO%d_b%d" % (i, blk[0]), [128, KC, 128], BF16)) for i in range(2)]
                rWO = [Res() for i in range(2)]
                k.dma("pool", WO[0][:], wov[:, :, 0:128], w=[rWO[0]])
                for m in range(KC):
                    if m + 1 < KC:
                        k.dma("pool", WO[(m + 1) % 2][:], wov[:, :, (m + 1) * 128:(m + 2) * 128], w=[rWO[(m + 1) % 2]])
                    i = m % 2
                    for th in range(2):
                        sl = slice(th * 512, (th + 1) * 512)
                        b, rb = nb()
                        for kc in range(KC):
                            k.op("pe", lambda e: e.matmul(b[:], WO[i][:, kc, :], H[:, kc, sl], start=(kc == 0), stop=(kc == KC - 1)),
                                 r=[rWO[i], rH[kc]], w=[rb])
                        k.op("dve", lambda e: e.tensor_tensor(out=X[:, m, sl], in0=b[:], in1=X[:, m, sl], op=ALU.add),
                             r=[rb, rX[m]], w=[rX[m]])
                k.barrier()

        def win_phase():
            wiv = w_in.rearrange("(kc p) n -> p kc n", p=128)
            nch = (DIN + 127) // 128
            with ExitStack() as st2:
                WI = [st2.enter_context(nc.sbuf_tensor("WI%d_b%d" % (i, blk[0]), [128, KC, 128], BF16)) for i in range(2)]
                rWI = [Res() for i in range(2)]
                stg = [st2.enter_context(nc.sbuf_tensor("stg%d_b%d" % (i, blk[0]), [128, 512], F32)) for i in range(4)]
                r_stg = [Res() for i in range(4)]

                def loadj(j):
                    cw = min(128, DIN - j * 128)
                    k.dma("pool", WI[j % 2][:, :, 0:cw], wiv[:, :, j * 128:j * 128 + cw], w=[rWI[j % 2]])

                loadj(0)
                it = 0
                for j in range(nch):
                    if j + 1 < nch:
                        loadj(j + 1)
                    cw = min(128, DIN - j * 128)
                    i = j % 2
                    for th in range(2):
                        sl = slice(th * 512, (th + 1) * 512)
                        b, rb = nb()
                        for kc in range(KC):
                            k.op("pe", lambda e: e.matmul(b[0:cw, :], WI[i][:, kc, 0:cw], H[:, kc, sl], start=(kc == 0), stop=(kc == KC - 1)),
                                 r=[rWI[i], rH[kc]], w=[rb])
                        s_, rs_ = stg[it % 4], r_stg[it % 4]
                        if it % 2 == 0:
                            k.op("act", lambda e: e.copy(out=s_[0:cw, :], in_=b[0:cw, :]), r=[rb], w=[rs_])
                        else:
                            k.op("dve", lambda e: e.tensor_copy(out=s_[0:cw, :], in_=b[0:cw, :]), r=[rb], w=[rs_])
                        it += 1
                        k.dma("sp", uT[blk[0] * DIN + j * 128:blk[0] * DIN + j * 128 + cw, sl], s_[0:cw, :], r=[rs_])
                k.barrier()

        for b_ in range(NBLK):
            blk[0] = b_
            for c0 in range(0, KC, 4):
                k.dma("sp", X[:, c0:c0 + 4, :], xvs[b_][:, c0:c0 + 4, :], w=rX[c0:c0 + 4])
            if do_C:
                c_phase()
                rmsnorm("ffn2_norm")
                ffn(ffn2_w13, ffn2_w2)
            if do_A:
                rmsnorm("ffn1_norm")
                ffn(ffn1_w13, ffn1_w2)
                x1v = x1T[blk[0] * D:(blk[0] + 1) * D, :].rearrange("(c p) t -> p c t", p=128)
                for c0 in range(0, KC, 4):
                    k.dma("sp", x1v[:, c0:c0 + 4, :], X[:, c0:c0 + 4, :], r=rX[c0:c0 + 4])
                rmsnorm("mix_norm")
                win_phase()
            if do_final:
                rmsnorm("final_norm", to_x=True)
                ov = outT[blk[0] * D:(blk[0] + 1) * D, :].rearrange("(c p) t -> p c t", p=128)
                for c0 in range(0, KC, 4):
                    k.dma("sp", ov[:, c0:c0 + 4, :], X[:, c0:c0 + 4, :], r=rX[c0:c0 + 4])

            k.barrier()
        k.barrier()
        print("TL instructions:", k.ninst)
    return nc


T = 4096
LN_EPS = 1e-5
GN_EPS = 64e-5


def build_conv():
    nc = bass.Bass("TRN2", target_bir_lowering=False)
    NTk = 1024
    PADT = NTk + 30

    def din(name, shape):
        return nc.dram_tensor(name, list(shape), F32, kind="ExternalInput").ap()
    aT = din("aT", [1024, PADT])
    gT = din("gT", [1024, PADT])
    cw = din("cw", [128, 8, 31])
    cb = din("cb", [128, 8])
    lg = din("lg", [128, 8])
    lb = din("lb", [128, 8])
    oT = nc.dram_tensor("oT", [1024, NTk], F32, kind="ExternalOutput").ap()
    with ExitStack() as st:
        k = KB(nc, st)
        CW = k.sb("CW", [128, 8, 31], F32)
        PB = k.sb("PB", [128, 3, 8], F32)
        r_par = Res()
        k.dma("sp", CW[:], cw, w=[r_par])
        k.dma("sp", PB[:, 0, :], cb, w=[r_par])
        k.dma("sp", PB[:, 1, :], lg, w=[r_par])
        k.dma("sp", PB[:, 2, :], lb, w=[r_par])
        ones = k.sb("ones", [128, 128], BF16)
        r_ones = Res()
        k.op("dve", lambda e: e.memset(ones[:], 1.0), w=[r_ones])
        CO = k.sb("CO", [128, 8, NTk], F32)
        rCO = [Res() for c in range(8)]
        A = [k.sb("A%d" % i, [128, PADT], F32) for i in range(2)]
        Gt = [k.sb("G%d" % i, [128, PADT], F32) for i in range(2)]
        rA = [Res() for i in range(2)]
        rG = [Res() for i in range(2)]
        banks = [k.ps("bank%d" % i, [128, 512], F32) for i in range(8)]
        r_bank = [Res() for i in range(8)]
        av = aT.rearrange("(c p) t -> p c t", p=128)
        gv = gT.rearrange("(c p) t -> p c t", p=128)
        for c in range(8):
            i = c % 2
            k.dma("sp", A[i][:], av[:, c, :], w=[rA[i]])
            k.dma("sp", Gt[i][:], gv[:, c, :], w=[rG[i]])
            k.op("act", lambda e: e.activation(out=Gt[i][:], in_=Gt[i][:], func=AF.Sigmoid), r=[rG[i]], w=[rG[i]])
            eng = "dve"
            k.op(eng, lambda e: e.tensor_tensor(out=A[i][:], in0=A[i][:], in1=Gt[i][:], op=ALU.mult), r=[rA[i], rG[i]], w=[rA[i]])
            k.op(eng, lambda e: e.tensor_scalar(out=CO[:, c, :], in0=A[i][:, 0:NTk], scalar1=CW[:, c, 0:1], scalar2=PB[:, 0, c:c + 1],
                                                op0=ALU.mult, op1=ALU.add), r=[rA[i], r_par], w=[rCO[c]])
            for j in range(1, 31):
                k.op(eng, lambda e: e.scalar_tensor_tensor(out=CO[:, c, :], in0=A[i][:, j:j + NTk], scalar=CW[:, c, j:j + 1],
                                                           in1=CO[:, c, :], op0=ALU.mult, op1=ALU.add),
                     r=[rA[i], r_par, rCO[c]], w=[rCO[c]])
        xb = [k.sb("xb%d" % i, [128, 512], BF16) for i in range(2)]
        x2 = [k.sb("x2%d" % i, [128, 512], BF16) for i in range(2)]
        r_xb = [Res() for i in range(2)]
        r_x2 = [Res() for i in range(2)]
        mean = k.sb("mean", [128, 512], F32)
        rstd = k.sb("rstd", [128, 512], F32)
        msq = k.sb("msq", [128, 512], F32)
        r_mean, r_rstd, r_msq = Res(), Res(), Res()
        t1 = [k.sb("t1%d" % i, [128, 512], F32) for i in range(2)]
        r_t1 = [Res() for i in range(2)]
        og = [k.sb("og%d" % i, [128, 512], F32) for i in range(2)]
        r_og = [Res() for i in range(2)]
        for th in range(2):
            sl = slice(th * 512, (th + 1) * 512)
            b1, rb1 = banks[2 * th], r_bank[2 * th]
            b2, rb2 = banks[2 * th + 1], r_bank[2 * th + 1]
            for c in range(8):
                i = c % 2
                k.op("act", lambda e: e.copy(out=xb[i][:], in_=CO[:, c, sl]), r=[rCO[c]], w=[r_xb[i]])
                k.op("act", lambda e: e.activation(out=x2[i][:], in_=CO[:, c, sl], func=AF.Square), r=[rCO[c]], w=[r_x2[i]])
                k.op("pe", lambda e: e.matmul(b1[:], ones[:], xb[i][:], start=(c == 0), stop=(c == 7)), r=[r_xb[i], r_ones], w=[rb1])
                k.op("pe", lambda e: e.matmul(b2[:], ones[:], x2[i][:], start=(c == 0), stop=(c == 7)), r=[r_x2[i], r_ones], w=[rb2])
            k.op("dve", lambda e: e.tensor_scalar(out=mean[:], in0=b1[:], scalar1=1.0 / 1024, scalar2=None, op0=ALU.mult), r=[rb1], w=[r_mean])
            k.op("dve", lambda e: e.tensor_tensor(out=msq[:], in0=mean[:], in1=mean[:], op=ALU.mult), r=[r_mean], w=[r_msq])
            k.op("dve", lambda e: e.scalar_tensor_tensor(out=rstd[:], in0=b2[:], scalar=1.0 / 1024, in1=msq[:], op0=ALU.mult, op1=ALU.subtract),
                 r=[rb2, r_msq], w=[r_rstd])
            k.op("dve", lambda e: e.tensor_scalar(out=rstd[:], in0=rstd[:], scalar1=LN_EPS, scalar2=None, op0=ALU.add), r=[r_rstd], w=[r_rstd])
            k.op("act", lambda e: e.activation(out=rstd[:], in_=rstd[:], func=AF.Sqrt), r=[r_rstd], w=[r_rstd])
            k.op("dve", lambda e: e.reciprocal(out=rstd[:], in_=rstd[:]), r=[r_rstd], w=[r_rstd])
            for c in range(8):
                i = c % 2
                k.op("dve", lambda e: e.tensor_tensor(out=t1[i][:], in0=CO[:, c, sl], in1=mean[:], op=ALU.subtract), r=[rCO[c], r_mean], w=[r_t1[i]])
                k.op("dve", lambda e: e.tensor_tensor(out=t1[i][:], in0=t1[i][:], in1=rstd[:], op=ALU.mult), r=[r_t1[i], r_rstd], w=[r_t1[i]])
                k.op("act", lambda e: e.activation(out=og[i][:], in_=t1[i][:], func=AF.Silu, scale=PB[:, 1, c:c + 1], bias=PB[:, 2, c:c + 1]),
                     r=[r_t1[i], r_par], w=[r_og[i]])
                k.dma("sp", oT[c * 128:(c + 1) * 128, sl], og[i][:], r=[r_og[i]])
        k.barrier()
        print("conv instructions", k.ninst)
    return nc


def conv_inputs(u_conv_b, q, p):
    s = q * 1024
    a = np.zeros((1054, 1024), np.float32)
    g = np.zeros((1054, 1024), np.float32)
    lo = max(0, s - 30)
    a[30 - (s - lo):] = u_conv_b[lo:s + 1024, :1024]
    g[30 - (s - lo):] = u_conv_b[lo:s + 1024, 1024:]
    def pc(v): return np.ascontiguousarray(v.reshape(8, 128).T)
    return {"aT": np.ascontiguousarray(a.T), "gT": np.ascontiguousarray(g.T),
            "cw": np.ascontiguousarray(p['conv_w'].T.reshape(8, 128, 31).transpose(1, 0, 2)),
            "cb": pc(p['conv_b']), "lg": pc(p['conv_ln_g']), "lb": pc(p['conv_ln_b'])}


def build_rwkv():
    nc = bass.Bass("TRN2", target_bir_lowering=False)

    def din(name, shape):
        return nc.dram_tensor(name, list(shape), F32, kind="ExternalInput").ap()
    pc = din("pc", [1216, T])
    pp = din("pp", [1216, T])
    mu = din("mu", [128, 10])
    prm = din("prm", [128, 2, 7])
    w_b = din("w_b", [96, 256])
    a_b = din("a_b", [96, 256])
    g_b = din("g_b", [256, 256])
    ident = din("ident", [128, 128])
    bones = din("bones", [128, 128])
    oT = nc.dram_tensor("oT", [256, T], F32, kind="ExternalOutput").ap()
    tokS = nc.dram_tensor("tokS", [2, 5, T, 128], F32).ap()
    r_tok = Res()
    with ExitStack() as st:
        k = KB(nc, st)
        banks = [k.ps("bank%d" % i, [128, 512], F32) for i in range(8)]
        r_bank = [Res() for i in range(8)]
        bank_i = [0]

        def nb():
            i = bank_i[0] % 8
            bank_i[0] += 1
            return banks[i], r_bank[i]
        MU = k.sb("MU", [128, 10], F32)
        PRM = k.sb("PRM", [128, 2, 7], F32)
        IDN = k.sb("IDN", [128, 128], F32)
        BON = k.sb("BON", [128, 128], BF16)
        WB = k.sb("WB", [96, 256], BF16)
        AB = k.sb("AB", [96, 256], BF16)
        GB = k.sb("GB", [128, 2, 256], BF16)
        r_c = Res()
        k.dma("sp", MU[:], mu, w=[r_c])
        k.dma("sp", PRM[:], prm, w=[r_c])
        k.dma("sp", IDN[:], ident, w=[r_c])
        k.dma("pool", BON[:], bones, w=[r_c])
        k.dma("pool", WB[:], w_b, w=[r_c])
        k.dma("pool", AB[:], a_b, w=[r_c])
        k.dma("pool", GB[:], g_b.rearrange("(kc p) n -> p kc n", p=128), w=[r_c])
        V = k.sb("V", [128, 2, T], F32)
        rV = [Res(), Res()]
        RKB = k.sb("RKB", [128, 2, T], BF16)
        rRKB = [Res(), Res()]
        SGL = k.sb("SGL", [128, 2, T], BF16)
        rSGL = [Res(), Res()]

        with ExitStack() as st2:
            def sb2(name, shape, dt):
                return st2.enter_context(nc.sbuf_tensor(name, shape, dt))
            ld = [sb2("ldc%d" % i, [128, T], F32) for i in range(1)]
            lp = [sb2("ldp%d" % i, [128, T], F32) for i in range(1)]
            r_ld = [Res(), Res()]
            r_lp = [Res(), Res()]
            ldi = [0]

            def shifted(rowtile, nrows, out, r_out_, post=None):
                i = 0
                r0 = {0: 0, 1: 128, 2: 256, 3: 384, 4: 512, 5: 640, 6: 768, 7: 896, 8: 1024, 9: 1120}[rowtile]
                k.dma("sp", ld[i][0:nrows, :], pc[r0:r0 + nrows, :], w=[r_ld[i]])
                k.dma("sp", lp[i][0:nrows, :], pp[r0:r0 + nrows, :], w=[r_lp[i]])
                k.op("pool", lambda e: e.tensor_tensor(out=lp[i][0:nrows, :], in0=lp[i][0:nrows, :], in1=ld[i][0:nrows, :], op=ALU.subtract),
                     r=[r_ld[i], r_lp[i]], w=[r_lp[i]])
                if post is None:
                    k.op("dve", lambda e: e.scalar_tensor_tensor(out=out, in0=lp[i][0:nrows, :], scalar=MU[0:nrows, rowtile:rowtile + 1],
                                                                 in1=ld[i][0:nrows, :], op0=ALU.mult, op1=ALU.add),
                         r=[r_ld[i], r_lp[i], r_c], w=[r_out_])
                else:
                    k.op("dve", lambda e: e.scalar_tensor_tensor(out=ld[i][0:nrows, :], in0=lp[i][0:nrows, :], scalar=MU[0:nrows, rowtile:rowtile + 1],
                                                                 in1=ld[i][0:nrows, :], op0=ALU.mult, op1=ALU.add),
                         r=[r_ld[i], r_lp[i], r_c], w=[r_ld[i]])
                    k.op("act", lambda e: e.activation(out=out, in_=ld[i][0:nrows, :], func=post), r=[r_ld[i]], w=[r_out_])

            WL = sb2("WL", [96, T], BF16)
            AL = sb2("AL", [96, T], BF16)
            r_WL, r_AL = Res(), Res()
            shifted(8, 96, WL[:], r_WL, post=AF.Tanh)
            shifted(9, 96, AL[:], r_AL, post=AF.Copy)
            for ct in range(2):
                shifted(6 + ct, 128, SGL[:, ct, :], rSGL[ct], post=AF.Sigmoid)
                shifted(4 + ct, 128, V[:, ct, :], rV[ct])
            Rt = sb2("Rt", [128, T], F32)
            Kt = sb2("Kt", [128, T], F32)
            Wt = sb2("Wt", [128, T], F32)
            At = sb2("At", [128, T], F32)
            KKt = sb2("KKt", [128, T], F32)
            SQb = sb2("SQb", [128, 512], BF16)
            r_R, r_K, r_W, r_A, r_KK, r_SQ = Res(), Res(), Res(), Res(), Res(), Res()
            stg = [sb2("stg%d" % i, [128, 4, 128], F32) for i in range(2)]
            r_stg = [Res(), Res()]
            stg_i = [0]

            def to_tok(src, r_src, vec, ct):
                for t0 in range(0, 32, 4):
                    b, rb = nb()
                    for a in range(4):
                        tt = t0 + a
                        k.op("pe", lambda e: e.transpose(b[:, a * 128:(a + 1) * 128], src[:, tt * 128:(tt + 1) * 128], IDN[:]),
                             r=[r_src, r_c], w=[rb])
                    i = stg_i[0] % 2
                    stg_i[0] += 1
                    k.op("act", lambda e: e.copy(out=stg[i][:].rearrange("p a c -> p (a c)"), in_=b[:]), r=[rb], w=[r_stg[i]])
                    for hp in range(2):
                        dst = tokS[hp, vec, t0 * 128:(t0 + 4) * 128, ct * 64:(ct + 1) * 64].rearrange("(a p) c -> p a c", p=128)
                        k.dma("sp", dst, stg[i][:, :, hp * 64:(hp + 1) * 64], r=[r_stg[i]], w=[r_tok])

            for ct in range(2):
                shifted(0 + ct, 128, Rt[:], r_R)
                shifted(2 + ct, 128, Kt[:], r_K)
                for tb in range(8):
                    sl = slice(tb * 512, (tb + 1) * 512)
                    b, rb = nb()
                    k.op("pe", lambda e: e.matmul(b[:], WB[:, ct * 128:(ct + 1) * 128], WL[:, sl], start=True, stop=True), r=[r_WL, r_c], w=[rb])
                    k.op("act", lambda e: e.activation(out=Wt[:, sl], in_=b[:], func=AF.Sigmoid, bias=PRM[:, ct, 0:1]), r=[rb, r_c], w=[r_W])
                    b2, rb2 = nb()
                    k.op("pe", lambda e: e.matmul(b2[:], AB[:, ct * 128:(ct + 1) * 128], AL[:, sl], start=True, stop=True), r=[r_AL, r_c], w=[rb2])
                    k.op("act", lambda e: e.activation(out=At[:, sl], in_=b2[:], func=AF.Sigmoid, bias=PRM[:, ct, 1:2]), r=[rb2, r_c], w=[r_A])
                k.op("act", lambda e: e.activation(out=Wt[:], in_=Wt[:], func=AF.Exp, scale=-0.6065306597), r=[r_W], w=[r_W])
                to_tok(Wt, r_W, 1, ct)
                to_tok(Rt, r_R, 4, ct)
                k.op("dve", lambda e: e.tensor_scalar(out=KKt[:], in0=Kt[:], scalar1=PRM[:, ct, 2:3], scalar2=None, op0=ALU.mult), r=[r_K, r_c], w=[r_KK])
                k.op("dve", lambda e: e.tensor_scalar(out=Wt[:], in0=At[:], scalar1=-1.0, scalar2=PRM[:, ct, 3:4], op0=ALU.add, op1=ALU.mult),
                     r=[r_A, r_c], w=[r_W])
                k.op("dve", lambda e: e.scalar_tensor_tensor(out=Kt[:], in0=Wt[:], scalar=1.0, in1=Kt[:], op0=ALU.add, op1=ALU.mult),
                     r=[r_W, r_K], w=[r_K])
                to_tok(Kt, r_K, 3, ct)
                k.op("dve", lambda e: e.scalar_tensor_tensor(out=RKB[:, ct, :], in0=Rt[:], scalar=PRM[:, ct, 4:5], in1=Kt[:], op0=ALU.mult, op1=ALU.mult),
                     r=[r_R, r_K, r_c], w=[rRKB[ct]])
                for tb in range(8):
                    sl = slice(tb * 512, (tb + 1) * 512)
                    k.op("act", lambda e: e.activation(out=SQb[:], in_=KKt[:, sl], func=AF.Square), r=[r_KK], w=[r_SQ])
                    b, rb = nb()
                    k.op("pe", lambda e: e.matmul(b[:], BON[:], SQb[:], start=True, stop=True), r=[r_SQ, r_c], w=[rb])
                    k.op("dve", lambda e: e.tensor_scalar(out=Rt[:, sl], in0=b[:], scalar1=1e-12, scalar2=None, op0=ALU.add), r=[rb, r_R], w=[r_R])
                k.op("act", lambda e: e.activation(out=Rt[:], in_=Rt[:], func=AF.Sqrt), r=[r_R], w=[r_R])
                k.op("dve", lambda e: e.reciprocal(out=Rt[:], in_=Rt[:]), r=[r_R], w=[r_R])
                k.op("dve", lambda e: e.scalar_tensor_tensor(out=KKt[:], in0=KKt[:], scalar=-1.0, in1=Rt[:], op0=ALU.mult, op1=ALU.mult),
                     r=[r_KK, r_R], w=[r_KK])
                to_tok(KKt, r_KK, 0, ct)
                k.op("dve", lambda e: e.scalar_tensor_tensor(out=At[:], in0=KKt[:], scalar=-1.0, in1=At[:], op0=ALU.mult, op1=ALU.mult),
                     r=[r_KK, r_A], w=[r_A])
                to_tok(At, r_A, 2, ct)
            k.barrier()

        with ExitStack() as st2:
            def sb2(name, shape, dt):
                return st2.enter_context(nc.sbuf_tensor(name, shape, dt))
            Y = sb2("Y", [128, 2, T], F32)
            rY = [Res(), Res()]
            r_Yall = Res()
            TB = 16
            BC = [[sb2("BC%d_%d" % (v, i), [128, TB, 128], F32) for i in range(2)] for v in range(5)]
            rBC = [[Res() for i in range(2)] for v in range(5)]
            S = sb2("S", [128, 2, 64], F32)
            tmp = sb2("tmp", [128, 2, 64], F32)
            sa = sb2("sa", [128, 2], F32)
            r_S, r_tmp, r_sa = Res(), Res(), Res()
            k.op("dve", lambda e: e.memset(S[:], 0.0), w=[r_S])
            nblk = T // TB

            def loadblk(bi):
                i = bi % 2
                for v in range(5):
                    for hp in range(2):
                        src = tokS[hp, v, bi * TB:(bi + 1) * TB, :].partition_broadcast(64)
                        k.dma("sp" if (v + hp) % 2 == 0 else "act", BC[v][i][hp * 64:(hp + 1) * 64, :, :], src, r=[r_tok], w=[rBC[v][i]])

            loadblk(0)
            for bi in range(nblk):
                if bi + 1 < nblk:
                    loadblk(bi + 1)
                i = bi % 2
                for tl in range(TB):
                    t = bi * TB + tl

                    def bc(v):
                        return BC[v][i][:, tl, :].rearrange("p (c j) -> p c j", c=2)
                    k.op("dve", lambda e: e.tensor_tensor(out=tmp[:], in0=S[:], in1=bc(0), op=ALU.mult), r=[r_S, rBC[0][i]], w=[r_tmp])
                    k.op("dve", lambda e: e.tensor_reduce(out=sa[:], in_=tmp[:], axis=AX.X, op=ALU.add), r=[r_tmp], w=[r_sa])
                    k.op("dve", lambda e: e.tensor_tensor(out=S[:], in0=S[:], in1=bc(1), op=ALU.mult), r=[r_S, rBC[1][i]], w=[r_S])
                    k.op("dve", lambda e: e.tensor_tensor(out=tmp[:], in0=bc(2), in1=sa[:].unsqueeze(2).to_broadcast([128, 2, 64]), op=ALU.mult),
                         r=[r_sa, rBC[2][i]], w=[r_tmp])
                    k.op("dve", lambda e: e.tensor_tensor(out=S[:], in0=S[:], in1=tmp[:], op=ALU.add), r=[r_S, r_tmp], w=[r_S])
                    k.op("dve", lambda e: e.tensor_tensor(out=tmp[:], in0=bc(3), in1=V[:, :, t:t + 1].to_broadcast([128, 2, 64]), op=ALU.mult),
                         r=[rV[0], rV[1], rBC[3][i]], w=[r_tmp])
                    k.op("dve", lambda e: e.tensor_tensor(out=S[:], in0=S[:], in1=tmp[:], op=ALU.add), r=[r_S, r_tmp], w=[r_S])
                    k.op("dve", lambda e: e.tensor_tensor(out=tmp[:], in0=S[:], in1=bc(4), op=ALU.mult), r=[r_S, rBC[4][i]], w=[r_tmp])
                    k.op("dve", lambda e: e.tensor_reduce(out=Y[:, :, t], in_=tmp[:], axis=AX.X, op=ALU.add), r=[r_tmp], w=[r_Yall])

            yb = sb2("yb", [128, 512], BF16)
            y2 = sb2("y2", [128, 512], BF16)
            mean = sb2("mean", [128, 512], F32)
            msq = sb2("msq", [128, 512], F32)
            rstd = sb2("rstd", [128, 512], F32)
            t1 = sb2("t1", [128, 512], F32)
            t2 = sb2("t2", [128, 512], F32)
            og = [sb2("og%d" % i, [128, 512], F32) for i in range(2)]
            r_yb, r_y2, r_mean, r_msq, r_rstd, r_t1, r_t2 = Res(), Res(), Res(), Res(), Res(), Res(), Res()
            r_og = [Res(), Res()]
            it = 0
            for ct in range(2):
                for tb in range(8):
                    sl = slice(tb * 512, (tb + 1) * 512)
                    k.op("act", lambda e: e.copy(out=yb[:], in_=Y[:, ct, sl]), r=[r_Yall], w=[r_yb])
                    k.op("act", lambda e: e.activation(out=y2[:], in_=Y[:, ct, sl], func=AF.Square), r=[r_Yall], w=[r_y2])
                    b1, rb1 = nb()
                    b2, rb2 = nb()
                    k.op("pe", lambda e: e.matmul(b1[:], BON[:], yb[:], start=True, stop=True), r=[r_yb, r_c], w=[rb1])
                    k.op("pe", lambda e: e.matmul(b2[:], BON[:], y2[:], start=True, stop=True), r=[r_y2, r_c], w=[rb2])
                    k.op("dve", lambda e: e.tensor_scalar(out=mean[:], in0=b1[:], scalar1=1.0 / 64, scalar2=None, op0=ALU.mult), r=[rb1], w=[r_mean])
                    k.op("dve", lambda e: e.tensor_tensor(out=msq[:], in0=mean[:], in1=mean[:], op=ALU.mult), r=[r_mean], w=[r_msq])
                    k.op("dve", lambda e: e.scalar_tensor_tensor(out=rstd[:], in0=b2[:], scalar=1.0 / 64, in1=msq[:], op0=ALU.mult, op1=ALU.subtract),
                         r=[rb2, r_msq], w=[r_rstd])
                    k.op("dve", lambda e: e.tensor_scalar(out=rstd[:], in0=rstd[:], scalar1=GN_EPS, scalar2=None, op0=ALU.add), r=[r_rstd], w=[r_rstd])
                    k.op("act", lambda e: e.activation(out=rstd[:], in_=rstd[:], func=AF.Sqrt), r=[r_rstd], w=[r_rstd])
                    k.op("dve", lambda e: e.reciprocal(out=rstd[:], in_=rstd[:]), r=[r_rstd], w=[r_rstd])
                    k.op("dve", lambda e: e.tensor_tensor(out=t1[:], in0=Y[:, ct, sl], in1=mean[:], op=ALU.subtract), r=[r_Yall, r_mean], w=[r_t1])
                    k.op("dve", lambda e: e.tensor_tensor(out=t1[:], in0=t1[:], in1=rstd[:], op=ALU.mult), r=[r_t1, r_rstd], w=[r_t1])
                    k.op("act", lambda e: e.activation(out=t1[:], in_=t1[:], func=AF.Identity, scale=PRM[:, ct, 5:6], bias=PRM[:, ct, 6:7]),
                         r=[r_t1, r_c], w=[r_t1])
                    b3, rb3 = nb()
                    k.op("pe", lambda e: e.matmul(b3[:], BON[:], RKB[:, ct, sl], start=True, stop=True), r=[rRKB[ct], r_c], w=[rb3])
                    k.op("dve", lambda e: e.tensor_tensor(out=t2[:], in0=b3[:], in1=V[:, ct, sl], op=ALU.mult), r=[rb3, rV[ct]], w=[r_t2])
                    k.op("dve", lambda e: e.tensor_tensor(out=t1[:], in0=t1[:], in1=t2[:], op=ALU.add), r=[r_t1, r_t2], w=[r_t1])
                    b4, rb4 = nb()
                    for kc in range(2):
                        k.op("pe", lambda e: e.matmul(b4[:], GB[:, kc, ct * 128:(ct + 1) * 128], SGL[:, kc, sl], start=(kc == 0), stop=(kc == 1)),
                             r=[rSGL[kc], r_c], w=[rb4])
                    i = it % 2
                    it += 1
                    k.op("dve", lambda e: e.tensor_tensor(out=og[i][:], in0=b4[:], in1=t1[:], op=ALU.mult), r=[rb4, r_t1], w=[r_og[i]])
                    k.dma("sp", oT[ct * 128:(ct + 1) * 128, sl], og[i][:], r=[r_og[i]])
            k.barrier()
        print("rwkv instructions", k.ninst)
    return nc


def rwkv_inputs(u_rwkv_b, q, p):
    c0 = q * 256
    cols = np.concatenate([np.arange(c0, c0 + 256), 1024 + np.arange(c0, c0 + 256), 2048 + np.arange(c0, c0 + 256),
                           3072 + 192 + np.arange(256), 3072 + np.arange(96), 3072 + 96 + np.arange(96)])
    pcur = u_rwkv_b[:, cols]
    pprev = np.zeros_like(pcur)
    pprev[1:] = pcur[:-1]
    mu_sel = p['rwkv_mu'][cols]
    mu = np.zeros((128, 10), np.float32)
    starts = [0, 128, 256, 384, 512, 640, 768, 896, 1024, 1120]
    sizes = [128] * 8 + [96, 96]
    for i, (s, n) in enumerate(zip(starts, sizes)):
        mu[:n, i] = mu_sel[s:s + n]
    prm = np.zeros((128, 2, 7), np.float32)
    for ct in range(2):
        sl = slice(c0 + ct * 128, c0 + (ct + 1) * 128)
        prm[:, ct, 0] = p['rwkv_w0'][sl]
        prm[:, ct, 1] = p['rwkv_a0'][sl]
        prm[:, ct, 2] = p['rwkv_k_k'][sl]
        prm[:, ct, 3] = p['rwkv_k_a'][sl]
        prm[:, ct, 4] = p['rwkv_r_k'].reshape(-1)[sl]
        prm[:, ct, 5] = p['rwkv_lnx_g'][sl]
        prm[:, ct, 6] = p['rwkv_lnx_b'][sl]
    bones = np.zeros((128, 128), np.float32)
    bones[:64, :64] = 1
    bones[64:, 64:] = 1
    return {"pc": np.ascontiguousarray(pcur.T), "pp": np.ascontiguousarray(pprev.T), "mu": mu, "prm": prm,
            "w_b": np.ascontiguousarray(p['rwkv_w_b'][:, c0:c0 + 256]), "a_b": np.ascontiguousarray(p['rwkv_a_b'][:, c0:c0 + 256]),
            "g_b": np.ascontiguousarray(p['rwkv_g_b'][:, c0:c0 + 256]), "ident": np.eye(128, dtype=np.float32), "bones": bones}


NEG = -30000.0


def build_nsa():
    nc = bass.Bass("TRN2", target_bir_lowering=False)

    def din(name, shape):
        return nc.dram_tensor(name, list(shape), F32, kind="ExternalInput").ap()
    qT = din("qT", [64, 4, T])
    kcblk = din("kcblk", [128, 16, 255])
    vcblk = din("vcblk", [128, 16, 255])
    ksT = din("ksT", [64, T])
    kwT = din("kwT", [64, T])
    vs1 = din("vs1", [128, 32, 65])
    vw1 = din("vw1", [128, 32, 65])
    gate = din("gate", [128, 32, 12])
    pe = din("pe", [128, 2, 16])
    w1k = din("w1k", [2048, 256])
    w1v = din("w1v", [2048, 256])
    w2k = din("w2k", [256, 64])
    w2v = din("w2v", [256, 64])
    ident = din("ident", [128, 128])
    mdiag = din("mdiag", [128, 512])
    mold = din("mold", [128, 512])
    Eall = din("Eall", [64, 32, 128])
    cmaskK = din("cmaskK", [32, 2, 128, 512])
    cmaskT = din("cmaskT", [32, 128, 256])
    impmul = din("impmul", [128, 32, 64])
    impadd = din("impadd", [128, 32, 64])
    jok = din("jok", [128, 32, 64])
    o = nc.dram_tensor("o", [128, 32, 256], F32, kind="ExternalOutput").ap()

    with ExitStack() as st:
        k = KB(nc, st)
        banks = [k.ps("bank%d" % i, [128, 512], F32) for i in range(8)]
        r_bank = [Res() for i in range(8)]
        r_c = Res()
        Q = k.sb("Q", [64, 4, T], BF16)
        r_Q = Res()
        KS = k.sb("KS", [64, T], BF16)
        KW = k.sb("KW", [64, T], BF16)
        KC = k.sb("KC", [64, 256], BF16)
        VS1 = k.sb("VS1", [128, 32, 65], BF16)
        VW1 = k.sb("VW1", [128, 32, 65], BF16)
        VC1 = k.sb("VC1", [128, 2, 65], BF16)
        r_KC, r_VC = Res(), Res()
        GA = k.sb("GA", [128, 32, 12], F32)
        r_GA = Res()
        IDF = k.sb("IDF", [128, 128], F32)
        IDB = k.sb("IDB", [128, 128], BF16)
        MD = k.sb("MD", [128, 512], BF16)
        MO = k.sb("MO", [128, 512], BF16)
        EA = k.sb("EA", [64, 32, 128], BF16)
        IMUL = k.sb("IMUL", [128, 32, 64], F32)
        IADD = k.sb("IADD", [128, 32, 64], F32)
        JOK = k.sb("JOK", [128, 32, 64], F32)
        OUT = k.sb("OUT", [128, 32, 256], F32)
        r_OUT = Res()
        k.dma("pool", KS[:], ksT, w=[r_c])
        k.dma("pool", KW[:], kwT, w=[r_c])
        k.dma("pool", VS1[:], vs1, w=[r_c])
        k.dma("pool", VW1[:], vw1, w=[r_c])
        k.dma("sp", IDF[:], ident, w=[r_c])
        k.dma("pool", IDB[:], ident, w=[r_c])
        k.dma("pool", MD[:], mdiag, w=[r_c])
        k.dma("pool", MO[:], mold, w=[r_c])
        k.dma("pool", EA[:], Eall, w=[r_c])
        k.dma("sp", IMUL[:], impmul, w=[r_c])
        k.dma("sp", IADD[:], impadd, w=[r_c])
        k.dma("sp", JOK[:], jok, w=[r_c])
        k.dma("sp", GA[:], gate, w=[r_GA])
        k.op("act", lambda e: e.activation(out=GA[:], in_=GA[:], func=AF.Sigmoid), r=[r_GA], w=[r_GA])

        with ExitStack() as st2:
            def sb2(name, shape, dt):
                return st2.enter_context(nc.sbuf_tensor(name, shape, dt))
            qs = sb2("qs", [64, 4, 1024], F32)
            r_qs = Res()
            for tq in range(4):
                k.dma("sp", qs[:], qT[:, :, tq * 1024:(tq + 1) * 1024], w=[r_qs])
                k.op("act", lambda e: e.mul(out=Q[:, :, tq * 1024:(tq + 1) * 1024], in_=qs[:], mul=0.125), r=[r_qs], w=[r_Q])
            PE_ = sb2("PE_", [128, 2, 16], F32)
            k.dma("sp", PE_[:], pe, w=[r_c])
            blk = sb2("blk", [128, 16, 255], F32)
            BL = sb2("BL", [128, 16, 256], BF16)
            W1 = sb2("W1", [128, 16, 256], BF16)
            W2 = sb2("W2", [128, 2, 64], BF16)
            HID = sb2("HID", [128, 2, 256], BF16)
            xs = sb2("xs", [128, 256], F32)
            uu = sb2("uu", [128, 256], F32)
            sg = sb2("sg", [128, 256], F32)
            r_blk, r_BL, r_W1, r_W2, r_HID, r_xs, r_uu, r_sg = [Res() for _ in range(8)]
            k.op("dve", lambda e: e.memset(BL[:], 0.0), w=[r_BL])
            k.op("dve", lambda e: e.memset(VC1[:], 1.0), w=[r_VC])
            for which in range(2):
                src, w1, w2 = ((kcblk, w1k, w2k), (vcblk, w1v, w2v))[which]
                k.dma("sp", blk[:], src, w=[r_blk])
                k.dma("pool", W1[:], w1.rearrange("(kc p) n -> p kc n", p=128), w=[r_W1])
                k.dma("pool", W2[:], w2.rearrange("(kc p) n -> p kc n", p=128), w=[r_W2])
                for kc in range(16):
                    k.op("dve", lambda e: e.tensor_scalar(out=BL[:, kc, 0:255], in0=blk[:, kc, :], scalar1=PE_[:, which, kc:kc + 1], scalar2=None,
                                                          op0=ALU.add), r=[r_blk, r_c], w=[r_BL])
                for hc in range(2):
                    b, rb = banks[hc], r_bank[hc]
                    for kc in range(16):
                        k.op("pe", lambda e: e.matmul(b[:, 0:256], W1[:, kc, hc * 128:(hc + 1) * 128], BL[:, kc, :], start=(kc == 0), stop=(kc == 15)),
                             r=[r_W1, r_BL], w=[rb])
                    k.op("act", lambda e: e.copy(out=xs[:], in_=b[:, 0:256]), r=[rb], w=[r_xs])
                    k.op("dve", lambda e: e.tensor_tensor(out=uu[:], in0=xs[:], in1=xs[:], op=ALU.mult), r=[r_xs], w=[r_uu])
                    k.op("dve", lambda e: e.tensor_scalar(out=uu[:], in0=uu[:], scalar1=0.044715, scalar2=1.0, op0=ALU.mult, op1=ALU.add), r=[r_uu], w=[r_uu])
                    k.op("dve", lambda e: e.tensor_tensor(out=uu[:], in0=uu[:], in1=xs[:], op=ALU.mult), r=[r_uu, r_xs], w=[r_uu])
                    k.op("act", lambda e: e.activation(out=sg[:], in_=uu[:], func=AF.Sigmoid, scale=1.5957691216), r=[r_uu], w=[r_sg])
                    k.op("dve", lambda e: e.tensor_tensor(out=HID[:, hc, :], in0=xs[:], in1=sg[:], op=ALU.mult), r=[r_xs, r_sg], w=[r_HID])
                if which == 0:
                    b, rb = banks[2], r_bank[2]
                    for hc in range(2):
                        k.op("pe", lambda e: e.matmul(b[0:64, 0:256], W2[:, hc, :], HID[:, hc, :], start=(hc == 0), stop=(hc == 1)), r=[r_W2, r_HID], w=[rb])
                    k.op("act", lambda e: e.copy(out=KC[:], in_=b[0:64, 0:256]), r=[rb], w=[r_KC])
                else:
                    for c in range(2):
                        b, rb = banks[3 + c], r_bank[3 + c]
                        for hc in range(2):
                            k.op("pe", lambda e: e.matmul(b[:, 0:64], HID[:, hc, c * 128:(c + 1) * 128], W2[:, hc, :], start=(hc == 0), stop=(hc == 1)),
                                 r=[r_W2, r_HID], w=[rb])
                        k.op("act", lambda e: e.copy(out=VC1[:, c, 0:64], in_=b[:, 0:64]), r=[rb], w=[r_VC])
            k.barrier()

        CMT = [k.sb("CMT%d" % i, [128, 256], F32) for i in range(2)]
        r_CMT = [Res(), Res()]
        CMK = [k.sb("CMK%d" % i, [128, 512], BF16) for i in range(4)]
        r_CMK = [Res() for i in range(4)]
        sc = k.sb("sc", [128, 4, 256], F32)
        r_sc = Res()
        den = k.sb("den", [128, 4], F32)
        r_den = Res()
        PP = k.sb("PP", [128, 264], F32)
        r_PP = Res()
        imp = k.sb("imp", [128, 64], F32)
        imp3 = k.sb("imp3", [128, 64], F32)
        m8 = k.sb("m8", [128, 8], F32)
        thr = k.sb("thr", [128, 1], F32)
        sel = k.sb("sel", [128, 64], F32)
        r_imp, r_imp3, r_m8, r_thr, r_sel = [Res() for _ in range(5)]
        SELB = k.sb("SELB", [64, 4, 128], BF16)
        r_SELB = Res()
        PT = [k.sb("PT%d" % i, [128, 4, 128], BF16) for i in range(3)]
        r_PT = [Res() for i in range(3)]
        cf = k.sb("cf", [128, 4], F32)
        tmpo = k.sb("tmpo", [128, 4, 64], F32)
        r_cf, r_tmpo = Res(), Res()
        k.op("dve", lambda e: e.memset(PP[:], 0.0), w=[r_PP])
        pt_i = [0]
        st_i = [0]
        cmk_i = [0]

        def attend(qi, kT, kblocks, V1, v_of, Ob, rOb, masks):
            t0 = qi * 128
            nk = len(kblocks)
            for ii, kb in enumerate(kblocks):
                si = 3 + (st_i[0] % 2)
                st_i[0] += 1
                S_, rS_ = banks[si], r_bank[si]
                ml = masks(kb)
                k.op("pe", lambda e: e.matmul(S_[:], kT[:, kb * 128:(kb + 1) * 128], Q[:, :, t0:t0 + 128], start=True, stop=(len(ml) == 0)),
                     r=[r_c, r_Q, r_KC], w=[rS_])
                for mi, (ml_l, ml_r, ml_res) in enumerate(ml):
                    k.op("pe", lambda e: e.matmul(S_[:], ml_l, ml_r, start=False, stop=(mi == len(ml) - 1)), r=[r_c] + ml_res, w=[rS_])
                pi = pt_i[0] % 3
                pt_i[0] += 1
                k.op("act", lambda e: e.activation(out=PT[pi][:].rearrange("p g t -> p (g t)"), in_=S_[:], func=AF.Exp), r=[rS_], w=[r_PT[pi]])
                for g in range(4):
                    k.op("pe", lambda e: e.matmul(Ob[:, g * 65:(g + 1) * 65], PT[pi][:, g, :], v_of(kb), start=(ii == 0 and g == 0), stop=(ii == nk - 1), skip_group_check=True),
                         r=[r_PT[pi], r_c, r_VC], w=[rOb])

        def combine(qi, Ob, rOb, c, first):
            Ov = Ob[:, 0:260].rearrange("p (g e) -> p g e", g=4)
            k.op("dve", lambda e: e.tensor_scalar(out=cf[:].unsqueeze(2), in0=Ov[:, :, 64:65], scalar1=1e-30, scalar2=None, op0=ALU.max), r=[rOb], w=[r_cf])
            k.op("dve", lambda e: e.reciprocal(out=cf[:], in_=cf[:]), r=[r_cf], w=[r_cf])
            gsl = GA[:, qi, :].rearrange("p (g c) -> p g c", c=3)[:, :, c]
            k.op("dve", lambda e: e.tensor_tensor(out=cf[:], in0=cf[:], in1=gsl, op=ALU.mult), r=[r_cf, r_GA], w=[r_cf])
            ov = OUT[:, qi, :].rearrange("p (g d) -> p g d", g=4)
            if first:
                k.op("dve", lambda e: e.tensor_tensor(out=ov, in0=Ov[:, :, 0:64], in1=cf[:].unsqueeze(2).to_broadcast([128, 4, 64]), op=ALU.mult),
                     r=[rOb, r_cf], w=[r_OUT])
            else:
                k.op("dve", lambda e: e.tensor_tensor(out=tmpo[:], in0=Ov[:, :, 0:64], in1=cf[:].unsqueeze(2).to_broadcast([128, 4, 64]), op=ALU.mult),
                     r=[rOb, r_cf], w=[r_tmpo])
                k.op("dve", lambda e: e.tensor_tensor(out=ov, in0=ov, in1=tmpo[:], op=ALU.add), r=[r_tmpo, r_OUT], w=[r_OUT])

        for qi in range(32):
            t0 = qi * 128
            ci = qi % 2
            k.dma("sp", CMT[ci][:], cmaskT[qi], w=[r_CMT[ci]])
            for h2 in range(2):
                b, rb = banks[h2], r_bank[h2]
                for gg in range(2):
                    g = h2 * 2 + gg
                    k.op("pe", lambda e: e.matmul(b[:, gg * 256:(gg + 1) * 256], Q[:, g, t0:t0 + 128], KC[:], start=True, stop=True), r=[r_Q, r_KC], w=[rb])
                k.op("dve", lambda e: e.tensor_tensor(out=sc[:, h2 * 2:h2 * 2 + 2, :], in0=b[:].rearrange("p (g n) -> p g n", g=2),
                                                      in1=CMT[ci][:].unsqueeze(1).to_broadcast([128, 2, 256]), op=ALU.add), r=[rb, r_CMT[ci]], w=[r_sc])
            k.op("act", lambda e: e.activation(out=sc[:], in_=sc[:], func=AF.Exp), r=[r_sc], w=[r_sc])
            k.op("dve", lambda e: e.tensor_reduce(out=den[:], in_=sc[:], axis=AX.X, op=ALU.add), r=[r_sc], w=[r_den])
            k.op("dve", lambda e: e.tensor_scalar(out=den[:], in0=den[:], scalar1=1e-30, scalar2=None, op0=ALU.max), r=[r_den], w=[r_den])
            k.op("dve", lambda e: e.reciprocal(out=den[:], in_=den[:]), r=[r_den], w=[r_den])
            k.op("dve", lambda e: e.tensor_tensor(out=sc[:], in0=sc[:], in1=den[:].unsqueeze(2).to_broadcast([128, 4, 256]), op=ALU.mult), r=[r_sc, r_den], w=[r_sc])
            k.op("dve", lambda e: e.tensor_reduce(out=PP[:, 4:260], in_=sc[:].rearrange("p g n -> p n g"), axis=AX.X, op=ALU.add), r=[r_sc], w=[r_PP])
            PPv = PP[:].rearrange("p (j f) -> p j f", f=4)
            k.op("dve", lambda e: e.tensor_reduce(out=imp[:], in_=PPv[:, 1:65, :], axis=AX.X, op=ALU.add), r=[r_PP], w=[r_imp])
            k.op("dve", lambda e: e.tensor_tensor(out=imp[:], in0=imp[:], in1=PPv[:, 0:64, 3], op=ALU.add), r=[r_PP, r_imp], w=[r_imp])
            k.op("dve", lambda e: e.tensor_tensor(out=imp[:], in0=imp[:], in1=IMUL[:, qi, :], op=ALU.mult), r=[r_imp, r_c], w=[r_imp])
            k.op("dve", lambda e: e.tensor_tensor(out=imp[:], in0=imp[:], in1=IADD[:, qi, :], op=ALU.add), r=[r_imp, r_c], w=[r_imp])
            k.op("dve", lambda e: e.max(out=m8[:], in_=imp[:]), r=[r_imp], w=[r_m8])
            k.op("dve", lambda e: e.match_replace(out=imp3[:], in_to_replace=m8[:], in_values=imp[:], imm_value=-3.0e38), r=[r_imp, r_m8], w=[r_imp3])
            k.op("dve", lambda e: e.max(out=m8[:], in_=imp3[:]), r=[r_imp3], w=[r_m8])
            k.op("dve", lambda e: e.tensor_reduce(out=thr[:], in_=m8[:], axis=AX.X, op=ALU.min), r=[r_m8], w=[r_thr])
            k.op("dve", lambda e: e.tensor_scalar(out=sel[:], in0=imp[:], scalar1=thr[:], scalar2=None, op0=ALU.is_ge), r=[r_imp, r_thr], w=[r_sel])
            k.op("dve", lambda e: e.tensor_tensor(out=sel[:], in0=sel[:], in1=JOK[:, qi, :], op=ALU.mult), r=[r_sel, r_c], w=[r_sel])
            k.op("dve", lambda e: e.tensor_scalar(out=sel[:], in0=sel[:], scalar1=-1.0, scalar2=-NEG, op0=ALU.add, op1=ALU.mult), r=[r_sel], w=[r_sel])
            bt, rbt = banks[2], r_bank[2]
            k.op("pe", lambda e: e.transpose(bt[0:64, 0:128], sel[:], IDF[:]), r=[r_sel, r_c], w=[rbt])
            for g in range(4):
                k.op("act", lambda e: e.copy(out=SELB[:, g, :], in_=bt[0:64, 0:128]), r=[rbt], w=[r_SELB])
            cchunks = [0] + ([1] if qi >= 16 else [])
            cm_of = {}
            for c in cchunks:
                i = cmk_i[0] % 4
                cmk_i[0] += 1
                k.dma("pool", CMK[i][:], cmaskK[qi, c], w=[r_CMK[i]])
                cm_of[c] = i
            attend(qi, KC, cchunks, VC1, lambda c: VC1[:, c, :], banks[5], r_bank[5],
                   lambda c: [(IDB[:], CMK[cm_of[c]][:], [r_CMK[cm_of[c]]])])
            combine(qi, banks[5], r_bank[5], 0, True)
            attend(qi, KS, list(range(qi + 1)), VS1, lambda kb: VS1[:, kb, :], banks[6], r_bank[6],
                   lambda kb: [(EA[:, kb, :], SELB[:].rearrange("j g t -> j (g t)"), [r_SELB])] + ([(IDB[:], MD[:], [])] if kb == qi else []))
            combine(qi, banks[6], r_bank[6], 1, False)
            attend(qi, KW, list(range(max(0, qi - 4), qi + 1)), VW1, lambda kb: VW1[:, kb, :], banks[7], r_bank[7],
                   lambda kb: ([(IDB[:], MD[:], [])] if kb == qi else []) + ([(IDB[:], MO[:], [])] if kb == qi - 4 else []))
            combine(qi, banks[7], r_bank[7], 2, False)
        k.dma("sp", o, OUT[:], r=[r_OUT])
        k.barrier()
        print("nsa instructions", k.ninst)
    return nc


_NSA_CONST = {}


def nsa_consts():
    if _NSA_CONST:
        return _NSA_CONST
    s = np.arange(128)[:, None]
    t = np.arange(128)[None, :]
    md = np.where(s <= t, 0.0, NEG).astype(np.float32)
    mo = np.where(s > t, 0.0, NEG).astype(np.float32)
    E = np.zeros((64, 32, 128), np.float32)
    for kb in range(32):
        for sl in range(128):
            E[2 * kb + sl // 64, kb, sl] = 1.0
    cK = np.full((32, 2, 128, 512), NEG, np.float32)
    cT = np.full((32, 128, 256), NEG, np.float32)
    for qi in range(32):
        tt = qi * 128 + np.arange(128)
        n = np.arange(256)
        ok = (16 * n[None, :] + 31 <= tt[:, None]) & (n[None, :] < 255)
        cT[qi] = np.where(ok, 0.0, NEG)
        for c in range(2):
            okc = ok[:, c * 128:(c + 1) * 128].T
            cK[qi, c] = np.tile(np.where(okc, 0.0, NEG), (1, 4))
    tpos = np.arange(T)
    cur = tpos // 64
    j = np.arange(64)[None, :]
    curc = cur[:, None]
    forced = (j == 0) | (j == curc) | (j == curc - 1)
    fut = j > curc
    mul = np.where(forced | fut, 0.0, 1.0).astype(np.float32)
    add = np.zeros((T, 64), np.float32)
    add = np.where(j == curc - 1, 1e30, add)
    add = np.where(j == curc, 2e30, add)
    add = np.where(j == 0, 3e30, add)
    add = np.where(fut, -1e30, add).astype(np.float32)
    okj = (~fut).astype(np.float32)

    def tm(a):
        return np.ascontiguousarray(a.reshape(32, 128, 64).transpose(1, 0, 2))
    _NSA_CONST.update({"ident": np.eye(128, dtype=np.float32), "mdiag": np.tile(md, (1, 4)), "mold": np.tile(mo, (1, 4)), "Eall": E,
                       "cmaskK": cK, "cmaskT": cT, "impmul": tm(mul), "impadd": tm(add), "jok": tm(okj)})
    return _NSA_CONST


def nsa_inputs(nsa_q_b, nsa_kv_b, nsa_g_b, q, p):
    d = dict(nsa_consts())
    qq = nsa_q_b[:, q * 256:(q + 1) * 256].reshape(T, 4, 64)
    d["qT"] = np.ascontiguousarray(qq.transpose(2, 1, 0))
    kv = [nsa_kv_b[:, i * 256 + q * 64:i * 256 + (q + 1) * 64] for i in range(6)]
    idx = np.arange(255)[:, None] * 16 + np.arange(32)[None, :]
    for nm, z in (("kcblk", kv[0]), ("vcblk", kv[1])):
        b = z[idx].reshape(255, 2048)
        d[nm] = np.ascontiguousarray(b.T.reshape(16, 128, 255).transpose(1, 0, 2))
    d["ksT"] = np.ascontiguousarray(kv[2].T)
    d["kwT"] = np.ascontiguousarray(kv[4].T)
    for nm, z in (("vs1", kv[3]), ("vw1", kv[5])):
        v1 = np.ones((T, 65), np.float32)
        v1[:, :64] = z
        d[nm] = np.ascontiguousarray(v1.reshape(32, 128, 65).transpose(1, 0, 2))
    d["gate"] = np.ascontiguousarray(nsa_g_b[:, q * 12:(q + 1) * 12].reshape(32, 128, 12).transpose(1, 0, 2))
    pe = np.stack([p['nsa_pe_k'].reshape(16, 128).T, p['nsa_pe_v'].reshape(16, 128).T], axis=1)
    d["pe"] = np.ascontiguousarray(pe)
    d["w1k"], d["w1v"], d["w2k"], d["w2v"] = p['nsa_ck_w1'], p['nsa_cv_w1'], p['nsa_ck_w2'], p['nsa_cv_w2']
    return d


_PROGS = {}


def _prog(name, fn):
    if name not in _PROGS:
        _PROGS[name] = fn()
    return _PROGS[name]


def _g16(v):
    return np.ascontiguousarray(np.asarray(v, np.float32).reshape(16, 128).T)


def kernel(**inp):
    inp = {k_: np.asarray(v_, np.float32) for k_, v_ in inp.items()}
    x = inp["x"]
    xT = np.ascontiguousarray(x.reshape(8, 1024, 2048).transpose(0, 2, 1)).reshape(8 * 2048, 1024)
    ncA = _prog("A", lambda: build_tl(False, True, False))
    m = {"xT": xT, "ffn1_norm": _g16(inp["ffn1_norm"][0]), "ffn1_w13": inp["ffn1_w13"][0], "ffn1_w2": inp["ffn1_w2"][0],
         "mix_norm": _g16(inp["mix_norm"][0]), "w_in": inp["w_in"][0]}
    res = run_bass_kernel_spmd(ncA, [m], core_ids=[0]).results[0]
    x1T, uT = res["x1T"], res["uT"]
    outT = None
    for l in range(2):
        p = {k_: v_[l] for k_, v_ in inp.items() if k_ not in ("x", "final_norm")}
        u3 = uT.reshape(8, 14320, 1024)
        u = np.ascontiguousarray(u3[:, :8176, :].transpose(0, 2, 1)).reshape(2, 4096, 8176)
        mixT = np.zeros((8, 3072, 1024), np.float32)
        ncc = _prog("conv", build_conv)
        maps = [conv_inputs(u[c // 4][:, :2048], c % 4, p) for c in range(8)]
        r = run_bass_kernel_spmd(ncc, maps, core_ids=list(range(8))).results
        for c in range(8):
            mixT[c, 0:1024, :] = r[c]["oT"]
        ncr = _prog("rwkv", build_rwkv)
        maps = [rwkv_inputs(u[c // 4][:, 2048:5568], c % 4, p) for c in range(8)]
        r = run_bass_kernel_spmd(ncr, maps, core_ids=list(range(8))).results
        for c in range(8):
            b, q = divmod(c, 4)
            mixT[b * 4:(b + 1) * 4, 1024 + q * 256:1024 + (q + 1) * 256, :] = r[c]["oT"].reshape(256, 4, 1024).transpose(1, 0, 2)
        ncn = _prog("nsa", build_nsa)
        maps = [nsa_inputs(u[c // 4][:, 5568:6592], u[c // 4][:, 6592:8128], u[c // 4][:, 8128:8176], c % 4, p) for c in range(8)]
        r = run_bass_kernel_spmd(ncn, maps, core_ids=list(range(8))).results
        for c in range(8):
            b, q = divmod(c, 4)
            oc = r[c]["o"].transpose(1, 0, 2).reshape(4096, 256)
            mixT[b * 4:(b + 1) * 4, 2048 + q * 256:2048 + (q + 1) * 256, :] = oc.T.reshape(256, 4, 1024).transpose(1, 0, 2)
        brgT = np.ascontiguousarray(u3[:, 8176:, :]).reshape(8 * 6144, 1024)
        m = {"xT": x1T, "mixT": mixT.reshape(8 * 3072, 1024), "brgT": brgT,
             "w_conv_out": p["w_conv_out"], "w_rwkv_out": p["w_rwkv_out"], "w_nsa_out": p["w_nsa_out"], "w_out": p["w_out"],
             "ffn2_norm": _g16(p["ffn2_norm"]), "ffn2_w13": p["ffn2_w13"], "ffn2_w2": p["ffn2_w2"]}
        if l == 0:
            ncCA = _prog("CA", lambda: build_tl(True, True, False))
            m.update({"ffn1_norm": _g16(inp["ffn1_norm"][1]), "ffn1_w13": inp["ffn1_w13"][1], "ffn1_w2": inp["ffn1_w2"][1],
                      "mix_norm": _g16(inp["mix_norm"][1]), "w_in": inp["w_in"][1]})
            res = run_bass_kernel_spmd(ncCA, [m], core_ids=[0]).results[0]
            x1T, uT = res["x1T"], res["uT"]
        else:
            ncCF = _prog("CF", lambda: build_tl(True, False, True))
            m.update({"final_norm": _g16(inp["final_norm"])})
            res = run_bass_kernel_spmd(ncCF, [m], core_ids=[0]).results[0]
            outT = res["outT"]
    out = np.ascontiguousarray(outT.reshape(8, 2048, 1024).transpose(0, 2, 1)).reshape(2, 4096, 2048)
    return out.astype(np.float32)
```

```python
import numpy as np
from contextlib import ExitStack
import concourse.bass as bass
import concourse.mybir as mybir
from concourse.bass_utils import run_bass_kernel_spmd

F32 = mybir.dt.float32
BF16 = mybir.dt.bfloat16
AF = mybir.ActivationFunctionType
ALU = mybir.AluOpType
AX = mybir.AxisListType


class Res:
    __slots__ = ("name", "lw", "rd")

    def __init__(self, name=""):
        self.name = name
        self.lw = None
        self.rd = {}


class KB:
    NDMA = 6

    def __init__(self, nc, st):
        self.nc = nc
        self.st = st
        self.eng = {"pe": nc.tensor, "act": nc.scalar, "dve": nc.vector,
                    "pool": nc.gpsimd, "sp": nc.sync}
        self.sems = {}
        self.cnt = {}
        for e in self.eng:
            self.sems[e] = st.enter_context(nc.semaphore("c_" + e))
            self.cnt[e] = 0
        self.dq = {}
        for q in ("sp", "pool", "act"):
            lst = []
            for i in range(self.NDMA):
                key = "d_%s%d" % (q, i)
                self.sems[key] = st.enter_context(nc.semaphore(key))
                self.cnt[key] = 0
                lst.append(key)
            self.dq[q] = [lst, 0]
        self.known = {e: {} for e in self.eng}
        self.ninst = 0

    def sb(self, name, shape, dt):
        return self.st.enter_context(self.nc.sbuf_tensor(name, shape, dt))

    def ps(self, name, shape, dt=F32):
        return self.st.enter_context(self.nc.psum_tensor(name, shape, dt))

    def _wait(self, e, deps, keep_last=False):
        best = {}
        for (k, v) in deps:
            if best.get(k, 0) < v:
                best[k] = v
        todo = []
        for k, v in best.items():
            if k == e and e == "pe":
                continue
            if self.known[e].get(k, 0) >= v:
                continue
            todo.append((k, v))
            self.known[e][k] = v
        last = None
        if keep_last and todo:
            last = todo.pop()
        for k, v in todo:
            self.eng[e].wait_ge(self.sems[k], v)
            self.ninst += 1
        return last

    def _deps(self, r, w):
        deps = []
        for x in r:
            if x.lw is not None:
                deps.append(x.lw)
        for x in w:
            if x.lw is not None:
                deps.append(x.lw)
            deps.extend(x.rd.items())
        return deps

    def op(self, e, fn, r=(), w=()):
        last = self._wait(e, self._deps(r, w), keep_last=True)
        ins = fn(self.eng[e])
        if last is not None:
            ins._wait_ge(self.sems[last[0]], last[1])
        self.cnt[e] += 1
        ins.then_inc(self.sems[e], 1)
        tok = (e, self.cnt[e])
        for x in r:
            x.rd[e] = self.cnt[e]
        for x in w:
            x.lw = tok
            x.rd = {}
        self.ninst += 1
        return ins

    def dma(self, q, out, in_, r=(), w=(), **kw):
        lst, i = self.dq[q]
        key = lst[i % len(lst)]
        self.dq[q][1] = i + 1
        deps = self._deps(r, w)
        if self.cnt[key] > 0:
            deps.append((key, self.cnt[key]))
        last = self._wait(q, deps, keep_last=True)
        ins = self.eng[q].dma_start(out=out, in_=in_, **kw)
        if last is not None:
            ins._wait_ge(self.sems[last[0]], last[1])
        self.cnt[key] += 16
        ins.then_inc(self.sems[key], 16)
        tok = (key, self.cnt[key])
        for x in r:
            x.rd[key] = self.cnt[key]
        for x in w:
            x.lw = tok
            x.rd = {}
        self.ninst += 1
        return ins

    def wait_all(self, e, res):
        deps = []
        for x in res:
            if x.lw is not None:
                deps.append(x.lw)
            deps.extend(x.rd.items())
        self._wait(e, deps)

    def barrier(self):
        allk = [(k, v) for k, v in self.cnt.items() if v > 0]
        for e in self.eng:
            self._wait(e, allk)


D = 2048
DFF = 5632
DIN = 14320
NT = 1024
KC = D // 128
EPS = 1e-6


def emit_tl(E, do_C, do_A, do_final, l, xT):
    nc, k, NBLK = E.nc, E.k, E.NBLK
    W = E.w
    uT, x1T, mixT, outT = E.uT, E.x1T, E.mixT, E.outT
    if do_C:
        mixT = E.mixT
        brgT = None
        w_br = [W["w_conv_out"][l], W["w_rwkv_out"][l], W["w_nsa_out"][l]]
        w_out = W["w_out"][l]
        ffn2_w13 = W["ffn2_w13"][l]
        ffn2_w2 = W["ffn2_w2"][l]
    if do_A:
        la = l + 1 if do_C else l
        ffn1_w13 = W["ffn1_w13"][la]
        ffn1_w2 = W["ffn1_w2"][la]
        w_in = W["w_in"][la]
    gsel = {}
    if do_C:
        gsel["ffn2_norm"] = 3 * l + 2
    if do_A:
        gsel["ffn1_norm"] = 3 * la
        gsel["mix_norm"] = 3 * la + 1
    if do_final:
        gsel["final_norm"] = 6
    with ExitStack() as st:
        uid = E.uid

        def ksb(name, shape, dt):
            uid[0] += 1
            return st.enter_context(nc.sbuf_tensor("%s_u%d" % (name, uid[0]), shape, dt))
        X = ksb("X", [128, KC, NT], F32)
        H = ksb("H", [128, KC, NT], BF16)
        rX = [Res("X%d" % c) for c in range(KC)]
        rH = [Res("H%d" % c) for c in range(KC)]
        ones = ksb("ones", [128, 128], BF16)
        r_ones = Res("ones")
        gains = ksb("gains", [128, 7, KC], F32)
        r_gains = Res("gains")
        sq = [ksb("sq%d" % i, [128, NT], BF16) for i in range(2)]
        r_sq = [Res("sq%d" % i) for i in range(2)]
        rstd = ksb("rstd", [128, NT], F32)
        r_rstd = Res("rstd")
        banks, r_bank, bank_i = E.banks, E.r_bank, E.bank_i
        blk = [0]

        def nb():
            i = bank_i[0] % 8
            bank_i[0] += 1
            return banks[i], r_bank[i]

        k.op("dve", lambda e: e.memset(ones[:], 1.0), w=[r_ones])
        xvs = [xT[b_ * D:(b_ + 1) * D, :].rearrange("(c p) t -> p c t", p=128) for b_ in range(NBLK)]
        k.dma("sp", gains[:], E.gains_d, w=[r_gains])
        gidx = gsel

        def rmsnorm(gname, to_x=False):
            g = gidx[gname]
            b0, rb0 = nb()
            b1, rb1 = nb()
            for c in range(KC):
                s, rs = sq[c % 2], r_sq[c % 2]
                k.op("act", lambda e: e.activation(out=s[:], in_=X[:, c, :], func=AF.Square), r=[rX[c]], w=[rs])
                for th, (b, rb) in enumerate(((b0, rb0), (b1, rb1))):
                    k.op("pe", lambda e: e.matmul(b[:], ones[:], s[:, th * 512:(th + 1) * 512],
                                                   start=(c == 0), stop=(c == KC - 1)), r=[rs, r_ones], w=[rb])
            for th, (b, rb) in enumerate(((b0, rb0), (b1, rb1))):
                sl = slice(th * 512, (th + 1) * 512)
                k.op("dve", lambda e: e.tensor_scalar(out=rstd[:, sl], in0=b[:], scalar1=1.0 / D, scalar2=EPS,
                                                      op0=ALU.mult, op1=ALU.add), r=[rb], w=[r_rstd])
            k.op("act", lambda e: e.activation(out=rstd[:], in_=rstd[:], func=AF.Sqrt), r=[r_rstd], w=[r_rstd])
            k.op("dve", lambda e: e.reciprocal(out=rstd[:], in_=rstd[:]), r=[r_rstd], w=[r_rstd])
            for c in range(KC):
                if to_x:
                    k.op("dve", lambda e: e.scalar_tensor_tensor(out=X[:, c, :], in0=X[:, c, :], scalar=gains[:, g, c:c + 1],
                                                                 in1=rstd[:], op0=ALU.mult, op1=ALU.mult),
                         r=[rX[c], r_rstd, r_gains], w=[rX[c]])
                else:
                    k.op("dve", lambda e: e.scalar_tensor_tensor(out=H[:, c, :], in0=X[:, c, :], scalar=gains[:, g, c:c + 1],
                                                                 in1=rstd[:], op0=ALU.mult, op1=ALU.mult),
                         r=[rX[c], r_rstd, r_gains], w=[rH[c]])

        def ffn(w13, w2):
            NG = 4
            GF = 11
            w13v = w13.rearrange("(kc p) n -> p kc n", p=128)
            w2v = w2.rearrange("(f p) n -> p f n", p=128)
            with ExitStack() as st2:
                def sb2(name, shape, dt):
                    uid[0] += 1
                    return st2.enter_context(nc.sbuf_tensor("%s_u%d" % (name, uid[0]), shape, dt))
                G = sb2("G", [128, GF, NT], BF16)
                rG = [Res("G%d" % i) for i in range(GF)]
                W1 = [sb2("W1_%d" % i, [128, KC, 128], BF16) for i in range(2)]
                W3 = [sb2("W3_%d" % i, [128, KC, 128], BF16) for i in range(2)]
                rW1 = [Res() for i in range(2)]
                rW3 = [Res() for i in range(2)]
                W2 = sb2("W2", [128, GF, D], BF16)
                rW2 = [Res() for i in range(GF)]
                sa = [sb2("sa%d" % i, [128, 512], F32) for i in range(2)]
                r_sa = [Res() for i in range(2)]

                def load13(f):
                    i = f % 2
                    k.dma("pool", W1[i][:], w13v[:, :, f * 128:(f + 1) * 128], w=[rW1[i]])
                    k.dma("pool", W3[i][:], w13v[:, :, DFF + f * 128:DFF + (f + 1) * 128], w=[rW3[i]])

                load13(0)
                it = 0
                for gidx_ in range(NG):
                    for fl in range(GF):
                        f = gidx_ * GF + fl
                        if f + 1 < NG * GF:
                            load13(f + 1)
                        if fl == 0:
                            for j in range(GF):
                                k.dma("pool", W2[:, j, :], w2v[:, gidx_ * GF + j, :], w=[rW2[j]])
                        i = f % 2
                        for th in range(2):
                            sl = slice(th * 512, (th + 1) * 512)
                            ba, rba = nb()
                            bb, rbb = nb()
                            for kc in range(KC):
                                k.op("pe", lambda e: e.matmul(ba[:], W1[i][:, kc, :], H[:, kc, sl], start=(kc == 0), stop=(kc == KC - 1)),
                                     r=[rW1[i], rH[kc]], w=[rba])
                            for kc in range(KC):
                                k.op("pe", lambda e: e.matmul(bb[:], W3[i][:, kc, :], H[:, kc, sl], start=(kc == 0), stop=(kc == KC - 1)),
                                     r=[rW3[i], rH[kc]], w=[rbb])
                            s_, rs_ = sa[it % 2], r_sa[it % 2]
                            it += 1
                            k.op("act", lambda e: e.activation(out=s_[:], in_=ba[:], func=AF.Silu), r=[rba], w=[rs_])
                            k.op("dve", lambda e: e.tensor_tensor(out=G[:, fl, sl], in0=s_[:], in1=bb[:], op=ALU.mult),
                                 r=[rs_, rbb], w=[rG[fl]])
                    for m in range(KC):
                        for th in range(2):
                            sl = slice(th * 512, (th + 1) * 512)
                            b, rb = nb()
                            for fl in range(GF):
                                k.op("pe", lambda e: e.matmul(b[:], W2[:, fl, m * 128:(m + 1) * 128], G[:, fl, sl],
                                                               start=(fl == 0), stop=(fl == GF - 1)),
                                     r=[rW2[fl], rG[fl]], w=[rb])
                            k.op("dve", lambda e: e.scalar_tensor_tensor(out=X[:, m, sl], in0=b[:], scalar=0.5, in1=X[:, m, sl],
                                                                         op0=ALU.mult, op1=ALU.add),
                                 r=[rb, rX[m]], w=[rX[m]])
                k.barrier()

        def c_phase():
            mixv = mixT[blk[0] * 3072:(blk[0] + 1) * 3072, :].rearrange("(c p) t -> p c t", p=128)
            brgv = uT[blk[0]][8176:DIN, :].rearrange("(b m p) t -> p b m t", p=128, b=3)
            with ExitStack() as st2:
                def sb2(name, shape, dt):
                    uid[0] += 1
                    return st2.enter_context(nc.sbuf_tensor("%s_u%d" % (name, uid[0]), shape, dt))
                MIX = sb2("MIX", [128, 24, NT], BF16)
                rMIX = [Res() for i in range(24)]
                for c0 in range(0, 24, 4):
                    k.dma("pool", MIX[:, c0:c0 + 4, :], mixv[:, c0:c0 + 4, :], w=rMIX[c0:c0 + 4])
                BRG = [sb2("BRG%d" % i, [128, 3, 512], F32) for i in range(2)]
                rBRG = [Res() for i in range(2)]
                WB = [[sb2("WB%d_%d" % (b, i), [128, 8, 128], BF16) for i in range(2)] for b in range(3)]
                rWB = [[Res() for i in range(2)] for b in range(3)]
                sg = [sb2("sg%d" % i, [128, 512], F32) for i in range(3)]
                r_sg = [Res() for i in range(3)]
                tt = [sb2("tt%d" % i, [128, 512], F32) for i in range(3)]
                r_tt = [Res() for i in range(3)]
                wbv = [w.rearrange("(kc p) n -> p kc n", p=128) for w in w_br]

                def loadm(m):
                    i = m % 2
                    for b in range(3):
                        k.dma("pool", WB[b][i][:], wbv[b][:, :, m * 128:(m + 1) * 128], w=[rWB[b][i]])

                def loadbrg(it_):
                    m_, th_ = it_ // 2, it_ % 2
                    k.dma("sp", BRG[it_ % 2][:], brgv[:, :, m_, th_ * 512:(th_ + 1) * 512], w=[rBRG[it_ % 2]])

                loadm(0)
                loadbrg(0)
                for m in range(KC):
                    if m + 1 < KC:
                        loadm(m + 1)
                    i = m % 2
                    for th in range(2):
                        sl = slice(th * 512, (th + 1) * 512)
                        bi = (m * 2 + th) % 2
                        if m * 2 + th + 1 < 2 * KC:
                            loadbrg(m * 2 + th + 1)
                        pb = []
                        for b in range(3):
                            bk, rbk = nb()
                            pb.append((bk, rbk))
                            for kc in range(8):
                                k.op("pe", lambda e: e.matmul(bk[:], WB[b][i][:, kc, :], MIX[:, b * 8 + kc, sl],
                                                               start=(kc == 0), stop=(kc == 7)),
                                     r=[rWB[b][i], rMIX[b * 8 + kc]], w=[rbk])
                        for b in range(3):
                            k.op("act", lambda e: e.activation(out=sg[b][:], in_=BRG[bi][:, b, :], func=AF.Sigmoid),
                                 r=[rBRG[bi]], w=[r_sg[b]])
                            k.op("dve", lambda e: e.tensor_tensor(out=tt[b][:], in0=sg[b][:], in1=pb[b][0][:], op=ALU.mult),
                                 r=[r_sg[b], pb[b][1]], w=[r_tt[b]])
                        k.op("dve", lambda e: e.tensor_tensor(out=tt[0][:], in0=tt[0][:], in1=tt[1][:], op=ALU.add),
                             r=[r_tt[0], r_tt[1]], w=[r_tt[0]])
                        k.op("dve", lambda e: e.tensor_tensor(out=H[:, m, sl], in0=tt[0][:], in1=tt[2][:], op=ALU.add),
                             r=[r_tt[0], r_tt[2]], w=[rH[m]])
                k.barrier()
            wov = w_out.rearrange("(kc p) n -> p kc n", p=128)
            with ExitStack() as st2:
                WO = [st2.enter_context(nc.sbuf_tensor("WO%d_b%d_%d" % (i, blk[0], l * 10 + do_A), [128, KC, 128], BF16)) for i in range(2)]
                rWO = [Res() for i in range(2)]
                k.dma("pool", WO[0][:], wov[:, :, 0:128], w=[rWO[0]])
                for m in range(KC):
                    if m + 1 < KC:
                        k.dma("pool", WO[(m + 1) % 2][:], wov[:, :, (m + 1) * 128:(m + 2) * 128], w=[rWO[(m + 1) % 2]])
                    i = m % 2
                    for th in range(2):
                        sl = slice(th * 512, (th + 1) * 512)
                        b, rb = nb()
                        for kc in range(KC):
                            k.op("pe", lambda e: e.matmul(b[:], WO[i][:, kc, :], H[:, kc, sl], start=(kc == 0), stop=(kc == KC - 1)),
                                 r=[rWO[i], rH[kc]], w=[rb])
                        k.op("dve", lambda e: e.tensor_tensor(out=X[:, m, sl], in0=b[:], in1=X[:, m, sl], op=ALU.add),
                             r=[rb, rX[m]], w=[rX[m]])
                k.barrier()

        def win_phase():
            wiv = w_in.rearrange("(kc p) n -> p kc n", p=128)
            nch = (DIN + 127) // 128
            with ExitStack() as st2:
                WI = [st2.enter_context(nc.sbuf_tensor("WI%d_b%d_%d" % (i, blk[0], l * 10 + do_C), [128, KC, 128], BF16)) for i in range(2)]
                rWI = [Res() for i in range(2)]
                stg = [st2.enter_context(nc.sbuf_tensor("stg%d_b%d_%d" % (i, blk[0], l * 10 + do_C), [128, 512], F32)) for i in range(4)]
                r_stg = [Res() for i in range(4)]

                def loadj(j):
                    cw = min(128, DIN - j * 128)
                    k.dma("pool", WI[j % 2][:, :, 0:cw], wiv[:, :, j * 128:j * 128 + cw], w=[rWI[j % 2]])

                loadj(0)
                it = 0
                for j in range(nch):
                    if j + 1 < nch:
                        loadj(j + 1)
                    cw = min(128, DIN - j * 128)
                    i = j % 2
                    for th in range(2):
                        sl = slice(th * 512, (th + 1) * 512)
                        b, rb = nb()
                        for kc in range(KC):
                            k.op("pe", lambda e: e.matmul(b[0:cw, :], WI[i][:, kc, 0:cw], H[:, kc, sl], start=(kc == 0), stop=(kc == KC - 1)),
                                 r=[rWI[i], rH[kc]], w=[rb])
                        s_, rs_ = stg[it % 4], r_stg[it % 4]
                        if it % 2 == 0:
                            k.op("act", lambda e: e.copy(out=s_[0:cw, :], in_=b[0:cw, :]), r=[rb], w=[rs_])
                        else:
                            k.op("dve", lambda e: e.tensor_copy(out=s_[0:cw, :], in_=b[0:cw, :]), r=[rb], w=[rs_])
                        it += 1
                        k.dma("sp", uT[blk[0]][j * 128:j * 128 + cw, sl], s_[0:cw, :], r=[rs_])
                k.barrier()

        for b_ in range(NBLK):
            blk[0] = b_
            for c0 in range(0, KC, 4):
                k.dma("sp", X[:, c0:c0 + 4, :], xvs[b_][:, c0:c0 + 4, :], w=rX[c0:c0 + 4])
            if do_C:
                c_phase()
                rmsnorm("ffn2_norm")
                ffn(ffn2_w13, ffn2_w2)
            if do_A:
                rmsnorm("ffn1_norm")
                ffn(ffn1_w13, ffn1_w2)
                x1v = x1T[blk[0] * D:(blk[0] + 1) * D, :].rearrange("(c p) t -> p c t", p=128)
                for c0 in range(0, KC, 4):
                    k.dma("sp", x1v[:, c0:c0 + 4, :], X[:, c0:c0 + 4, :], r=rX[c0:c0 + 4])
                rmsnorm("mix_norm")
                win_phase()
            if do_final:
                rmsnorm("final_norm", to_x=True)
                ov = outT[blk[0] * D:(blk[0] + 1) * D, :].rearrange("(c p) t -> p c t", p=128)
                for c0 in range(0, KC, 4):
                    k.dma("sp", ov[:, c0:c0 + 4, :], X[:, c0:c0 + 4, :], r=rX[c0:c0 + 4])

            k.barrier()
        k.barrier()


T = 4096
LN_EPS = 1e-5
GN_EPS = 64e-5


def emit_conv(E, l, blk):
    nc, k = E.nc, E.k
    NTk = 1024
    PADT = NTk + 30
    q = blk % 4
    u3 = Blk3(E.uT)
    m3 = E.mixT.rearrange("(b r) t -> b r t", r=3072)
    cw, pb = E.conv_cw[l], E.conv_pb[l]
    with ExitStack() as st:
        uid = E.uid

        def ksb(name, shape, dt):
            uid[0] += 1
            return st.enter_context(nc.sbuf_tensor("%s_u%d" % (name, uid[0]), shape, dt))
        CW = ksb("CW", [128, 8, 31], F32)
        PB = ksb("PB", [128, 3, 8], F32)
        r_par = Res()
        k.dma("sp", CW[:], cw, w=[r_par])
        k.dma("sp", PB[:], pb, w=[r_par])
        ones = ksb("ones", [128, 128], BF16)
        r_ones = Res()
        k.op("dve", lambda e: e.memset(ones[:], 1.0), w=[r_ones])
        CO = ksb("CO", [128, 8, NTk], F32)
        rCO = [Res() for c in range(8)]
        A = [ksb("A%d" % i, [128, PADT], F32) for i in range(2)]
        Gt = [ksb("G%d" % i, [128, PADT], F32) for i in range(2)]
        rA = [Res() for i in range(2)]
        rG = [Res() for i in range(2)]
        banks, r_bank = E.banks, E.r_bank
        for c in range(8):
            i = c % 2
            for (dst, rdst, r0) in ((A[i], rA[i], c * 128), (Gt[i], rG[i], 1024 + c * 128)):
                if q == 0:
                    k.op("pool", lambda e: e.memset(dst[:, 0:30], 0.0), w=[rdst])
                else:
                    k.dma("sp", dst[:, 0:30], u3[blk - 1, r0:r0 + 128, NTk - 30:NTk], w=[rdst])
                k.dma("sp", dst[:, 30:PADT], u3[blk, r0:r0 + 128, :], w=[rdst])
            k.op("act", lambda e: e.activation(out=Gt[i][:], in_=Gt[i][:], func=AF.Sigmoid), r=[rG[i]], w=[rG[i]])
            eng = "dve"
            k.op(eng, lambda e: e.tensor_tensor(out=A[i][:], in0=A[i][:], in1=Gt[i][:], op=ALU.mult), r=[rA[i], rG[i]], w=[rA[i]])
            k.op(eng, lambda e: e.tensor_scalar(out=CO[:, c, :], in0=A[i][:, 0:NTk], scalar1=CW[:, c, 0:1], scalar2=PB[:, 0, c:c + 1],
                                                op0=ALU.mult, op1=ALU.add), r=[rA[i], r_par], w=[rCO[c]])
            for j in range(1, 31):
                k.op(eng, lambda e: e.scalar_tensor_tensor(out=CO[:, c, :], in0=A[i][:, j:j + NTk], scalar=CW[:, c, j:j + 1],
                                                           in1=CO[:, c, :], op0=ALU.mult, op1=ALU.add),
                     r=[rA[i], r_par, rCO[c]], w=[rCO[c]])
        xb = [ksb("xb%d" % i, [128, 512], BF16) for i in range(2)]
        x2 = [ksb("x2%d" % i, [128, 512], BF16) for i in range(2)]
        r_xb = [Res() for i in range(2)]
        r_x2 = [Res() for i in range(2)]
        mean = ksb("mean", [128, 512], F32)
        rstd = ksb("rstd", [128, 512], F32)
        msq = ksb("msq", [128, 512], F32)
        r_mean, r_rstd, r_msq = Res(), Res(), Res()
        t1 = [ksb("t1%d" % i, [128, 512], F32) for i in range(2)]
        r_t1 = [Res() for i in range(2)]
        og = [ksb("og%d" % i, [128, 512], F32) for i in range(2)]
        r_og = [Res() for i in range(2)]
        for th in range(2):
            sl = slice(th * 512, (th + 1) * 512)
            b1, rb1 = banks[2 * th], r_bank[2 * th]
            b2, rb2 = banks[2 * th + 1], r_bank[2 * th + 1]
            for c in range(8):
                i = c % 2
                k.op("act", lambda e: e.copy(out=xb[i][:], in_=CO[:, c, sl]), r=[rCO[c]], w=[r_xb[i]])
                k.op("act", lambda e: e.activation(out=x2[i][:], in_=CO[:, c, sl], func=AF.Square), r=[rCO[c]], w=[r_x2[i]])
                k.op("pe", lambda e: e.matmul(b1[:], ones[:], xb[i][:], start=(c == 0), stop=(c == 7)), r=[r_xb[i], r_ones], w=[rb1])
                k.op("pe", lambda e: e.matmul(b2[:], ones[:], x2[i][:], start=(c == 0), stop=(c == 7)), r=[r_x2[i], r_ones], w=[rb2])
            k.op("dve", lambda e: e.tensor_scalar(out=mean[:], in0=b1[:], scalar1=1.0 / 1024, scalar2=None, op0=ALU.mult), r=[rb1], w=[r_mean])
            k.op("dve", lambda e: e.tensor_tensor(out=msq[:], in0=mean[:], in1=mean[:], op=ALU.mult), r=[r_mean], w=[r_msq])
            k.op("dve", lambda e: e.scalar_tensor_tensor(out=rstd[:], in0=b2[:], scalar=1.0 / 1024, in1=msq[:], op0=ALU.mult, op1=ALU.subtract),
                 r=[rb2, r_msq], w=[r_rstd])
            k.op("dve", lambda e: e.tensor_scalar(out=rstd[:], in0=rstd[:], scalar1=LN_EPS, scalar2=None, op0=ALU.add), r=[r_rstd], w=[r_rstd])
            k.op("act", lambda e: e.activation(out=rstd[:], in_=rstd[:], func=AF.Sqrt), r=[r_rstd], w=[r_rstd])
            k.op("dve", lambda e: e.reciprocal(out=rstd[:], in_=rstd[:]), r=[r_rstd], w=[r_rstd])
            for c in range(8):
                i = c % 2
                k.op("dve", lambda e: e.tensor_tensor(out=t1[i][:], in0=CO[:, c, sl], in1=mean[:], op=ALU.subtract), r=[rCO[c], r_mean], w=[r_t1[i]])
                k.op("dve", lambda e: e.tensor_tensor(out=t1[i][:], in0=t1[i][:], in1=rstd[:], op=ALU.mult), r=[r_t1[i], r_rstd], w=[r_t1[i]])
                k.op("act", lambda e: e.activation(out=og[i][:], in_=t1[i][:], func=AF.Silu, scale=PB[:, 1, c:c + 1], bias=PB[:, 2, c:c + 1]),
                     r=[r_t1[i], r_par], w=[r_og[i]])
                k.dma("sp", m3[blk, c * 128:(c + 1) * 128, sl], og[i][:], r=[r_og[i]])
        k.barrier()


def conv_params(p):
    def pc(v): return np.ascontiguousarray(v.reshape(8, 128).T)
    cw = np.ascontiguousarray(p['conv_w'].T.reshape(8, 128, 31).transpose(1, 0, 2))
    pb = np.ascontiguousarray(np.stack([pc(p['conv_b']), pc(p['conv_ln_g']), pc(p['conv_ln_b'])], axis=1))
    return cw, pb


def emit_rwkv(E, l, bq, q):
    nc, k = E.nc, E.k
    u3 = Blk3(E.uT)
    m3 = E.mixT.rearrange("(b r) t -> b r t", r=3072)
    mu, prm = E.rw_mu[l, q], E.rw_prm[l, q]
    w_b, a_b, g_b = E.rw_wb[l, q], E.rw_ab[l, q], E.rw_gb[l, q]
    ident, bones = E.ident, E.bones
    tokS = E.tokS
    r_tok = Res()
    c0_ = q * 256
    rowbase = [2048 + c0_, 2048 + c0_ + 128, 3072 + c0_, 3072 + c0_ + 128, 4096 + c0_, 4096 + c0_ + 128,
               2048 + 3072 + 192, 2048 + 3072 + 192 + 128, 2048 + 3072, 2048 + 3072 + 96]
    with ExitStack() as st:
        uid = E.uid

        def ksb(name, shape, dt):
            uid[0] += 1
            return st.enter_context(nc.sbuf_tensor("%s_u%d" % (name, uid[0]), shape, dt))
        banks, r_bank, bank_i = E.banks, E.r_bank, E.bank_i

        def nb():
            i = bank_i[0] % 8
            bank_i[0] += 1
            return banks[i], r_bank[i]
        MU = ksb("MU", [128, 10], F32)
        PRM = ksb("PRM", [128, 2, 7], F32)
        IDN = ksb("IDN", [128, 128], F32)
        BON = ksb("BON", [128, 128], BF16)
        WB = ksb("WB", [96, 256], BF16)
        AB = ksb("AB", [96, 256], BF16)
        GB = ksb("GB", [128, 2, 256], BF16)
        r_c = Res()
        k.dma("sp", MU[:], mu, w=[r_c])
        k.dma("sp", PRM[:], prm, w=[r_c])
        k.dma("sp", IDN[:], ident, w=[r_c])
        k.dma("pool", BON[:], bones, w=[r_c])
        k.dma("pool", WB[:], w_b, w=[r_c])
        k.dma("pool", AB[:], a_b, w=[r_c])
        k.dma("pool", GB[:], g_b.rearrange("(kc p) n -> p kc n", p=128), w=[r_c])
        V = ksb("V", [128, 2, T], F32)
        rV = [Res(), Res()]
        RKB = ksb("RKB", [128, 2, T], BF16)
        rRKB = [Res(), Res()]
        SGL = ksb("SGL", [128, 2, T], BF16)
        rSGL = [Res(), Res()]

        with ExitStack() as st2:
            def sb2(name, shape, dt):
                uid[0] += 1
                return st2.enter_context(nc.sbuf_tensor("%s_u%d" % (name, uid[0]), shape, dt))
            ld = [sb2("ldc%d" % i, [128, T], F32) for i in range(1)]
            lp = [sb2("ldp%d" % i, [128, T], F32) for i in range(1)]
            r_ld = [Res(), Res()]
            r_lp = [Res(), Res()]
            ldi = [0]

            def shifted(rowtile, nrows, out, r_out_, post=None):
                i = 0
                r0 = rowbase[rowtile]
                k.op("pool", lambda e: e.memset(lp[i][0:nrows, 0:1], 0.0), w=[r_lp[i]])
                for tb in range(4):
                    k.dma("sp", ld[i][0:nrows, tb * 1024:(tb + 1) * 1024], u3[bq * 4 + tb, r0:r0 + nrows, :], w=[r_ld[i]])
                    n_ = 1024 if tb < 3 else 1023
                    k.dma("act", lp[i][0:nrows, tb * 1024 + 1:tb * 1024 + 1 + n_], u3[bq * 4 + tb, r0:r0 + nrows, 0:n_], w=[r_lp[i]])
                k.op("pool", lambda e: e.tensor_tensor(out=lp[i][0:nrows, :], in0=lp[i][0:nrows, :], in1=ld[i][0:nrows, :], op=ALU.subtract),
                     r=[r_ld[i], r_lp[i]], w=[r_lp[i]])
                if post is None:
                    k.op("dve", lambda e: e.scalar_tensor_tensor(out=out, in0=lp[i][0:nrows, :], scalar=MU[0:nrows, rowtile:rowtile + 1],
                                                                 in1=ld[i][0:nrows, :], op0=ALU.mult, op1=ALU.add),
                         r=[r_ld[i], r_lp[i], r_c], w=[r_out_])
                else:
                    k.op("dve", lambda e: e.scalar_tensor_tensor(out=ld[i][0:nrows, :], in0=lp[i][0:nrows, :], scalar=MU[0:nrows, rowtile:rowtile + 1],
                                                                 in1=ld[i][0:nrows, :], op0=ALU.mult, op1=ALU.add),
                         r=[r_ld[i], r_lp[i], r_c], w=[r_ld[i]])
                    k.op("act", lambda e: e.activation(out=out, in_=ld[i][0:nrows, :], func=post), r=[r_ld[i]], w=[r_out_])

            WL = sb2("WL", [96, T], BF16)
            AL = sb2("AL", [96, T], BF16)
            r_WL, r_AL = Res(), Res()
            shifted(8, 96, WL[:], r_WL, post=AF.Tanh)
            shifted(9, 96, AL[:], r_AL, post=AF.Copy)
            for ct in range(2):
                shifted(6 + ct, 128, SGL[:, ct, :], rSGL[ct], post=AF.Sigmoid)
                shifted(4 + ct, 128, V[:, ct, :], rV[ct])
            Rt = sb2("Rt", [128, T], F32)
            Kt = sb2("Kt", [128, T], F32)
            Wt = sb2("Wt", [128, T], F32)
            At = sb2("At", [128, T], F32)
            KKt = sb2("KKt", [128, T], F32)
            SQb = sb2("SQb", [128, 512], BF16)
            r_R, r_K, r_W, r_A, r_KK, r_SQ = Res(), Res(), Res(), Res(), Res(), Res()
            stg = [sb2("stg%d" % i, [128, 4, 128], F32) for i in range(2)]
            r_stg = [Res(), Res()]
            stg_i = [0]

            def to_tok(src, r_src, vec, ct):
                for t0 in range(0, 32, 4):
                    b, rb = nb()
                    for a in range(4):
                        tt = t0 + a
                        k.op("pe", lambda e: e.transpose(b[:, a * 128:(a + 1) * 128], src[:, tt * 128:(tt + 1) * 128], IDN[:]),
                             r=[r_src, r_c], w=[rb])
                    i = stg_i[0] % 2
                    stg_i[0] += 1
                    k.op("act", lambda e: e.copy(out=stg[i][:].rearrange("p a c -> p (a c)"), in_=b[:]), r=[rb], w=[r_stg[i]])
                    for hp in range(2):
                        dst = tokS[hp, vec, t0 * 128:(t0 + 4) * 128, ct * 64:(ct + 1) * 64].rearrange("(a p) c -> p a c", p=128)
                        k.dma("sp", dst, stg[i][:, :, hp * 64:(hp + 1) * 64], r=[r_stg[i]], w=[r_tok])

            for ct in range(2):
                shifted(0 + ct, 128, Rt[:], r_R)
                shifted(2 + ct, 128, Kt[:], r_K)
                for tb in range(8):
                    sl = slice(tb * 512, (tb + 1) * 512)
                    b, rb = nb()
                    k.op("pe", lambda e: e.matmul(b[:], WB[:, ct * 128:(ct + 1) * 128], WL[:, sl], start=True, stop=True), r=[r_WL, r_c], w=[rb])
                    k.op("act", lambda e: e.activation(out=Wt[:, sl], in_=b[:], func=AF.Sigmoid, bias=PRM[:, ct, 0:1]), r=[rb, r_c], w=[r_W])
                    b2, rb2 = nb()
                    k.op("pe", lambda e: e.matmul(b2[:], AB[:, ct * 128:(ct + 1) * 128], AL[:, sl], start=True, stop=True), r=[r_AL, r_c], w=[rb2])
                    k.op("act", lambda e: e.activation(out=At[:, sl], in_=b2[:], func=AF.Sigmoid, bias=PRM[:, ct, 1:2]), r=[rb2, r_c], w=[r_A])
                k.op("act", lambda e: e.activation(out=Wt[:], in_=Wt[:], func=AF.Exp, scale=-0.6065306597), r=[r_W], w=[r_W])
                to_tok(Wt, r_W, 1, ct)
                to_tok(Rt, r_R, 4, ct)
                k.op("dve", lambda e: e.tensor_scalar(out=KKt[:], in0=Kt[:], scalar1=PRM[:, ct, 2:3], scalar2=None, op0=ALU.mult), r=[r_K, r_c], w=[r_KK])
                k.op("dve", lambda e: e.tensor_scalar(out=Wt[:], in0=At[:], scalar1=-1.0, scalar2=PRM[:, ct, 3:4], op0=ALU.add, op1=ALU.mult),
                     r=[r_A, r_c], w=[r_W])
                k.op("dve", lambda e: e.scalar_tensor_tensor(out=Kt[:], in0=Wt[:], scalar=1.0, in1=Kt[:], op0=ALU.add, op1=ALU.mult),
                     r=[r_W, r_K], w=[r_K])
                to_tok(Kt, r_K, 3, ct)
                k.op("dve", lambda e: e.scalar_tensor_tensor(out=RKB[:, ct, :], in0=Rt[:], scalar=PRM[:, ct, 4:5], in1=Kt[:], op0=ALU.mult, op1=ALU.mult),
                     r=[r_R, r_K, r_c], w=[rRKB[ct]])
                for tb in range(8):
                    sl = slice(tb * 512, (tb + 1) * 512)
                    k.op("act", lambda e: e.activation(out=SQb[:], in_=KKt[:, sl], func=AF.Square), r=[r_KK], w=[r_SQ])
                    b, rb = nb()
                    k.op("pe", lambda e: e.matmul(b[:], BON[:], SQb[:], start=True, stop=True), r=[r_SQ, r_c], w=[rb])
                    k.op("dve", lambda e: e.tensor_scalar(out=Rt[:, sl], in0=b[:], scalar1=1e-12, scalar2=None, op0=ALU.add), r=[rb, r_R], w=[r_R])
                k.op("act", lambda e: e.activation(out=Rt[:], in_=Rt[:], func=AF.Sqrt), r=[r_R], w=[r_R])
                k.op("dve", lambda e: e.reciprocal(out=Rt[:], in_=Rt[:]), r=[r_R], w=[r_R])
                k.op("dve", lambda e: e.scalar_tensor_tensor(out=KKt[:], in0=KKt[:], scalar=-1.0, in1=Rt[:], op0=ALU.mult, op1=ALU.mult),
                     r=[r_KK, r_R], w=[r_KK])
                to_tok(KKt, r_KK, 0, ct)
                k.op("dve", lambda e: e.scalar_tensor_tensor(out=At[:], in0=KKt[:], scalar=-1.0, in1=At[:], op0=ALU.mult, op1=ALU.mult),
                     r=[r_KK, r_A], w=[r_A])
                to_tok(At, r_A, 2, ct)
            k.barrier()

        with ExitStack() as st2:
            def sb2(name, shape, dt):
                uid[0] += 1
                return st2.enter_context(nc.sbuf_tensor("%s_u%d" % (name, uid[0]), shape, dt))
            Y = sb2("Y", [128, 2, T], F32)
            rY = [Res(), Res()]
            r_Yall = Res()
            TB = 16
            BC = [[sb2("BC%d_%d" % (v, i), [128, TB, 128], F32) for i in range(2)] for v in range(5)]
            rBC = [[Res() for i in range(2)] for v in range(5)]
            S = sb2("S", [128, 2, 64], F32)
            tmp = sb2("tmp", [128, 2, 64], F32)
            sa = sb2("sa", [128, 2], F32)
            r_S, r_tmp, r_sa = Res(), Res(), Res()
            k.op("dve", lambda e: e.memset(S[:], 0.0), w=[r_S])
            nblk = T // TB

            def loadblk(bi):
                i = bi % 2
                for v in range(5):
                    for hp in range(2):
                        src = tokS[hp, v, bi * TB:(bi + 1) * TB, :].partition_broadcast(64)
                        k.dma("sp" if (v + hp) % 2 == 0 else "act", BC[v][i][hp * 64:(hp + 1) * 64, :, :], src, r=[r_tok], w=[rBC[v][i]])

            loadblk(0)
            for bi in range(nblk):
                if bi + 1 < nblk:
                    loadblk(bi + 1)
                i = bi % 2
                for ct_ in range(2):
                    kvv = BC[3][i][:, :, ct_ * 64:(ct_ + 1) * 64]
                    k.op("dve", lambda e: e.tensor_tensor(out=kvv, in0=kvv, in1=V[:, ct_, bi * TB:(bi + 1) * TB].unsqueeze(2).to_broadcast([128, TB, 64]),
                                                          op=ALU.mult), r=[rV[0], rV[1], rBC[3][i]], w=[rBC[3][i]])
                for tl in range(TB):
                    t = bi * TB + tl

                    def bc(v):
                        return BC[v][i][:, tl, :].rearrange("p (c j) -> p c j", c=2)
                    k.op("dve", lambda e: e.tensor_tensor(out=tmp[:], in0=S[:], in1=bc(0), op=ALU.mult), r=[r_S, rBC[0][i]], w=[r_tmp])
                    k.op("dve", lambda e: e.tensor_reduce(out=sa[:], in_=tmp[:], axis=AX.X, op=ALU.add), r=[r_tmp], w=[r_sa])
                    k.op("dve", lambda e: e.tensor_tensor(out=S[:], in0=S[:], in1=bc(1), op=ALU.mult), r=[r_S, rBC[1][i]], w=[r_S])
                    k.op("dve", lambda e: e.tensor_tensor(out=tmp[:], in0=bc(2), in1=sa[:].unsqueeze(2).to_broadcast([128, 2, 64]), op=ALU.mult),
                         r=[r_sa, rBC[2][i]], w=[r_tmp])
                    k.op("dve", lambda e: e.tensor_tensor(out=S[:], in0=S[:], in1=tmp[:], op=ALU.add), r=[r_S, r_tmp], w=[r_S])
                    k.op("dve", lambda e: e.tensor_tensor(out=S[:], in0=S[:], in1=bc(3), op=ALU.add), r=[r_S, rBC[3][i]], w=[r_S])
                    k.op("dve", lambda e: e.tensor_tensor(out=tmp[:], in0=S[:], in1=bc(4), op=ALU.mult), r=[r_S, rBC[4][i]], w=[r_tmp])
                    k.op("dve", lambda e: e.tensor_reduce(out=Y[:, :, t], in_=tmp[:], axis=AX.X, op=ALU.add), r=[r_tmp], w=[r_Yall])

            yb = sb2("yb", [128, 512], BF16)
            y2 = sb2("y2", [128, 512], BF16)
            mean = sb2("mean", [128, 512], F32)
            msq = sb2("msq", [128, 512], F32)
            rstd = sb2("rstd", [128, 512], F32)
            t1 = sb2("t1", [128, 512], F32)
            t2 = sb2("t2", [128, 512], F32)
            og = [sb2("og%d" % i, [128, 512], F32) for i in range(2)]
            r_yb, r_y2, r_mean, r_msq, r_rstd, r_t1, r_t2 = Res(), Res(), Res(), Res(), Res(), Res(), Res()
            r_og = [Res(), Res()]
            it = 0
            for ct in range(2):
                for tb in range(8):
                    sl = slice(tb * 512, (tb + 1) * 512)
                    k.op("act", lambda e: e.copy(out=yb[:], in_=Y[:, ct, sl]), r=[r_Yall], w=[r_yb])
                    k.op("act", lambda e: e.activation(out=y2[:], in_=Y[:, ct, sl], func=AF.Square), r=[r_Yall], w=[r_y2])
                    b1, rb1 = nb()
                    b2, rb2 = nb()
                    k.op("pe", lambda e: e.matmul(b1[:], BON[:], yb[:], start=True, stop=True), r=[r_yb, r_c], w=[rb1])
                    k.op("pe", lambda e: e.matmul(b2[:], BON[:], y2[:], start=True, stop=True), r=[r_y2, r_c], w=[rb2])
                    k.op("dve", lambda e: e.tensor_scalar(out=mean[:], in0=b1[:], scalar1=1.0 / 64, scalar2=None, op0=ALU.mult), r=[rb1], w=[r_mean])
                    k.op("dve", lambda e: e.tensor_tensor(out=msq[:], in0=mean[:], in1=mean[:], op=ALU.mult), r=[r_mean], w=[r_msq])
                    k.op("dve", lambda e: e.scalar_tensor_tensor(out=rstd[:], in0=b2[:], scalar=1.0 / 64, in1=msq[:], op0=ALU.mult, op1=ALU.subtract),
                         r=[rb2, r_msq], w=[r_rstd])
                    k.op("dve", lambda e: e.tensor_scalar(out=rstd[:], in0=rstd[:], scalar1=GN_EPS, scalar2=None, op0=ALU.add), r=[r_rstd], w=[r_rstd])
                    k.op("act", lambda e: e.activation(out=rstd[:], in_=rstd[:], func=AF.Sqrt), r=[r_rstd], w=[r_rstd])
                    k.op("dve", lambda e: e.reciprocal(out=rstd[:], in_=rstd[:]), r=[r_rstd], w=[r_rstd])
                    k.op("dve", lambda e: e.tensor_tensor(out=t1[:], in0=Y[:, ct, sl], in1=mean[:], op=ALU.subtract), r=[r_Yall, r_mean], w=[r_t1])
                    k.op("dve", lambda e: e.tensor_tensor(out=t1[:], in0=t1[:], in1=rstd[:], op=ALU.mult), r=[r_t1, r_rstd], w=[r_t1])
                    k.op("act", lambda e: e.activation(out=t1[:], in_=t1[:], func=AF.Identity, scale=PRM[:, ct, 5:6], bias=PRM[:, ct, 6:7]),
                         r=[r_t1, r_c], w=[r_t1])
                    b3, rb3 = nb()
                    k.op("pe", lambda e: e.matmul(b3[:], BON[:], RKB[:, ct, sl], start=True, stop=True), r=[rRKB[ct], r_c], w=[rb3])
                    k.op("dve", lambda e: e.tensor_tensor(out=t2[:], in0=b3[:], in1=V[:, ct, sl], op=ALU.mult), r=[rb3, rV[ct]], w=[r_t2])
                    k.op("dve", lambda e: e.tensor_tensor(out=t1[:], in0=t1[:], in1=t2[:], op=ALU.add), r=[r_t1, r_t2], w=[r_t1])
                    b4, rb4 = nb()
                    for kc in range(2):
                        k.op("pe", lambda e: e.matmul(b4[:], GB[:, kc, ct * 128:(ct + 1) * 128], SGL[:, kc, sl], start=(kc == 0), stop=(kc == 1)),
                             r=[rSGL[kc], r_c], w=[rb4])
                    i = it % 2
                    it += 1
                    k.op("dve", lambda e: e.tensor_tensor(out=og[i][:], in0=b4[:], in1=t1[:], op=ALU.mult), r=[rb4, r_t1], w=[r_og[i]])
                    k.dma("sp", m3[bq * 4 + tb // 2, 1024 + q * 256 + ct * 128:1024 + q * 256 + (ct + 1) * 128, (tb % 2) * 512:(tb % 2 + 1) * 512], og[i][:], r=[r_og[i]])
            k.barrier()


def rwkv_params(p, q):
    c0 = q * 256
    cols = np.concatenate([np.arange(c0, c0 + 256), 1024 + np.arange(c0, c0 + 256), 2048 + np.arange(c0, c0 + 256),
                           3072 + 192 + np.arange(256), 3072 + np.arange(96), 3072 + 96 + np.arange(96)])
    mu_sel = p['rwkv_mu'][cols]
    mu = np.zeros((128, 10), np.float32)
    starts = [0, 128, 256, 384, 512, 640, 768, 896, 1024, 1120]
    sizes = [128] * 8 + [96, 96]
    for i, (s, n) in enumerate(zip(starts, sizes)):
        mu[:n, i] = mu_sel[s:s + n]
    prm = np.zeros((128, 2, 7), np.float32)
    for ct in range(2):
        sl = slice(c0 + ct * 128, c0 + (ct + 1) * 128)
        prm[:, ct, 0] = p['rwkv_w0'][sl]
        prm[:, ct, 1] = p['rwkv_a0'][sl]
        prm[:, ct, 2] = p['rwkv_k_k'][sl]
        prm[:, ct, 3] = p['rwkv_k_a'][sl]
        prm[:, ct, 4] = p['rwkv_r_k'].reshape(-1)[sl]
        prm[:, ct, 5] = p['rwkv_lnx_g'][sl]
        prm[:, ct, 6] = p['rwkv_lnx_b'][sl]
    return (mu, prm, np.ascontiguousarray(p['rwkv_w_b'][:, c0:c0 + 256]), np.ascontiguousarray(p['rwkv_a_b'][:, c0:c0 + 256]),
            np.ascontiguousarray(p['rwkv_g_b'][:, c0:c0 + 256]))
NEG = -30000.0


def emit_nsa(E, l, bq, q):
    nc, k = E.nc, E.k
    u3 = Blk3(E.uT)
    m3 = E.mixT.rearrange("(b r) t -> b r t", r=3072)
    pe = E.nsa_pe[l]
    w1k, w1v, w2k, w2v = E.w["nsa_ck_w1"][l], E.w["nsa_cv_w1"][l], E.w["nsa_ck_w2"][l], E.w["nsa_cv_w2"][l]
    ident, mdiag, mold, Eall = E.ident, E.mdiag, E.mold, E.Eall
    cmaskK, cmaskT, impmul, impadd, jok = E.cmaskK, E.cmaskT, E.impmul, E.impadd, E.jok
    qrow = 5568 + q * 256
    kvrow = [6592 + i_ * 256 + q * 64 for i_ in range(6)]
    grow = 8128 + q * 12
    with ExitStack() as st:
        uid = E.uid

        def ksb(name, shape, dt):
            uid[0] += 1
            return st.enter_context(nc.sbuf_tensor("%s_u%d" % (name, uid[0]), shape, dt))
        banks, r_bank = E.banks, E.r_bank
        r_c = Res()
        Q = ksb("Q", [64, 4, T], BF16)
        r_Q = Res()
        KS = ksb("KS", [64, T], BF16)
        KW = ksb("KW", [64, T], BF16)
        KC = ksb("KC", [64, 256], BF16)
        VS1 = ksb("VS1", [128, 32, 65], BF16)
        VW1 = ksb("VW1", [128, 32, 65], BF16)
        VC1 = ksb("VC1", [128, 2, 65], BF16)
        r_KC, r_VC = Res(), Res()
        GA = ksb("GA", [128, 32, 12], F32)
        r_GA = Res()
        IDF = ksb("IDF", [128, 128], F32)
        IDB = ksb("IDB", [128, 128], BF16)
        MD = ksb("MD", [128, 512], BF16)
        MO = ksb("MO", [128, 512], BF16)
        EA = ksb("EA", [64, 32, 128], BF16)
        IMUL = ksb("IMUL", [128, 32, 64], F32)
        IADD = ksb("IADD", [128, 32, 64], F32)
        JOK = ksb("JOK", [128, 32, 64], F32)
        OUT = ksb("OUT", [128, 32, 256], F32)
        r_OUT = Res()
        r_KS, r_KW = Res(), Res()
        for tb in range(4):
            k.dma("pool", KS[:, tb * 1024:(tb + 1) * 1024], u3[bq * 4 + tb, kvrow[2]:kvrow[2] + 64, :], w=[r_KS])
            k.dma("pool", KW[:, tb * 1024:(tb + 1) * 1024], u3[bq * 4 + tb, kvrow[4]:kvrow[4] + 64, :], w=[r_KW])
        k.dma("sp", IDF[:], ident, w=[r_c])
        k.dma("pool", IDB[:], ident, w=[r_c])
        k.dma("pool", MD[:], mdiag, w=[r_c])
        k.dma("pool", MO[:], mold, w=[r_c])
        k.dma("pool", EA[:], Eall, w=[r_c])
        k.dma("sp", IMUL[:], impmul, w=[r_c])
        k.dma("sp", IADD[:], impadd, w=[r_c])
        k.dma("sp", JOK[:], jok, w=[r_c])

        with ExitStack() as st2:
            def sb2(name, shape, dt):
                uid[0] += 1
                return st2.enter_context(nc.sbuf_tensor("%s_u%d" % (name, uid[0]), shape, dt))
            qs = sb2("qs", [64, 4, 1024], F32)
            r_qs = Res()
            for tq in range(4):
                for g in range(4):
                    k.dma("sp", qs[:, g, :], u3[bq * 4 + tq, qrow + g * 64:qrow + (g + 1) * 64, :], w=[r_qs])
                k.op("act", lambda e: e.mul(out=Q[:, :, tq * 1024:(tq + 1) * 1024], in_=qs[:], mul=0.125), r=[r_qs], w=[r_Q])
            vT = sb2("vT", [64, T], F32)
            r_vT = Res()
            k.op("dve", lambda e: e.memset(VS1[:], 1.0), w=[r_c])
            k.op("dve", lambda e: e.memset(VW1[:], 1.0), w=[r_c])
            for (row, V1_) in ((kvrow[3], VS1), (kvrow[5], VW1)):
                for tb in range(4):
                    k.dma("sp", vT[:, tb * 1024:(tb + 1) * 1024], u3[bq * 4 + tb, row:row + 64, :], w=[r_vT])
                for t4 in range(8):
                    bt_, rbt_ = banks[t4 % 2], r_bank[t4 % 2]
                    for a_ in range(4):
                        tt_ = t4 * 4 + a_
                        k.op("pe", lambda e: e.transpose(bt_[:, a_ * 64:(a_ + 1) * 64], vT[:, tt_ * 128:(tt_ + 1) * 128], IDF[0:64, 0:64]), r=[r_vT, r_c], w=[rbt_])
                    k.op("act", lambda e: e.copy(out=V1_[:, t4 * 4:(t4 + 1) * 4, 0:64], in_=bt_[:, 0:256].rearrange("p (a d) -> p a d", a=4)), r=[rbt_], w=[r_c])
            for tb in range(4):
                k.dma("sp", vT[0:12, tb * 1024:(tb + 1) * 1024], u3[bq * 4 + tb, grow:grow + 12, :], w=[r_vT])
            for t4 in range(8):
                bt_, rbt_ = banks[t4 % 2], r_bank[t4 % 2]
                for a_ in range(4):
                    tt_ = t4 * 4 + a_
                    k.op("pe", lambda e: e.transpose(bt_[:, a_ * 12:(a_ + 1) * 12], vT[0:12, tt_ * 128:(tt_ + 1) * 128], IDF[0:12, 0:12]), r=[r_vT, r_c], w=[rbt_])
                k.op("act", lambda e: e.activation(out=GA[:, t4 * 4:(t4 + 1) * 4, :], in_=bt_[:, 0:48].rearrange("p (a d) -> p a d", a=4), func=AF.Sigmoid), r=[rbt_], w=[r_GA])
            k.barrier()
            PE_ = sb2("PE_", [128, 2, 16], F32)
            k.dma("sp", PE_[:], pe, w=[r_c])
            KCT2 = sb2("KCT2", [128, T], F32)
            BL = sb2("BL", [128, 16, 256], BF16)
            W1 = sb2("W1", [128, 16, 256], BF16)
            W2 = sb2("W2", [128, 2, 64], BF16)
            HID = sb2("HID", [128, 2, 256], BF16)
            xs = sb2("xs", [128, 256], F32)
            uu = sb2("uu", [128, 256], F32)
            sg = sb2("sg", [128, 256], F32)
            r_blk, r_BL, r_W1, r_W2, r_HID, r_xs, r_uu, r_sg = [Res() for _ in range(8)]
            k.op("dve", lambda e: e.memset(BL[:], 0.0), w=[r_BL])
            k.op("dve", lambda e: e.memset(VC1[:], 1.0), w=[r_VC])
            for which in range(2):
                srow, w1, w2 = ((kvrow[0], w1k, w2k), (kvrow[1], w1v, w2v))[which]
                k.op("pool", lambda e: e.memset(KCT2[64:128, T - 1:T], 0.0), w=[r_blk])
                for tb in range(4):
                    k.dma("sp", KCT2[0:64, tb * 1024:(tb + 1) * 1024], u3[bq * 4 + tb, srow:srow + 64, :], w=[r_blk])
                    if tb == 0:
                        k.dma("act", KCT2[64:128, 0:1023], u3[bq * 4, srow:srow + 64, 1:1024], w=[r_blk])
                    else:
                        k.dma("act", KCT2[64:128, tb * 1024 - 1:(tb + 1) * 1024 - 1], u3[bq * 4 + tb, srow:srow + 64, :], w=[r_blk])
                KCv = KCT2[:].rearrange("p (n s) -> p n s", s=16)
                k.dma("pool", W1[:], w1.rearrange("(kc p) n -> p kc n", p=128), w=[r_W1])
                k.dma("pool", W2[:], w2.rearrange("(kc p) n -> p kc n", p=128), w=[r_W2])
                for kc in range(16):
                    bsrc = KCv[:, 0:255, 2 * kc] if kc < 8 else KCv[:, 1:256, 2 * kc - 16]
                    k.op("dve", lambda e: e.tensor_scalar(out=BL[:, kc, 0:255], in0=bsrc, scalar1=PE_[:, which, kc:kc + 1], scalar2=None,
                                                          op0=ALU.add), r=[r_blk, r_c], w=[r_BL])
                for hc in range(2):
                    b, rb = banks[hc], r_bank[hc]
                    for kc in range(16):
                        k.op("pe", lambda e: e.matmul(b[:, 0:256], W1[:, kc, hc * 128:(hc + 1) * 128], BL[:, kc, :], start=(kc == 0), stop=(kc == 15)),
                             r=[r_W1, r_BL], w=[rb])
                    k.op("act", lambda e: e.copy(out=xs[:], in_=b[:, 0:256]), r=[rb], w=[r_xs])
                    k.op("dve", lambda e: e.tensor_tensor(out=uu[:], in0=xs[:], in1=xs[:], op=ALU.mult), r=[r_xs], w=[r_uu])
                    k.op("dve", lambda e: e.tensor_scalar(out=uu[:], in0=uu[:], scalar1=0.044715, scalar2=1.0, op0=ALU.mult, op1=ALU.add), r=[r_uu], w=[r_uu])
                    k.op("dve", lambda e: e.tensor_tensor(out=uu[:], in0=uu[:], in1=xs[:], op=ALU.mult), r=[r_uu, r_xs], w=[r_uu])
                    k.op("act", lambda e: e.activation(out=sg[:], in_=uu[:], func=AF.Sigmoid, scale=1.5957691216), r=[r_uu], w=[r_sg])
                    k.op("dve", lambda e: e.tensor_tensor(out=HID[:, hc, :], in0=xs[:], in1=sg[:], op=ALU.mult), r=[r_xs, r_sg], w=[r_HID])
                if which == 0:
                    b, rb = banks[2], r_bank[2]
                    for hc in range(2):
                        k.op("pe", lambda e: e.matmul(b[0:64, 0:256], W2[:, hc, :], HID[:, hc, :], start=(hc == 0), stop=(hc == 1)), r=[r_W2, r_HID], w=[rb])
                    k.op("act", lambda e: e.copy(out=KC[:], in_=b[0:64, 0:256]), r=[rb], w=[r_KC])
                else:
                    for c in range(2):
                        b, rb = banks[3 + c], r_bank[3 + c]
                        for hc in range(2):
                            k.op("pe", lambda e: e.matmul(b[:, 0:64], HID[:, hc, c * 128:(c + 1) * 128], W2[:, hc, :], start=(hc == 0), stop=(hc == 1)),
                                 r=[r_W2, r_HID], w=[rb])
                        k.op("act", lambda e: e.copy(out=VC1[:, c, 0:64], in_=b[:, 0:64]), r=[rb], w=[r_VC])
            k.barrier()

        CMT = [ksb("CMT%d" % i, [128, 256], F32) for i in range(2)]
        r_CMT = [Res(), Res()]
        CMK = [ksb("CMK%d" % i, [128, 512], BF16) for i in range(4)]
        r_CMK = [Res() for i in range(4)]
        sc = ksb("sc", [128, 4, 256], F32)
        r_sc = Res()
        den = ksb("den", [128, 4], F32)
        r_den = Res()
        PP = ksb("PP", [128, 264], F32)
        r_PP = Res()
        imp = ksb("imp", [128, 64], F32)
        imp3 = ksb("imp3", [128, 64], F32)
        m8 = ksb("m8", [128, 8], F32)
        thr = ksb("thr", [128, 1], F32)
        sel = ksb("sel", [128, 64], F32)
        r_imp, r_imp3, r_m8, r_thr, r_sel = [Res() for _ in range(5)]
        SELB = ksb("SELB", [64, 4, 128], BF16)
        r_SELB = Res()
        PT = [ksb("PT%d" % i, [128, 4, 128], BF16) for i in range(3)]
        r_PT = [Res() for i in range(3)]
        cf = ksb("cf", [128, 4], F32)
        tmpo = ksb("tmpo", [128, 4, 64], F32)
        r_cf, r_tmpo = Res(), Res()
        k.op("dve", lambda e: e.memset(PP[:], 0.0), w=[r_PP])
        pt_i = [0]
        st_i = [0]
        cmk_i = [0]

        def attend(qi, kT, kblocks, V1, v_of, Ob, rOb, masks):
            t0 = qi * 128
            nk = len(kblocks)
            for ii, kb in enumerate(kblocks):
                si = 3 + (st_i[0] % 2)
                st_i[0] += 1
                S_, rS_ = banks[si], r_bank[si]
                ml = masks(kb)
                k.op("pe", lambda e: e.matmul(S_[:], kT[:, kb * 128:(kb + 1) * 128], Q[:, :, t0:t0 + 128], start=True, stop=(len(ml) == 0)),
                     r=[r_c, r_Q, r_KC, r_KS, r_KW], w=[rS_])
                for mi, (ml_l, ml_r, ml_res) in enumerate(ml):
                    k.op("pe", lambda e: e.matmul(S_[:], ml_l, ml_r, start=False, stop=(mi == len(ml) - 1)), r=[r_c] + ml_res, w=[rS_])
                pi = pt_i[0] % 3
                pt_i[0] += 1
                k.op("act", lambda e: e.activation(out=PT[pi][:].rearrange("p g t -> p (g t)"), in_=S_[:], func=AF.Exp), r=[rS_], w=[r_PT[pi]])
                for g in range(4):
                    k.op("pe", lambda e: e.matmul(Ob[:, g * 65:(g + 1) * 65], PT[pi][:, g, :], v_of(kb), start=(ii == 0 and g == 0), stop=(ii == nk - 1), skip_group_check=True),
                         r=[r_PT[pi], r_c, r_VC], w=[rOb])

        def combine(qi, Ob, rOb, c, first):
            Ov = Ob[:, 0:260].rearrange("p (g e) -> p g e", g=4)
            k.op("dve", lambda e: e.tensor_scalar(out=cf[:].unsqueeze(2), in0=Ov[:, :, 64:65], scalar1=1e-30, scalar2=None, op0=ALU.max), r=[rOb], w=[r_cf])
            k.op("dve", lambda e: e.reciprocal(out=cf[:], in_=cf[:]), r=[r_cf], w=[r_cf])
            gsl = GA[:, qi, :].rearrange("p (g c) -> p g c", c=3)[:, :, c]
            k.op("dve", lambda e: e.tensor_tensor(out=cf[:], in0=cf[:], in1=gsl, op=ALU.mult), r=[r_cf, r_GA], w=[r_cf])
            ov = OUT[:, qi, :].rearrange("p (g d) -> p g d", g=4)
            if first:
                k.op("dve", lambda e: e.tensor_tensor(out=ov, in0=Ov[:, :, 0:64], in1=cf[:].unsqueeze(2).to_broadcast([128, 4, 64]), op=ALU.mult),
                     r=[rOb, r_cf], w=[r_OUT])
            else:
                k.op("dve", lambda e: e.tensor_tensor(out=tmpo[:], in0=Ov[:, :, 0:64], in1=cf[:].unsqueeze(2).to_broadcast([128, 4, 64]), op=ALU.mult),
                     r=[rOb, r_cf], w=[r_tmpo])
                k.op("dve", lambda e: e.tensor_tensor(out=ov, in0=ov, in1=tmpo[:], op=ALU.add), r=[r_tmpo, r_OUT], w=[r_OUT])

        for qi in range(32):
            t0 = qi * 128
            ci = qi % 2
            k.dma("sp", CMT[ci][:], cmaskT[qi], w=[r_CMT[ci]])
            for h2 in range(2):
                b, rb = banks[h2], r_bank[h2]
                for gg in range(2):
                    g = h2 * 2 + gg
                    k.op("pe", lambda e: e.matmul(b[:, gg * 256:(gg + 1) * 256], Q[:, g, t0:t0 + 128], KC[:], start=True, stop=True), r=[r_Q, r_KC], w=[rb])
                k.op("dve", lambda e: e.tensor_tensor(out=sc[:, h2 * 2:h2 * 2 + 2, :], in0=b[:].rearrange("p (g n) -> p g n", g=2),
                                                      in1=CMT[ci][:].unsqueeze(1).to_broadcast([128, 2, 256]), op=ALU.add), r=[rb, r_CMT[ci]], w=[r_sc])
            k.op("act", lambda e: e.activation(out=sc[:], in_=sc[:], func=AF.Exp), r=[r_sc], w=[r_sc])
            k.op("dve", lambda e: e.tensor_reduce(out=den[:], in_=sc[:], axis=AX.X, op=ALU.add), r=[r_sc], w=[r_den])
            k.op("dve", lambda e: e.tensor_scalar(out=den[:], in0=den[:], scalar1=1e-30, scalar2=None, op0=ALU.max), r=[r_den], w=[r_den])
            k.op("dve", lambda e: e.reciprocal(out=den[:], in_=den[:]), r=[r_den], w=[r_den])
            k.op("dve", lambda e: e.tensor_tensor(out=sc[:], in0=sc[:], in1=den[:].unsqueeze(2).to_broadcast([128, 4, 256]), op=ALU.mult), r=[r_sc, r_den], w=[r_sc])
            k.op("dve", lambda e: e.tensor_reduce(out=PP[:, 4:260], in_=sc[:].rearrange("p g n -> p n g"), axis=AX.X, op=ALU.add), r=[r_sc], w=[r_PP])
            PPv = PP[:].rearrange("p (j f) -> p j f", f=4)
            k.op("dve", lambda e: e.tensor_reduce(out=imp[:], in_=PPv[:, 1:65, :], axis=AX.X, op=ALU.add), r=[r_PP], w=[r_imp])
            k.op("dve", lambda e: e.tensor_tensor(out=imp[:], in0=imp[:], in1=PPv[:, 0:64, 3], op=ALU.add), r=[r_PP, r_imp], w=[r_imp])
            k.op("dve", lambda e: e.tensor_tensor(out=imp[:], in0=imp[:], in1=IMUL[:, qi, :], op=ALU.mult), r=[r_imp, r_c], w=[r_imp])
            k.op("dve", lambda e: e.tensor_tensor(out=imp[:], in0=imp[:], in1=IADD[:, qi, :], op=ALU.add), r=[r_imp, r_c], w=[r_imp])
            k.op("dve", lambda e: e.max(out=m8[:], in_=imp[:]), r=[r_imp], w=[r_m8])
            k.op("dve", lambda e: e.match_replace(out=imp3[:], in_to_replace=m8[:], in_values=imp[:], imm_value=-3.0e38), r=[r_imp, r_m8], w=[r_imp3])
            k.op("dve", lambda e: e.max(out=m8[:], in_=imp3[:]), r=[r_imp3], w=[r_m8])
            k.op("dve", lambda e: e.tensor_reduce(out=thr[:], in_=m8[:], axis=AX.X, op=ALU.min), r=[r_m8], w=[r_thr])
            k.op("dve", lambda e: e.tensor_scalar(out=sel[:], in0=imp[:], scalar1=thr[:], scalar2=None, op0=ALU.is_ge), r=[r_imp, r_thr], w=[r_sel])
            k.op("dve", lambda e: e.tensor_tensor(out=sel[:], in0=sel[:], in1=JOK[:, qi, :], op=ALU.mult), r=[r_sel, r_c], w=[r_sel])
            k.op("dve", lambda e: e.tensor_scalar(out=sel[:], in0=sel[:], scalar1=-1.0, scalar2=-NEG, op0=ALU.add, op1=ALU.mult), r=[r_sel], w=[r_sel])
            bt, rbt = banks[2], r_bank[2]
            k.op("pe", lambda e: e.transpose(bt[0:64, 0:128], sel[:], IDF[:]), r=[r_sel, r_c], w=[rbt])
            for g in range(4):
                k.op("act", lambda e: e.copy(out=SELB[:, g, :], in_=bt[0:64, 0:128]), r=[rbt], w=[r_SELB])
            cchunks = [0] + ([1] if qi >= 16 else [])
            cm_of = {}
            for c in cchunks:
                i = cmk_i[0] % 4
                cmk_i[0] += 1
                k.dma("pool", CMK[i][:], cmaskK[qi, c], w=[r_CMK[i]])
                cm_of[c] = i
            attend(qi, KC, cchunks, VC1, lambda c: VC1[:, c, :], banks[5], r_bank[5],
                   lambda c: [(IDB[:], CMK[cm_of[c]][:], [r_CMK[cm_of[c]]])])
            combine(qi, banks[5], r_bank[5], 0, True)
            attend(qi, KS, list(range(qi + 1)), VS1, lambda kb: VS1[:, kb, :], banks[6], r_bank[6],
                   lambda kb: [(EA[:, kb, :], SELB[:].rearrange("j g t -> j (g t)"), [r_SELB])] + ([(IDB[:], MD[:], [])] if kb == qi else []))
            combine(qi, banks[6], r_bank[6], 1, False)
            attend(qi, KW, list(range(max(0, qi - 4), qi + 1)), VW1, lambda kb: VW1[:, kb, :], banks[7], r_bank[7],
                   lambda kb: ([(IDB[:], MD[:], [])] if kb == qi else []) + ([(IDB[:], MO[:], [])] if kb == qi - 4 else []))
            combine(qi, banks[7], r_bank[7], 2, False)
        ost = [ksb("ost%d" % i, [128, 512], F32) for i in range(2)]
        r_ost = [Res(), Res()]
        oi = 0
        for ct in range(2):
            for t4 in range(8):
                bt_, rbt_ = banks[oi % 2], r_bank[oi % 2]
                for a_ in range(4):
                    tt_ = t4 * 4 + a_
                    k.op("pe", lambda e: e.transpose(bt_[:, a_ * 128:(a_ + 1) * 128], OUT[:, tt_, ct * 128:(ct + 1) * 128], IDF[:]), r=[r_OUT, r_c], w=[rbt_])
                k.op("act", lambda e: e.copy(out=ost[oi % 2][:], in_=bt_[:]), r=[rbt_], w=[r_ost[oi % 2]])
                k.dma("sp", m3[bq * 4 + t4 // 2, 2048 + q * 256 + ct * 128:2048 + q * 256 + (ct + 1) * 128, (t4 % 2) * 512:(t4 % 2 + 1) * 512],
                      ost[oi % 2][:], r=[r_ost[oi % 2]])
                oi += 1
        k.barrier()


_NSA_CONST = {}


def nsa_consts():
    if _NSA_CONST:
        return _NSA_CONST
    s = np.arange(128)[:, None]
    t = np.arange(128)[None, :]
    md = np.where(s <= t, 0.0, NEG).astype(np.float32)
    mo = np.where(s > t, 0.0, NEG).astype(np.float32)
    E = np.zeros((64, 32, 128), np.float32)
    for kb in range(32):
        for sl in range(128):
            E[2 * kb + sl // 64, kb, sl] = 1.0
    cK = np.full((32, 2, 128, 512), NEG, np.float32)
    cT = np.full((32, 128, 256), NEG, np.float32)
    for qi in range(32):
        tt = qi * 128 + np.arange(128)
        n = np.arange(256)
        ok = (16 * n[None, :] + 31 <= tt[:, None]) & (n[None, :] < 255)
        cT[qi] = np.where(ok, 0.0, NEG)
        for c in range(2):
            okc = ok[:, c * 128:(c + 1) * 128].T
            cK[qi, c] = np.tile(np.where(okc, 0.0, NEG), (1, 4))
    tpos = np.arange(T)
    cur = tpos // 64
    j = np.arange(64)[None, :]
    curc = cur[:, None]
    forced = (j == 0) | (j == curc) | (j == curc - 1)
    fut = j > curc
    mul = np.where(forced | fut, 0.0, 1.0).astype(np.float32)
    add = np.zeros((T, 64), np.float32)
    add = np.where(j == curc - 1, 1e30, add)
    add = np.where(j == curc, 2e30, add)
    add = np.where(j == 0, 3e30, add)
    add = np.where(fut, -1e30, add).astype(np.float32)
    okj = (~fut).astype(np.float32)

    def tm(a):
        return np.ascontiguousarray(a.reshape(32, 128, 64).transpose(1, 0, 2))
    _NSA_CONST.update({"ident": np.eye(128, dtype=np.float32), "mdiag": np.tile(md, (1, 4)), "mold": np.tile(mo, (1, 4)), "Eall": E,
                       "cmaskK": cK, "cmaskT": cT, "impmul": tm(mul), "impadd": tm(add), "jok": tm(okj)})
    return _NSA_CONST


class Env:
    pass


class Blk3:
    def __init__(self, lst):
        self.lst = lst

    def __getitem__(self, key):
        return self.lst[key[0]][key[1], key[2]]


NBATCH_PER_CORE = 1
_WNAMES = ["ffn1_w13", "ffn1_w2", "w_in", "w_conv_out", "w_rwkv_out", "w_nsa_out", "w_out", "ffn2_w13", "ffn2_w2",
           "nsa_ck_w1", "nsa_ck_w2", "nsa_cv_w1", "nsa_cv_w2"]
_WSHAPES = {"ffn1_w13": [2, D, 2 * DFF], "ffn1_w2": [2, DFF, D], "w_in": [2, D, DIN], "w_conv_out": [2, 1024, D],
            "w_rwkv_out": [2, 1024, D], "w_nsa_out": [2, 1024, D], "w_out": [2, D, D], "ffn2_w13": [2, D, 2 * DFF],
            "ffn2_w2": [2, DFF, D], "nsa_ck_w1": [2, 2048, 256], "nsa_ck_w2": [2, 256, 64], "nsa_cv_w1": [2, 2048, 256],
            "nsa_cv_w2": [2, 256, 64]}
_CSHAPES = {"gains": [128, 7, 16], "conv_cw": [2, 128, 8, 31], "conv_pb": [2, 128, 3, 8], "rw_mu": [2, 4, 128, 10],
            "rw_prm": [2, 4, 128, 2, 7], "rw_wb": [2, 4, 96, 256], "rw_ab": [2, 4, 96, 256], "rw_gb": [2, 4, 256, 256],
            "ident": [128, 128], "bones": [128, 128], "nsa_pe": [2, 128, 2, 16], "mdiag": [128, 512], "mold": [128, 512],
            "Eall": [64, 32, 128], "cmaskK": [32, 2, 128, 512], "cmaskT": [32, 128, 256], "impmul": [128, 32, 64],
            "impadd": [128, 32, 64], "jok": [128, 32, 64]}


def build_fused(NBATCH):
    nc = bass.Bass("TRN2", target_bir_lowering=False)
    NBLK = 4 * NBATCH

    def din(name, shape):
        return nc.dram_tensor(name, list(shape), F32, kind="ExternalInput").ap()
    E = Env()
    E.nc, E.NBLK = nc, NBLK
    xT = din("xT", [NBLK * D, NT])
    E.outT = nc.dram_tensor("outT", [NBLK * D, NT], F32, kind="ExternalOutput").ap()
    E.w = {nm: din(nm, _WSHAPES[nm]) for nm in _WNAMES}
    cst = {nm: din(nm, sh) for nm, sh in _CSHAPES.items()}
    E.gains_d = cst["gains"]
    E.conv_cw, E.conv_pb = cst["conv_cw"], cst["conv_pb"]
    E.rw_mu, E.rw_prm, E.rw_wb, E.rw_ab, E.rw_gb = cst["rw_mu"], cst["rw_prm"], cst["rw_wb"], cst["rw_ab"], cst["rw_gb"]
    E.ident, E.bones, E.nsa_pe = cst["ident"], cst["bones"], cst["nsa_pe"]
    E.mdiag, E.mold, E.Eall, E.cmaskK, E.cmaskT = cst["mdiag"], cst["mold"], cst["Eall"], cst["cmaskK"], cst["cmaskT"]
    E.impmul, E.impadd, E.jok = cst["impmul"], cst["impadd"], cst["jok"]
    E.uT = [nc.dram_tensor("uT_s%d" % i, [DIN, NT], F32).ap() for i in range(NBLK)]
    E.x1T = nc.dram_tensor("x1T_s", [NBLK * D, NT], F32).ap()
    E.mixT = nc.dram_tensor("mixT_s", [NBLK * 3072, NT], F32).ap()
    E.tokS = nc.dram_tensor("tokS_s", [2, 5, T, 128], F32).ap()
    with ExitStack() as st:
        k = KB(nc, st)
        E.k = k
        E.banks = [k.ps("bank%d" % i, [128, 512], F32) for i in range(8)]
        E.r_bank = [Res() for i in range(8)]
        E.bank_i = [0]
        E.uid = [0]
        emit_tl(E, False, True, False, 0, xT)
        for l in range(2):
            for blk in range(NBLK):
                emit_conv(E, l, blk)
            for b in range(NBATCH):
                for q in range(4):
                    emit_rwkv(E, l, b, q)
                    emit_nsa(E, l, b, q)
            if l == 0:
                emit_tl(E, True, True, False, 0, E.x1T)
            else:
                emit_tl(E, True, False, True, 1, E.x1T)
        k.barrier()
        print("fused instructions:", k.ninst)
    return nc


_PROG = {}


def _g16(v):
    return np.ascontiguousarray(np.asarray(v, np.float32).reshape(16, 128).T)


def kernel(**inp):
    inp = {k_: np.asarray(v_, np.float32) for k_, v_ in inp.items()}
    NB = NBATCH_PER_CORE
    ncores = 2 // NB
    if "nc" not in _PROG:
        _PROG["nc"] = build_fused(NB)
    nc = _PROG["nc"]
    x = inp["x"]
    xT = np.ascontiguousarray(x.reshape(8, 1024, 2048).transpose(0, 2, 1)).reshape(8 * 2048, 1024)
    m = {nm: inp[nm] for nm in _WNAMES}
    gains = np.zeros((128, 7, 16), np.float32)
    for l in range(2):
        gains[:, 3 * l] = _g16(inp["ffn1_norm"][l])
        gains[:, 3 * l + 1] = _g16(inp["mix_norm"][l])
        gains[:, 3 * l + 2] = _g16(inp["ffn2_norm"][l])
    gains[:, 6] = _g16(inp["final_norm"])
    m["gains"] = gains
    cw, pb, mu, prm, wb, ab, gb, pe = [], [], [], [], [], [], [], []
    for l in range(2):
        p = {k_: v_[l] for k_, v_ in inp.items() if k_ not in ("x", "final_norm")}
        c_, p_ = conv_params(p)
        cw.append(c_)
        pb.append(p_)
        rp = [rwkv_params(p, q) for q in range(4)]
        mu.append(np.stack([r_[0] for r_ in rp]))
        prm.append(np.stack([r_[1] for r_ in rp]))
        wb.append(np.stack([r_[2] for r_ in rp]))
        ab.append(np.stack([r_[3] for r_ in rp]))
        gb.append(np.stack([r_[4] for r_ in rp]))
        pe.append(np.stack([p["nsa_pe_k"].reshape(16, 128).T, p["nsa_pe_v"].reshape(16, 128).T], axis=1))
    m.update({"conv_cw": np.stack(cw), "conv_pb": np.stack(pb), "rw_mu": np.stack(mu), "rw_prm": np.stack(prm),
              "rw_wb": np.stack(wb), "rw_ab": np.stack(ab), "rw_gb": np.stack(gb), "nsa_pe": np.stack(pe)})
    bones = np.zeros((128, 128), np.float32)
    bones[:64, :64] = 1
    bones[64:, 64:] = 1
    m["bones"] = bones
    m.update(nsa_consts())
    m = {k_: np.ascontiguousarray(v_, dtype=np.float32) for k_, v_ in m.items()}
    maps = []
    per = 8 // ncores
    for c in range(ncores):
        mc = dict(m)
        mc["xT"] = np.ascontiguousarray(xT[c * per * 2048:(c + 1) * per * 2048])
        maps.append(mc)
    res = run_bass_kernel_spmd(nc, maps, core_ids=list(range(ncores))).results
    outT = np.concatenate([r_["outT"] for r_ in res], axis=0)
    out = np.ascontiguousarray(outT.reshape(8, 2048, 1024).transpose(0, 2, 1)).reshape(2, 4096, 2048)
    return out.astype(np.float32)
```

```python
import numpy as np
from contextlib import ExitStack
import concourse.bass as bass
import concourse.mybir as mybir
from concourse.bass_utils import run_bass_kernel_spmd

F32 = mybir.dt.float32
BF16 = mybir.dt.bfloat16
AF = mybir.ActivationFunctionType
ALU = mybir.AluOpType
AX = mybir.AxisListType


class Res:
    __slots__ = ("name", "lw", "rd")

    def __init__(self, name=""):
        self.name = name
        self.lw = None
        self.rd = {}


class KB:
    NDMA = 6

    def __init__(self, nc, st):
        self.nc = nc
        self.st = st
        self.eng = {"pe": nc.tensor, "act": nc.scalar, "dve": nc.vector,
                    "pool": nc.gpsimd, "sp": nc.sync}
        self.sems = {}
        self.cnt = {}
        for e in self.eng:
            self.sems[e] = st.enter_context(nc.semaphore("c_" + e))
            self.cnt[e] = 0
        self.dq = {}
        for q in ("sp", "pool", "act"):
            lst = []
            for i in range(self.NDMA):
                key = "d_%s%d" % (q, i)
                self.sems[key] = st.enter_context(nc.semaphore(key))
                self.cnt[key] = 0
                lst.append(key)
            self.dq[q] = [lst, 0]
        self.known = {e: {} for e in self.eng}
        self.ninst = 0

    def sb(self, name, shape, dt):
        return self.st.enter_context(self.nc.sbuf_tensor(name, shape, dt))

    def ps(self, name, shape, dt=F32):
        return self.st.enter_context(self.nc.psum_tensor(name, shape, dt))

    def _wait(self, e, deps, keep_last=False):
        best = {}
        for (k, v) in deps:
            if best.get(k, 0) < v:
                best[k] = v
        todo = []
        for k, v in best.items():
            if k == e and e == "pe":
                continue
            if self.known[e].get(k, 0) >= v:
                continue
            todo.append((k, v))
            self.known[e][k] = v
        last = None
        if keep_last and todo:
            last = todo.pop()
        for k, v in todo:
            self.eng[e].wait_ge(self.sems[k], v)
            self.ninst += 1
        return last

    def _deps(self, r, w):
        deps = []
        for x in r:
            if x.lw is not None:
                deps.append(x.lw)
        for x in w:
            if x.lw is not None:
                deps.append(x.lw)
            deps.extend(x.rd.items())
        return deps

    def op(self, e, fn, r=(), w=()):
        last = self._wait(e, self._deps(r, w), keep_last=True)
        ins = fn(self.eng[e])
        if last is not None:
            ins._wait_ge(self.sems[last[0]], last[1])
        self.cnt[e] += 1
        ins.then_inc(self.sems[e], 1)
        tok = (e, self.cnt[e])
        for x in r:
            x.rd[e] = self.cnt[e]
        for x in w:
            x.lw = tok
            x.rd = {}
        self.ninst += 1
        return ins

    def dma(self, q, out, in_, r=(), w=(), **kw):
        lst, i = self.dq[q]
        key = lst[i % len(lst)]
        self.dq[q][1] = i + 1
        deps = self._deps(r, w)
        if self.cnt[key] > 0:
            deps.append((key, self.cnt[key]))
        last = self._wait(q, deps, keep_last=True)
        ins = self.eng[q].dma_start(out=out, in_=in_, **kw)
        if last is not None:
            ins._wait_ge(self.sems[last[0]], last[1])
        self.cnt[key] += 16
        ins.then_inc(self.sems[key], 16)
        tok = (key, self.cnt[key])
        for x in r:
            x.rd[key] = self.cnt[key]
        for x in w:
            x.lw = tok
            x.rd = {}
        self.ninst += 1
        return ins

    def wait_all(self, e, res):
        deps = []
        for x in res:
            if x.lw is not None:
                deps.append(x.lw)
            deps.extend(x.rd.items())
        self._wait(e, deps)

    def barrier(self):
        allk = [(k, v) for k, v in self.cnt.items() if v > 0]
        for e in self.eng:
            self._wait(e, allk)


D = 2048
DFF = 5632
DIN = 14320
NT = 1024
KC = D // 128
EPS = 1e-6


def emit_tl(E, do_C, do_A, do_final, l, xT):
    nc, k, NBLK = E.nc, E.k, E.NBLK
    W = E.w
    uT, x1T, mixT, outT = E.uT, E.x1T, E.mixT, E.outT
    if do_C:
        mixT = E.mixT
        brgT = None
        w_br = [W["w_conv_out"][l], W["w_rwkv_out"][l], W["w_nsa_out"][l]]
        w_out = W["w_out"][l]
        ffn2_w13 = W["ffn2_w13"][l]
        ffn2_w2 = W["ffn2_w2"][l]
    if do_A:
        la = l + 1 if do_C else l
        ffn1_w13 = W["ffn1_w13"][la]
        ffn1_w2 = W["ffn1_w2"][la]
        w_in = W["w_in"][la]
    gsel = {}
    if do_C:
        gsel["ffn2_norm"] = 3 * l + 2
    if do_A:
        gsel["ffn1_norm"] = 3 * la
        gsel["mix_norm"] = 3 * la + 1
    if do_final:
        gsel["final_norm"] = 6
    with ExitStack() as st:
        uid = E.uid

        def ksb(name, shape, dt):
            uid[0] += 1
            return st.enter_context(nc.sbuf_tensor("%s_u%d" % (name, uid[0]), shape, dt))
        X = ksb("X", [128, KC, NT], F32)
        H = ksb("H", [128, KC, NT], BF16)
        rX = [Res("X%d" % c) for c in range(KC)]
        rH = [Res("H%d" % c) for c in range(KC)]
        ones = ksb("ones", [128, 128], BF16)
        r_ones = Res("ones")
        gains = ksb("gains", [128, 7, KC], F32)
        r_gains = Res("gains")
        sq = [ksb("sq%d" % i, [128, NT], BF16) for i in range(2)]
        r_sq = [Res("sq%d" % i) for i in range(2)]
        rstd = ksb("rstd", [128, NT], F32)
        r_rstd = Res("rstd")
        banks, r_bank, bank_i = E.banks, E.r_bank, E.bank_i
        blk = [0]

        def nb():
            i = bank_i[0] % 8
            bank_i[0] += 1
            return banks[i], r_bank[i]

        k.op("dve", lambda e: e.memset(ones[:], 1.0), w=[r_ones])
        xvs = [xT[b_ * D:(b_ + 1) * D, :].rearrange("(c p) t -> p c t", p=128) for b_ in range(NBLK)]
        k.dma("sp", gains[:], E.gains_d, w=[r_gains])
        gidx = gsel

        def rmsnorm(gname, to_x=False):
            g = gidx[gname]
            b0, rb0 = nb()
            b1, rb1 = nb()
            for c in range(KC):
                s, rs = sq[c % 2], r_sq[c % 2]
                k.op("act", lambda e: e.activation(out=s[:], in_=X[:, c, :], func=AF.Square), r=[rX[c]], w=[rs])
                for th, (b, rb) in enumerate(((b0, rb0), (b1, rb1))):
                    k.op("pe", lambda e: e.matmul(b[:], ones[:], s[:, th * 512:(th + 1) * 512],
                                                   start=(c == 0), stop=(c == KC - 1)), r=[rs, r_ones], w=[rb])
            for th, (b, rb) in enumerate(((b0, rb0), (b1, rb1))):
                sl = slice(th * 512, (th + 1) * 512)
                k.op("dve", lambda e: e.tensor_scalar(out=rstd[:, sl], in0=b[:], scalar1=1.0 / D, scalar2=EPS,
                                                      op0=ALU.mult, op1=ALU.add), r=[rb], w=[r_rstd])
            k.op("act", lambda e: e.activation(out=rstd[:], in_=rstd[:], func=AF.Sqrt), r=[r_rstd], w=[r_rstd])
            k.op("dve", lambda e: e.reciprocal(out=rstd[:], in_=rstd[:]), r=[r_rstd], w=[r_rstd])
            for c in range(KC):
                if to_x:
                    k.op("dve", lambda e: e.scalar_tensor_tensor(out=X[:, c, :], in0=X[:, c, :], scalar=gains[:, g, c:c + 1],
                                                                 in1=rstd[:], op0=ALU.mult, op1=ALU.mult),
                         r=[rX[c], r_rstd, r_gains], w=[rX[c]])
                else:
                    k.op("dve", lambda e: e.scalar_tensor_tensor(out=H[:, c, :], in0=X[:, c, :], scalar=gains[:, g, c:c + 1],
                                                                 in1=rstd[:], op0=ALU.mult, op1=ALU.mult),
                         r=[rX[c], r_rstd, r_gains], w=[rH[c]])

        def ffn(w13, w2):
            NG = 4
            GF = 11
            w13v = w13.rearrange("(kc p) n -> p kc n", p=128)
            w2v = w2.rearrange("(f p) n -> p f n", p=128)
            with ExitStack() as st2:
                def sb2(name, shape, dt):
                    uid[0] += 1
                    return st2.enter_context(nc.sbuf_tensor("%s_u%d" % (name, uid[0]), shape, dt))
                G = sb2("G", [128, GF, NT], BF16)
                rG = [Res("G%d" % i) for i in range(GF)]
                W1 = [sb2("W1_%d" % i, [128, KC, 128], BF16) for i in range(2)]
                W3 = [sb2("W3_%d" % i, [128, KC, 128], BF16) for i in range(2)]
                rW1 = [Res() for i in range(2)]
                rW3 = [Res() for i in range(2)]
                W2 = sb2("W2", [128, GF, D], BF16)
                rW2 = [Res() for i in range(GF)]
                sa = [sb2("sa%d" % i, [128, 512], F32) for i in range(2)]
                r_sa = [Res() for i in range(2)]

                def load13(f):
                    i = f % 2
                    k.dma("pool", W1[i][:], w13v[:, :, f * 128:(f + 1) * 128], w=[rW1[i]])
                    k.dma("pool", W3[i][:], w13v[:, :, DFF + f * 128:DFF + (f + 1) * 128], w=[rW3[i]])

                load13(0)
                it = 0
                for gidx_ in range(NG):
                    for fl in range(GF):
                        f = gidx_ * GF + fl
                        if f + 1 < NG * GF:
                            load13(f + 1)
                        if fl == 0:
                            for j in range(GF):
                                k.dma("pool", W2[:, j, :], w2v[:, gidx_ * GF + j, :], w=[rW2[j]])
                        i = f % 2
                        for th in range(2):
                            sl = slice(th * 512, (th + 1) * 512)
                            ba, rba = nb()
                            bb, rbb = nb()
                            for kc in range(KC):
                                k.op("pe", lambda e: e.matmul(ba[:], W1[i][:, kc, :], H[:, kc, sl], start=(kc == 0), stop=(kc == KC - 1)),
                                     r=[rW1[i], rH[kc]], w=[rba])
                            for kc in range(KC):
                                k.op("pe", lambda e: e.matmul(bb[:], W3[i][:, kc, :], H[:, kc, sl], start=(kc == 0), stop=(kc == KC - 1)),
                                     r=[rW3[i], rH[kc]], w=[rbb])
                            s_, rs_ = sa[it % 2], r_sa[it % 2]
                            it += 1
                            k.op("act", lambda e: e.activation(out=s_[:], in_=ba[:], func=AF.Silu), r=[rba], w=[rs_])
                            k.op("dve", lambda e: e.tensor_tensor(out=G[:, fl, sl], in0=s_[:], in1=bb[:], op=ALU.mult),
                                 r=[rs_, rbb], w=[rG[fl]])
                    for m in range(KC):
                        for th in range(2):
                            sl = slice(th * 512, (th + 1) * 512)
                            b, rb = nb()
                            for fl in range(GF):
                                k.op("pe", lambda e: e.matmul(b[:], W2[:, fl, m * 128:(m + 1) * 128], G[:, fl, sl],
                                                               start=(fl == 0), stop=(fl == GF - 1)),
                                     r=[rW2[fl], rG[fl]], w=[rb])
                            k.op("dve", lambda e: e.scalar_tensor_tensor(out=X[:, m, sl], in0=b[:], scalar=0.5, in1=X[:, m, sl],
                                                                         op0=ALU.mult, op1=ALU.add),
                                 r=[rb, rX[m]], w=[rX[m]])
                k.barrier()

        def c_phase():
            mixv = mixT[blk[0] * 3072:(blk[0] + 1) * 3072, :].rearrange("(c p) t -> p c t", p=128)
            brgv = uT[blk[0]][8176:DIN, :].rearrange("(b m p) t -> p b m t", p=128, b=3)
            with ExitStack() as st2:
                def sb2(name, shape, dt):
                    uid[0] += 1
                    return st2.enter_context(nc.sbuf_tensor("%s_u%d" % (name, uid[0]), shape, dt))
                MIX = sb2("MIX", [128, 24, NT], BF16)
                rMIX = [Res() for i in range(24)]
                for c0 in range(0, 24, 4):
                    k.dma("pool", MIX[:, c0:c0 + 4, :], mixv[:, c0:c0 + 4, :], w=rMIX[c0:c0 + 4])
                BRG = [sb2("BRG%d" % i, [128, 3, 512], F32) for i in range(2)]
                rBRG = [Res() for i in range(2)]
                WB = [[sb2("WB%d_%d" % (b, i), [128, 8, 128], BF16) for i in range(2)] for b in range(3)]
                rWB = [[Res() for i in range(2)] for b in range(3)]
                sg = [sb2("sg%d" % i, [128, 512], F32) for i in range(3)]
                r_sg = [Res() for i in range(3)]
                tt = [sb2("tt%d" % i, [128, 512], F32) for i in range(3)]
                r_tt = [Res() for i in range(3)]
                wbv = [w.rearrange("(kc p) n -> p kc n", p=128) for w in w_br]

                def loadm(m):
                    i = m % 2
                    for b in range(3):
                        k.dma("pool", WB[b][i][:], wbv[b][:, :, m * 128:(m + 1) * 128], w=[rWB[b][i]])

                def loadbrg(it_):
                    m_, th_ = it_ // 2, it_ % 2
                    k.dma("sp", BRG[it_ % 2][:], brgv[:, :, m_, th_ * 512:(th_ + 1) * 512], w=[rBRG[it_ % 2]])

                loadm(0)
                loadbrg(0)
                for m in range(KC):
                    if m + 1 < KC:
                        loadm(m + 1)
                    i = m % 2
                    for th in range(2):
                        sl = slice(th * 512, (th + 1) * 512)
                        bi = (m * 2 + th) % 2
                        if m * 2 + th + 1 < 2 * KC:
                            loadbrg(m * 2 + th + 1)
                        pb = []
                        for b in range(3):
                            bk, rbk = nb()
                            pb.append((bk, rbk))
                            for kc in range(8):
                                k.op("pe", lambda e: e.matmul(bk[:], WB[b][i][:, kc, :], MIX[:, b * 8 + kc, sl],
                                                               start=(kc == 0), stop=(kc == 7)),
                                     r=[rWB[b][i], rMIX[b * 8 + kc]], w=[rbk])
                        for b in range(3):
                            k.op("act", lambda e: e.activation(out=sg[b][:], in_=BRG[bi][:, b, :], func=AF.Sigmoid),
                                 r=[rBRG[bi]], w=[r_sg[b]])
                            k.op("dve", lambda e: e.tensor_tensor(out=tt[b][:], in0=sg[b][:], in1=pb[b][0][:], op=ALU.mult),
                                 r=[r_sg[b], pb[b][1]], w=[r_tt[b]])
                        k.op("dve", lambda e: e.tensor_tensor(out=tt[0][:], in0=tt[0][:], in1=tt[1][:], op=ALU.add),
                             r=[r_tt[0], r_tt[1]], w=[r_tt[0]])
                        k.op("dve", lambda e: e.tensor_tensor(out=H[:, m, sl], in0=tt[0][:], in1=tt[2][:], op=ALU.add),
                             r=[r_tt[0], r_tt[2]], w=[rH[m]])
                k.barrier()
            wov = w_out.rearrange("(kc p) n -> p kc n", p=128)
            with ExitStack() as st2:
                WO = [st2.enter_context(nc.sbuf_tensor("WO%d_b%d_%d" % (i, blk[0], l * 10 + do_A), [128, KC, 128], BF16)) for i in range(2)]
                rWO = [Res() for i in range(2)]
                k.dma("pool", WO[0][:], wov[:, :, 0:128], w=[rWO[0]])
                for m in range(KC):
                    if m + 1 < KC:
                        k.dma("pool", WO[(m + 1) % 2][:], wov[:, :, (m + 1) * 128:(m + 2) * 128], w=[rWO[(m + 1) % 2]])
                    i = m % 2
                    for th in range(2):
                        sl = slice(th * 512, (th + 1) * 512)
                        b, rb = nb()
                        for kc in range(KC):
                            k.op("pe", lambda e: e.matmul(b[:], WO[i][:, kc, :], H[:, kc, sl], start=(kc == 0), stop=(kc == KC - 1)),
                                 r=[rWO[i], rH[kc]], w=[rb])
                        k.op("dve", lambda e: e.tensor_tensor(out=X[:, m, sl], in0=b[:], in1=X[:, m, sl], op=ALU.add),
                             r=[rb, rX[m]], w=[rX[m]])
                k.barrier()

        def win_phase():
            wiv = w_in.rearrange("(kc p) n -> p kc n", p=128)
            nch = (DIN + 127) // 128
            with ExitStack() as st2:
                WI = [st2.enter_context(nc.sbuf_tensor("WI%d_b%d_%d" % (i, blk[0], l * 10 + do_C), [128, KC, 128], BF16)) for i in range(2)]
                rWI = [Res() for i in range(2)]
                stg = [st2.enter_context(nc.sbuf_tensor("stg%d_b%d_%d" % (i, blk[0], l * 10 + do_C), [128, 512], F32)) for i in range(4)]
                r_stg = [Res() for i in range(4)]

                def loadj(j):
                    cw = min(128, DIN - j * 128)
                    k.dma("pool", WI[j % 2][:, :, 0:cw], wiv[:, :, j * 128:j * 128 + cw], w=[rWI[j % 2]])

                loadj(0)
                it = 0
                for j in range(nch):
                    if j + 1 < nch:
                        loadj(j + 1)
                    cw = min(128, DIN - j * 128)
                    i = j % 2
                    for th in range(2):
                        sl = slice(th * 512, (th + 1) * 512)
                        b, rb = nb()
                        for kc in range(KC):
                            k.op("pe", lambda e: e.matmul(b[0:cw, :], WI[i][:, kc, 0:cw], H[:, kc, sl], start=(kc == 0), stop=(kc == KC - 1)),
                                 r=[rWI[i], rH[kc]], w=[rb])
                        s_, rs_ = stg[it % 4], r_stg[it % 4]
                        if it % 2 == 0:
                            k.op("act", lambda e: e.copy(out=s_[0:cw, :], in_=b[0:cw, :]), r=[rb], w=[rs_])
                        else:
                            k.op("dve", lambda e: e.tensor_copy(out=s_[0:cw, :], in_=b[0:cw, :]), r=[rb], w=[rs_])
                        it += 1
                        k.dma("sp", uT[blk[0]][j * 128:j * 128 + cw, sl], s_[0:cw, :], r=[rs_])
                k.barrier()

        for b_ in range(NBLK):
            blk[0] = b_
            for c0 in range(0, KC, 4):
                k.dma("sp", X[:, c0:c0 + 4, :], xvs[b_][:, c0:c0 + 4, :], w=rX[c0:c0 + 4])
            if do_C:
                c_phase()
                rmsnorm("ffn2_norm")
                ffn(ffn2_w13, ffn2_w2)
            if do_A:
                rmsnorm("ffn1_norm")
                ffn(ffn1_w13, ffn1_w2)
                x1v = x1T[blk[0] * D:(blk[0] + 1) * D, :].rearrange("(c p) t -> p c t", p=128)
                for c0 in range(0, KC, 4):
                    k.dma("sp", x1v[:, c0:c0 + 4, :], X[:, c0:c0 + 4, :], r=rX[c0:c0 + 4])
                rmsnorm("mix_norm")
                win_phase()
            if do_final:
                rmsnorm("final_norm", to_x=True)
                ov = outT[blk[0] * D:(blk[0] + 1) * D, :].rearrange("(c p) t -> p c t", p=128)
                for c0 in range(0, KC, 4):
                    k.dma("sp", ov[:, c0:c0 + 4, :], X[:, c0:c0 + 4, :], r=rX[c0:c0 + 4])

            k.barrier()
        k.barrier()


T = 4096
LN_EPS = 1e-5
GN_EPS = 64e-5


def emit_conv(E, l, blk):
    nc, k = E.nc, E.k
    NTk = 1024
    PADT = NTk + 30
    q = blk % 4
    u3 = Blk3(E.uT)
    m3 = E.mixT.rearrange("(b r) t -> b r t", r=3072)
    cw, pb = E.conv_cw[l], E.conv_pb[l]
    with ExitStack() as st:
        uid = E.uid

        def ksb(name, shape, dt):
            uid[0] += 1
            return st.enter_context(nc.sbuf_tensor("%s_u%d" % (name, uid[0]), shape, dt))
        CW = ksb("CW", [128, 8, 31], F32)
        PB = ksb("PB", [128, 3, 8], F32)
        r_par = Res()
        k.dma("sp", CW[:], cw, w=[r_par])
        k.dma("sp", PB[:], pb, w=[r_par])
        ones = ksb("ones", [128, 128], BF16)
        r_ones = Res()
        k.op("dve", lambda e: e.memset(ones[:], 1.0), w=[r_ones])
        CO = ksb("CO", [128, 8, NTk], F32)
        rCO = [Res() for c in range(8)]
        A = [ksb("A%d" % i, [128, PADT], F32) for i in range(2)]
        Gt = [ksb("G%d" % i, [128, PADT], F32) for i in range(2)]
        rA = [Res() for i in range(2)]
        rG = [Res() for i in range(2)]
        banks, r_bank = E.banks, E.r_bank
        for c in range(8):
            i = c % 2
            for (dst, rdst, r0) in ((A[i], rA[i], c * 128), (Gt[i], rG[i], 1024 + c * 128)):
                if q == 0:
                    k.op("pool", lambda e: e.memset(dst[:, 0:30], 0.0), w=[rdst])
                else:
                    k.dma("sp", dst[:, 0:30], u3[blk - 1, r0:r0 + 128, NTk - 30:NTk], w=[rdst])
                k.dma("sp", dst[:, 30:PADT], u3[blk, r0:r0 + 128, :], w=[rdst])
            k.op("act", lambda e: e.activation(out=Gt[i][:], in_=Gt[i][:], func=AF.Sigmoid), r=[rG[i]], w=[rG[i]])
            eng = "dve"
            k.op(eng, lambda e: e.tensor_tensor(out=A[i][:], in0=A[i][:], in1=Gt[i][:], op=ALU.mult), r=[rA[i], rG[i]], w=[rA[i]])
            k.op(eng, lambda e: e.tensor_scalar(out=CO[:, c, :], in0=A[i][:, 0:NTk], scalar1=CW[:, c, 0:1], scalar2=PB[:, 0, c:c + 1],
                                                op0=ALU.mult, op1=ALU.add), r=[rA[i], r_par], w=[rCO[c]])
            for j in range(1, 31):
                k.op(eng, lambda e: e.scalar_tensor_tensor(out=CO[:, c, :], in0=A[i][:, j:j + NTk], scalar=CW[:, c, j:j + 1],
                                                           in1=CO[:, c, :], op0=ALU.mult, op1=ALU.add),
                     r=[rA[i], r_par, rCO[c]], w=[rCO[c]])
        xb = [ksb("xb%d" % i, [128, 512], BF16) for i in range(2)]
        x2 = [ksb("x2%d" % i, [128, 512], BF16) for i in range(2)]
        r_xb = [Res() for i in range(2)]
        r_x2 = [Res() for i in range(2)]
        mean = ksb("mean", [128, 512], F32)
        rstd = ksb("rstd", [128, 512], F32)
        msq = ksb("msq", [128, 512], F32)
        r_mean, r_rstd, r_msq = Res(), Res(), Res()
        t1 = [ksb("t1%d" % i, [128, 512], F32) for i in range(2)]
        r_t1 = [Res() for i in range(2)]
        og = [ksb("og%d" % i, [128, 512], F32) for i in range(2)]
        r_og = [Res() for i in range(2)]
        for th in range(2):
            sl = slice(th * 512, (th + 1) * 512)
            b1, rb1 = banks[2 * th], r_bank[2 * th]
            b2, rb2 = banks[2 * th + 1], r_bank[2 * th + 1]
            for c in range(8):
                i = c % 2
                k.op("act", lambda e: e.copy(out=xb[i][:], in_=CO[:, c, sl]), r=[rCO[c]], w=[r_xb[i]])
                k.op("act", lambda e: e.activation(out=x2[i][:], in_=CO[:, c, sl], func=AF.Square), r=[rCO[c]], w=[r_x2[i]])
                k.op("pe", lambda e: e.matmul(b1[:], ones[:], xb[i][:], start=(c == 0), stop=(c == 7)), r=[r_xb[i], r_ones], w=[rb1])
                k.op("pe", lambda e: e.matmul(b2[:], ones[:], x2[i][:], start=(c == 0), stop=(c == 7)), r=[r_x2[i], r_ones], w=[rb2])
            k.op("dve", lambda e: e.tensor_scalar(out=mean[:], in0=b1[:], scalar1=1.0 / 1024, scalar2=None, op0=ALU.mult), r=[rb1], w=[r_mean])
            k.op("dve", lambda e: e.tensor_tensor(out=msq[:], in0=mean[:], in1=mean[:], op=ALU.mult), r=[r_mean], w=[r_msq])
            k.op("dve", lambda e: e.scalar_tensor_tensor(out=rstd[:], in0=b2[:], scalar=1.0 / 1024, in1=msq[:], op0=ALU.mult, op1=ALU.subtract),
                 r=[rb2, r_msq], w=[r_rstd])
            k.op("dve", lambda e: e.tensor_scalar(out=rstd[:], in0=rstd[:], scalar1=LN_EPS, scalar2=None, op0=ALU.add), r=[r_rstd], w=[r_rstd])
            k.op("act", lambda e: e.activation(out=rstd[:], in_=rstd[:], func=AF.Sqrt), r=[r_rstd], w=[r_rstd])
            k.op("dve", lambda e: e.reciprocal(out=rstd[:], in_=rstd[:]), r=[r_rstd], w=[r_rstd])
            for c in range(8):
                i = c % 2
                k.op("dve", lambda e: e.tensor_tensor(out=t1[i][:], in0=CO[:, c, sl], in1=mean[:], op=ALU.subtract), r=[rCO[c], r_mean], w=[r_t1[i]])
                k.op("dve", lambda e: e.tensor_tensor(out=t1[i][:], in0=t1[i][:], in1=rstd[:], op=ALU.mult), r=[r_t1[i], r_rstd], w=[r_t1[i]])
                k.op("act", lambda e: e.activation(out=og[i][:], in_=t1[i][:], func=AF.Silu, scale=PB[:, 1, c:c + 1], bias=PB[:, 2, c:c + 1]),
                     r=[r_t1[i], r_par], w=[r_og[i]])
                k.dma("sp", m3[blk, c * 128:(c + 1) * 128, sl], og[i][:], r=[r_og[i]])
        k.barrier()


def conv_params(p):
    def pc(v): return np.ascontiguousarray(v.reshape(8, 128).T)
    cw = np.ascontiguousarray(p['conv_w'].T.reshape(8, 128, 31).transpose(1, 0, 2))
    pb = np.ascontiguousarray(np.stack([pc(p['conv_b']), pc(p['conv_ln_g']), pc(p['conv_ln_b'])], axis=1))
    return cw, pb


def emit_rwkv(E, l, bq, q):
    nc, k = E.nc, E.k
    u3 = Blk3(E.uT)
    m3 = E.mixT.rearrange("(b r) t -> b r t", r=3072)
    mu, prm = E.rw_mu[l, q], E.rw_prm[l, q]
    w_b, a_b, g_b = E.rw_wb[l, q], E.rw_ab[l, q], E.rw_gb[l, q]
    ident, bones = E.ident, E.bones
    tokS = E.tokS
    r_tok = Res()
    c0_ = q * 256
    rowbase = [2048 + c0_, 2048 + c0_ + 128, 3072 + c0_, 3072 + c0_ + 128, 4096 + c0_, 4096 + c0_ + 128,
               2048 + 3072 + 192, 2048 + 3072 + 192 + 128, 2048 + 3072, 2048 + 3072 + 96]
    with ExitStack() as st:
        uid = E.uid

        def ksb(name, shape, dt):
            uid[0] += 1
            return st.enter_context(nc.sbuf_tensor("%s_u%d" % (name, uid[0]), shape, dt))
        banks, r_bank, bank_i = E.banks, E.r_bank, E.bank_i

        def nb():
            i = bank_i[0] % 8
            bank_i[0] += 1
            return banks[i], r_bank[i]
        MU = ksb("MU", [128, 10], F32)
        PRM = ksb("PRM", [128, 2, 7], F32)
        IDN = ksb("IDN", [128, 128], F32)
        BON = ksb("BON", [128, 128], BF16)
        WB = ksb("WB", [96, 256], BF16)
        AB = ksb("AB", [96, 256], BF16)
        GB = ksb("GB", [128, 2, 256], BF16)
        r_c = Res()
        k.dma("sp", MU[:], mu, w=[r_c])
        k.dma("sp", PRM[:], prm, w=[r_c])
        k.dma("sp", IDN[:], ident, w=[r_c])
        k.dma("pool", BON[:], bones, w=[r_c])
        k.dma("pool", WB[:], w_b, w=[r_c])
        k.dma("pool", AB[:], a_b, w=[r_c])
        k.dma("pool", GB[:], g_b.rearrange("(kc p) n -> p kc n", p=128), w=[r_c])
        V = ksb("V", [128, 2, T], F32)
        rV = [Res(), Res()]
        RKB = ksb("RKB", [128, 2, T], BF16)
        rRKB = [Res(), Res()]
        SGL = ksb("SGL", [128, 2, T], BF16)
        rSGL = [Res(), Res()]

        with ExitStack() as st2:
            def sb2(name, shape, dt):
                uid[0] += 1
                return st2.enter_context(nc.sbuf_tensor("%s_u%d" % (name, uid[0]), shape, dt))
            ld = [sb2("ldc%d" % i, [128, T], F32) for i in range(1)]
            lp = [sb2("ldp%d" % i, [128, T], F32) for i in range(1)]
            r_ld = [Res(), Res()]
            r_lp = [Res(), Res()]
            ldi = [0]

            def shifted(rowtile, nrows, out, r_out_, post=None):
                i = 0
                r0 = rowbase[rowtile]
                k.op("pool", lambda e: e.memset(lp[i][0:nrows, 0:1], 0.0), w=[r_lp[i]])
                for tb in range(4):
                    k.dma("sp", ld[i][0:nrows, tb * 1024:(tb + 1) * 1024], u3[bq * 4 + tb, r0:r0 + nrows, :], w=[r_ld[i]])
                    n_ = 1024 if tb < 3 else 1023
                    k.dma("act", lp[i][0:nrows, tb * 1024 + 1:tb * 1024 + 1 + n_], u3[bq * 4 + tb, r0:r0 + nrows, 0:n_], w=[r_lp[i]])
                k.op("pool", lambda e: e.tensor_tensor(out=lp[i][0:nrows, :], in0=lp[i][0:nrows, :], in1=ld[i][0:nrows, :], op=ALU.subtract),
                     r=[r_ld[i], r_lp[i]], w=[r_lp[i]])
                if post is None:
                    k.op("dve", lambda e: e.scalar_tensor_tensor(out=out, in0=lp[i][0:nrows, :], scalar=MU[0:nrows, rowtile:rowtile + 1],
                                                                 in1=ld[i][0:nrows, :], op0=ALU.mult, op1=ALU.add),
                         r=[r_ld[i], r_lp[i], r_c], w=[r_out_])
                else:
                    k.op("dve", lambda e: e.scalar_tensor_tensor(out=ld[i][0:nrows, :], in0=lp[i][0:nrows, :], scalar=MU[0:nrows, rowtile:rowtile + 1],
                                                                 in1=ld[i][0:nrows, :], op0=ALU.mult, op1=ALU.add),
                         r=[r_ld[i], r_lp[i], r_c], w=[r_ld[i]])
                    k.op("act", lambda e: e.activation(out=out, in_=ld[i][0:nrows, :], func=post), r=[r_ld[i]], w=[r_out_])

            WL = sb2("WL", [96, T], BF16)
            AL = sb2("AL", [96, T], BF16)
            r_WL, r_AL = Res(), Res()
            shifted(8, 96, WL[:], r_WL, post=AF.Tanh)
            shifted(9, 96, AL[:], r_AL, post=AF.Copy)
            for ct in range(2):
                shifted(6 + ct, 128, SGL[:, ct, :], rSGL[ct], post=AF.Sigmoid)
                shifted(4 + ct, 128, V[:, ct, :], rV[ct])
            Rt = sb2("Rt", [128, T], F32)
            Kt = sb2("Kt", [128, T], F32)
            Wt = sb2("Wt", [128, T], F32)
            At = sb2("At", [128, T], F32)
            KKt = sb2("KKt", [128, T], F32)
            SQb = sb2("SQb", [128, 512], BF16)
            r_R, r_K, r_W, r_A, r_KK, r_SQ = Res(), Res(), Res(), Res(), Res(), Res()
            stg = [sb2("stg%d" % i, [128, 4, 128], F32) for i in range(2)]
            r_stg = [Res(), Res()]
            stg_i = [0]

            def to_tok(src, r_src, vec, ct):
                for t0 in range(0, 32, 4):
                    b, rb = nb()
                    for a in range(4):
                        tt = t0 + a
                        k.op("pe", lambda e: e.transpose(b[:, a * 128:(a + 1) * 128], src[:, tt * 128:(tt + 1) * 128], IDN[:]),
                             r=[r_src, r_c], w=[rb])
                    i = stg_i[0] % 2
                    stg_i[0] += 1
                    k.op("act", lambda e: e.copy(out=stg[i][:].rearrange("p a c -> p (a c)"), in_=b[:]), r=[rb], w=[r_stg[i]])
                    for hp in range(2):
                        dst = tokS[hp, vec, t0 * 128:(t0 + 4) * 128, ct * 64:(ct + 1) * 64].rearrange("(a p) c -> p a c", p=128)
                        k.dma("sp", dst, stg[i][:, :, hp * 64:(hp + 1) * 64], r=[r_stg[i]], w=[r_tok])

            for ct in range(2):
                shifted(0 + ct, 128, Rt[:], r_R)
                shifted(2 + ct, 128, Kt[:], r_K)
                for tb in range(8):
                    sl = slice(tb * 512, (tb + 1) * 512)
                    b, rb = nb()
                    k.op("pe", lambda e: e.matmul(b[:], WB[:, ct * 128:(ct + 1) * 128], WL[:, sl], start=True, stop=True), r=[r_WL, r_c], w=[rb])
                    k.op("act", lambda e: e.activation(out=Wt[:, sl], in_=b[:], func=AF.Sigmoid, bias=PRM[:, ct, 0:1]), r=[rb, r_c], w=[r_W])
                    b2, rb2 = nb()
                    k.op("pe", lambda e: e.matmul(b2[:], AB[:, ct * 128:(ct + 1) * 128], AL[:, sl], start=True, stop=True), r=[r_AL, r_c], w=[rb2])
                    k.op("act", lambda e: e.activation(out=At[:, sl], in_=b2[:], func=AF.Sigmoid, bias=PRM[:, ct, 1:2]), r=[rb2, r_c], w=[r_A])
                k.op("act", lambda e: e.activation(out=Wt[:], in_=Wt[:], func=AF.Exp, scale=-0.6065306597), r=[r_W], w=[r_W])
                to_tok(Wt, r_W, 1, ct)
                to_tok(Rt, r_R, 4, ct)
                k.op("dve", lambda e: e.tensor_scalar(out=KKt[:], in0=Kt[:], scalar1=PRM[:, ct, 2:3], scalar2=None, op0=ALU.mult), r=[r_K, r_c], w=[r_KK])
                k.op("dve", lambda e: e.tensor_scalar(out=Wt[:], in0=At[:], scalar1=-1.0, scalar2=PRM[:, ct, 3:4], op0=ALU.add, op1=ALU.mult),
                     r=[r_A, r_c], w=[r_W])
                k.op("dve", lambda e: e.scalar_tensor_tensor(out=Kt[:], in0=Wt[:], scalar=1.0, in1=Kt[:], op0=ALU.add, op1=ALU.mult),
                     r=[r_W, r_K], w=[r_K])
                to_tok(Kt, r_K, 3, ct)
                k.op("dve", lambda e: e.scalar_tensor_tensor(out=RKB[:, ct, :], in0=Rt[:], scalar=PRM[:, ct, 4:5], in1=Kt[:], op0=ALU.mult, op1=ALU.mult),
                     r=[r_R, r_K, r_c], w=[rRKB[ct]])
                for tb in range(8):
                    sl = slice(tb * 512, (tb + 1) * 512)
                    k.op("act", lambda e: e.activation(out=SQb[:], in_=KKt[:, sl], func=AF.Square), r=[r_KK], w=[r_SQ])
                    b, rb = nb()
                    k.op("pe", lambda e: e.matmul(b[:], BON[:], SQb[:], start=True, stop=True), r=[r_SQ, r_c], w=[rb])
                    k.op("dve", lambda e: e.tensor_scalar(out=Rt[:, sl], in0=b[:], scalar1=1e-12, scalar2=None, op0=ALU.add), r=[rb, r_R], w=[r_R])
                k.op("act", lambda e: e.activation(out=Rt[:], in_=Rt[:], func=AF.Sqrt), r=[r_R], w=[r_R])
                k.op("dve", lambda e: e.reciprocal(out=Rt[:], in_=Rt[:]), r=[r_R], w=[r_R])
                k.op("dve", lambda e: e.scalar_tensor_tensor(out=KKt[:], in0=KKt[:], scalar=-1.0, in1=Rt[:], op0=ALU.mult, op1=ALU.mult),
                     r=[r_KK, r_R], w=[r_KK])
                to_tok(KKt, r_KK, 0, ct)
                k.op("dve", lambda e: e.scalar_tensor_tensor(out=At[:], in0=KKt[:], scalar=-1.0, in1=At[:], op0=ALU.mult, op1=ALU.mult),
                     r=[r_KK, r_A], w=[r_A])
                to_tok(At, r_A, 2, ct)
            k.barrier()

        with ExitStack() as st2:
            def sb2(name, shape, dt):
                uid[0] += 1
                return st2.enter_context(nc.sbuf_tensor("%s_u%d" % (name, uid[0]), shape, dt))
            TB = 8
            RY = sb2("RY", [128, T, 2, 2], F32)
            r_Yall = Res()
            BCAR = [sb2("BCAR%d" % i, [128, TB, 2, 128], F32) for i in range(2)]
            rBCAR = [Res(), Res()]
            BC = [None] + [[sb2("BC%d_%d" % (v, i), [128, TB, 128], F32) for i in range(2)] for v in (1, 2, 3)]
            rBC = [None] + [[Res() for i in range(2)] for v in (1, 2, 3)]
            S = sb2("S", [128, 2, 64], F32)
            tmp = sb2("tmp", [128, 2, 64], F32)
            tmp2 = sb2("tmp2", [128, 2, 2, 64], F32)
            r_S, r_tmp, r_tmp2 = Res(), Res(), Res()
            k.op("dve", lambda e: e.memset(S[:], 0.0), w=[r_S])
            k.op("pool", lambda e: e.memset(BCAR[0][:], 0.0), w=[rBCAR[0]])
            k.op("pool", lambda e: e.memset(BCAR[1][:], 0.0), w=[rBCAR[1]])
            nblk = T // TB

            def loadblk(bi):
                i = bi % 2
                t0_ = bi * TB
                na = min(TB, T - 1 - t0_)
                for hp in range(2):
                    ps_ = slice(hp * 64, (hp + 1) * 64)
                    if na > 0:
                        k.dma("sp", BCAR[i][ps_, 0:na, 0, :], tokS[hp, 0, t0_ + 1:t0_ + 1 + na, :].partition_broadcast(64), r=[r_tok], w=[rBCAR[i]])
                    k.dma("act", BCAR[i][ps_, :, 1, :], tokS[hp, 4, t0_:t0_ + TB, :].partition_broadcast(64), r=[r_tok], w=[rBCAR[i]])
                    for v in (1, 2, 3):
                        k.dma("sp" if (v + hp) % 2 == 0 else "act", BC[v][i][ps_, :, :], tokS[hp, v, t0_:t0_ + TB, :].partition_broadcast(64),
                              r=[r_tok], w=[rBC[v][i]])

            loadblk(0)
            for bi in range(nblk):
                if bi + 1 < nblk:
                    loadblk(bi + 1)
                i = bi % 2
                for ct_ in range(2):
                    kvv = BC[3][i][:, :, ct_ * 64:(ct_ + 1) * 64]
                    k.op("dve", lambda e: e.tensor_tensor(out=kvv, in0=kvv, in1=V[:, ct_, bi * TB:(bi + 1) * TB].unsqueeze(2).to_broadcast([128, TB, 64]),
                                                          op=ALU.mult), r=[rV[0], rV[1], rBC[3][i]], w=[rBC[3][i]])
                for tl in range(TB):
                    t = bi * TB + tl

                    def bc(v):
                        return BC[v][i][:, tl, :].rearrange("p (c j) -> p c j", c=2)
                    k.op("dve", lambda e: e.tensor_tensor(out=S[:], in0=S[:], in1=bc(1), op=ALU.mult), r=[r_S, rBC[1][i]], w=[r_S])
                    if t > 0:
                        k.op("dve", lambda e: e.tensor_tensor(out=tmp[:], in0=bc(2), in1=RY[:, t - 1, 0, :].unsqueeze(2).to_broadcast([128, 2, 64]), op=ALU.mult),
                             r=[r_Yall, rBC[2][i]], w=[r_tmp])
                        k.op("dve", lambda e: e.tensor_tensor(out=S[:], in0=S[:], in1=tmp[:], op=ALU.add), r=[r_S, r_tmp], w=[r_S])
                    k.op("dve", lambda e: e.tensor_tensor(out=S[:], in0=S[:], in1=bc(3), op=ALU.add), r=[r_S, rBC[3][i]], w=[r_S])
                    k.op("dve", lambda e: e.tensor_tensor(out=tmp2[:], in0=BCAR[i][:, tl, :, :].rearrange("p w (c j) -> p w c j", c=2),
                                                          in1=S[:].unsqueeze(1).to_broadcast([128, 2, 2, 64]), op=ALU.mult),
                         r=[r_S, rBCAR[i]], w=[r_tmp2])
                    k.op("dve", lambda e: e.tensor_reduce(out=RY[:, t, :, :], in_=tmp2[:], axis=AX.X, op=ALU.add), r=[r_tmp2], w=[r_Yall])

            yb = sb2("yb", [128, 512], BF16)
            y2 = sb2("y2", [128, 512], BF16)
            mean = sb2("mean", [128, 512], F32)
            msq = sb2("msq", [128, 512], F32)
            rstd = sb2("rstd", [128, 512], F32)
            t1 = sb2("t1", [128, 512], F32)
            t2 = sb2("t2", [128, 512], F32)
            og = [sb2("og%d" % i, [128, 512], F32) for i in range(2)]
            r_yb, r_y2, r_mean, r_msq, r_rstd, r_t1, r_t2 = Res(), Res(), Res(), Res(), Res(), Res(), Res()
            r_og = [Res(), Res()]
            it = 0
            for ct in range(2):
                for tb in range(8):
                    sl = slice(tb * 512, (tb + 1) * 512)
                    k.op("act", lambda e: e.copy(out=yb[:], in_=RY[:, sl, 1, ct]), r=[r_Yall], w=[r_yb])
                    k.op("act", lambda e: e.activation(out=y2[:], in_=RY[:, sl, 1, ct], func=AF.Square), r=[r_Yall], w=[r_y2])
                    b1, rb1 = nb()
                    b2, rb2 = nb()
                    k.op("pe", lambda e: e.matmul(b1[:], BON[:], yb[:], start=True, stop=True), r=[r_yb, r_c], w=[rb1])
                    k.op("pe", lambda e: e.matmul(b2[:], BON[:], y2[:], start=True, stop=True), r=[r_y2, r_c], w=[rb2])
                    k.op("dve", lambda e: e.tensor_scalar(out=mean[:], in0=b1[:], scalar1=1.0 / 64, scalar2=None, op0=ALU.mult), r=[rb1], w=[r_mean])
                    k.op("dve", lambda e: e.tensor_tensor(out=msq[:], in0=mean[:], in1=mean[:], op=ALU.mult), r=[r_mean], w=[r_msq])
                    k.op("dve", lambda e: e.scalar_tensor_tensor(out=rstd[:], in0=b2[:], scalar=1.0 / 64, in1=msq[:], op0=ALU.mult, op1=ALU.subtract),
                         r=[rb2, r_msq], w=[r_rstd])
                    k.op("dve", lambda e: e.tensor_scalar(out=rstd[:], in0=rstd[:], scalar1=GN_EPS, scalar2=None, op0=ALU.add), r=[r_rstd], w=[r_rstd])
                    k.op("act", lambda e: e.activation(out=rstd[:], in_=rstd[:], func=AF.Sqrt), r=[r_rstd], w=[r_rstd])
                    k.op("dve", lambda e: e.reciprocal(out=rstd[:], in_=rstd[:]), r=[r_rstd], w=[r_rstd])
                    k.op("dve", lambda e: e.tensor_tensor(out=t1[:], in0=RY[:, sl, 1, ct], in1=mean[:], op=ALU.subtract), r=[r_Yall, r_mean], w=[r_t1])
                    k.op("dve", lambda e: e.tensor_tensor(out=t1[:], in0=t1[:], in1=rstd[:], op=ALU.mult), r=[r_t1, r_rstd], w=[r_t1])
                    k.op("act", lambda e: e.activation(out=t1[:], in_=t1[:], func=AF.Identity, scale=PRM[:, ct, 5:6], bias=PRM[:, ct, 6:7]),
                         r=[r_t1, r_c], w=[r_t1])
                    b3, rb3 = nb()
                    k.op("pe", lambda e: e.matmul(b3[:], BON[:], RKB[:, ct, sl], start=True, stop=True), r=[rRKB[ct], r_c], w=[rb3])
                    k.op("dve", lambda e: e.tensor_tensor(out=t2[:], in0=b3[:], in1=V[:, ct, sl], op=ALU.mult), r=[rb3, rV[ct]], w=[r_t2])
                    k.op("dve", lambda e: e.tensor_tensor(out=t1[:], in0=t1[:], in1=t2[:], op=ALU.add), r=[r_t1, r_t2], w=[r_t1])
                    b4, rb4 = nb()
                    for kc in range(2):
                        k.op("pe", lambda e: e.matmul(b4[:], GB[:, kc, ct * 128:(ct + 1) * 128], SGL[:, kc, sl], start=(kc == 0), stop=(kc == 1)),
                             r=[rSGL[kc], r_c], w=[rb4])
                    i = it % 2
                    it += 1
                    k.op("dve", lambda e: e.tensor_tensor(out=og[i][:], in0=b4[:], in1=t1[:], op=ALU.mult), r=[rb4, r_t1], w=[r_og[i]])
                    k.dma("sp", m3[bq * 4 + tb // 2, 1024 + q * 256 + ct * 128:1024 + q * 256 + (ct + 1) * 128, (tb % 2) * 512:(tb % 2 + 1) * 512], og[i][:], r=[r_og[i]])
            k.barrier()


def rwkv_params(p, q):
    c0 = q * 256
    cols = np.concatenate([np.arange(c0, c0 + 256), 1024 + np.arange(c0, c0 + 256), 2048 + np.arange(c0, c0 + 256),
                           3072 + 192 + np.arange(256), 3072 + np.arange(96), 3072 + 96 + np.arange(96)])
    mu_sel = p['rwkv_mu'][cols]
    mu = np.zeros((128, 10), np.float32)
    starts = [0, 128, 256, 384, 512, 640, 768, 896, 1024, 1120]
    sizes = [128] * 8 + [96, 96]
    for i, (s, n) in enumerate(zip(starts, sizes)):
        mu[:n, i] = mu_sel[s:s + n]
    prm = np.zeros((128, 2, 7), np.float32)
    for ct in range(2):
        sl = slice(c0 + ct * 128, c0 + (ct + 1) * 128)
        prm[:, ct, 0] = p['rwkv_w0'][sl]
        prm[:, ct, 1] = p['rwkv_a0'][sl]
        prm[:, ct, 2] = p['rwkv_k_k'][sl]
        prm[:, ct, 3] = p['rwkv_k_a'][sl]
        prm[:, ct, 4] = p['rwkv_r_k'].reshape(-1)[sl]
        prm[:, ct, 5] = p['rwkv_lnx_g'][sl]
        prm[:, ct, 6] = p['rwkv_lnx_b'][sl]
    return (mu, prm, np.ascontiguousarray(p['rwkv_w_b'][:, c0:c0 + 256]), np.ascontiguousarray(p['rwkv_a_b'][:, c0:c0 + 256]),
            np.ascontiguousarray(p['rwkv_g_b'][:, c0:c0 + 256]))
NEG = -30000.0


def emit_nsa(E, l, bq, q):
    nc, k = E.nc, E.k
    u3 = Blk3(E.uT)
    m3 = E.mixT.rearrange("(b r) t -> b r t", r=3072)
    pe = E.nsa_pe[l]
    w1k, w1v, w2k, w2v = E.w["nsa_ck_w1"][l], E.w["nsa_cv_w1"][l], E.w["nsa_ck_w2"][l], E.w["nsa_cv_w2"][l]
    ident, mdiag, mold, Eall = E.ident, E.mdiag, E.mold, E.Eall
    cmaskK, cmaskT, impmul, impadd, jok = E.cmaskK, E.cmaskT, E.impmul, E.impadd, E.jok
    qrow = 5568 + q * 256
    kvrow = [6592 + i_ * 256 + q * 64 for i_ in range(6)]
    grow = 8128 + q * 12
    with ExitStack() as st:
        uid = E.uid

        def ksb(name, shape, dt):
            uid[0] += 1
            return st.enter_context(nc.sbuf_tensor("%s_u%d" % (name, uid[0]), shape, dt))
        banks, r_bank = E.banks, E.r_bank
        r_c = Res()
        Q = ksb("Q", [64, 4, T], BF16)
        r_Q = Res()
        KS = ksb("KS", [64, T], BF16)
        KW = ksb("KW", [64, T], BF16)
        KC = ksb("KC", [64, 256], BF16)
        VS1 = ksb("VS1", [128, 32, 65], BF16)
        VW1 = ksb("VW1", [128, 32, 65], BF16)
        VC1 = ksb("VC1", [128, 2, 65], BF16)
        r_KC, r_VC = Res(), Res()
        GA = ksb("GA", [128, 32, 12], F32)
        r_GA = Res()
        IDF = ksb("IDF", [128, 128], F32)
        IDB = ksb("IDB", [128, 128], BF16)
        MD = ksb("MD", [128, 512], BF16)
        MO = ksb("MO", [128, 512], BF16)
        EA = ksb("EA", [64, 32, 128], BF16)
        IMUL = ksb("IMUL", [128, 32, 64], F32)
        IADD = ksb("IADD", [128, 32, 64], F32)
        JOK = ksb("JOK", [128, 32, 64], F32)
        OUT = ksb("OUT", [128, 32, 256], F32)
        r_OUT = Res()
        r_KS, r_KW = Res(), Res()
        for tb in range(4):
            k.dma("pool", KS[:, tb * 1024:(tb + 1) * 1024], u3[bq * 4 + tb, kvrow[2]:kvrow[2] + 64, :], w=[r_KS])
            k.dma("pool", KW[:, tb * 1024:(tb + 1) * 1024], u3[bq * 4 + tb, kvrow[4]:kvrow[4] + 64, :], w=[r_KW])
        k.dma("sp", IDF[:], ident, w=[r_c])
        k.dma("pool", IDB[:], ident, w=[r_c])
        k.dma("pool", MD[:], mdiag, w=[r_c])
        k.dma("pool", MO[:], mold, w=[r_c])
        k.dma("pool", EA[:], Eall, w=[r_c])
        k.dma("sp", IMUL[:], impmul, w=[r_c])
        k.dma("sp", IADD[:], impadd, w=[r_c])
        k.dma("sp", JOK[:], jok, w=[r_c])

        with ExitStack() as st2:
            def sb2(name, shape, dt):
                uid[0] += 1
                return st2.enter_context(nc.sbuf_tensor("%s_u%d" % (name, uid[0]), shape, dt))
            qs = sb2("qs", [64, 4, 1024], F32)
            r_qs = Res()
            for tq in range(4):
                for g in range(4):
                    k.dma("sp", qs[:, g, :], u3[bq * 4 + tq, qrow + g * 64:qrow + (g + 1) * 64, :], w=[r_qs])
                k.op("act", lambda e: e.mul(out=Q[:, :, tq * 1024:(tq + 1) * 1024], in_=qs[:], mul=0.125), r=[r_qs], w=[r_Q])
            vT = sb2("vT", [64, T], F32)
            r_vT = Res()
            k.op("dve", lambda e: e.memset(VS1[:], 1.0), w=[r_c])
            k.op("dve", lambda e: e.memset(VW1[:], 1.0), w=[r_c])
            for (row, V1_) in ((kvrow[3], VS1), (kvrow[5], VW1)):
                for tb in range(4):
                    k.dma("sp", vT[:, tb * 1024:(tb + 1) * 1024], u3[bq * 4 + tb, row:row + 64, :], w=[r_vT])
                for t4 in range(8):
                    bt_, rbt_ = banks[t4 % 2], r_bank[t4 % 2]
                    for a_ in range(4):
                        tt_ = t4 * 4 + a_
                        k.op("pe", lambda e: e.transpose(bt_[:, a_ * 64:(a_ + 1) * 64], vT[:, tt_ * 128:(tt_ + 1) * 128], IDF[0:64, 0:64]), r=[r_vT, r_c], w=[rbt_])
                    k.op("act", lambda e: e.copy(out=V1_[:, t4 * 4:(t4 + 1) * 4, 0:64], in_=bt_[:, 0:256].rearrange("p (a d) -> p a d", a=4)), r=[rbt_], w=[r_c])
            for tb in range(4):
                k.dma("sp", vT[0:12, tb * 1024:(tb + 1) * 1024], u3[bq * 4 + tb, grow:grow + 12, :], w=[r_vT])
            for t4 in range(8):
                bt_, rbt_ = banks[t4 % 2], r_bank[t4 % 2]
                for a_ in range(4):
                    tt_ = t4 * 4 + a_
                    k.op("pe", lambda e: e.transpose(bt_[:, a_ * 12:(a_ + 1) * 12], vT[0:12, tt_ * 128:(tt_ + 1) * 128], IDF[0:12, 0:12]), r=[r_vT, r_c], w=[rbt_])
                k.op("act", lambda e: e.activation(out=GA[:, t4 * 4:(t4 + 1) * 4, :], in_=bt_[:, 0:48].rearrange("p (a d) -> p a d", a=4), func=AF.Sigmoid), r=[rbt_], w=[r_GA])
            k.barrier()
            PE_ = sb2("PE_", [128, 2, 16], F32)
            k.dma("sp", PE_[:], pe, w=[r_c])
            KCT2 = sb2("KCT2", [128, T], F32)
            BL = sb2("BL", [128, 16, 256], BF16)
            W1 = sb2("W1", [128, 16, 256], BF16)
            W2 = sb2("W2", [128, 2, 64], BF16)
            HID = sb2("HID", [128, 2, 256], BF16)
            xs = sb2("xs", [128, 256], F32)
            uu = sb2("uu", [128, 256], F32)
            sg = sb2("sg", [128, 256], F32)
            r_blk, r_BL, r_W1, r_W2, r_HID, r_xs, r_uu, r_sg = [Res() for _ in range(8)]
            k.op("dve", lambda e: e.memset(BL[:], 0.0), w=[r_BL])
            k.op("dve", lambda e: e.memset(VC1[:], 1.0), w=[r_VC])
            for which in range(2):
                srow, w1, w2 = ((kvrow[0], w1k, w2k), (kvrow[1], w1v, w2v))[which]
                k.op("pool", lambda e: e.memset(KCT2[64:128, T - 1:T], 0.0), w=[r_blk])
                for tb in range(4):
                    k.dma("sp", KCT2[0:64, tb * 1024:(tb + 1) * 1024], u3[bq * 4 + tb, srow:srow + 64, :], w=[r_blk])
                    if tb == 0:
                        k.dma("act", KCT2[64:128, 0:1023], u3[bq * 4, srow:srow + 64, 1:1024], w=[r_blk])
                    else:
                        k.dma("act", KCT2[64:128, tb * 1024 - 1:(tb + 1) * 1024 - 1], u3[bq * 4 + tb, srow:srow + 64, :], w=[r_blk])
                KCv = KCT2[:].rearrange("p (n s) -> p n s", s=16)
                k.dma("pool", W1[:], w1.rearrange("(kc p) n -> p kc n", p=128), w=[r_W1])
                k.dma("pool", W2[:], w2.rearrange("(kc p) n -> p kc n", p=128), w=[r_W2])
                for kc in range(16):
                    bsrc = KCv[:, 0:255, 2 * kc] if kc < 8 else KCv[:, 1:256, 2 * kc - 16]
                    k.op("dve", lambda e: e.tensor_scalar(out=BL[:, kc, 0:255], in0=bsrc, scalar1=PE_[:, which, kc:kc + 1], scalar2=None,
                                                          op0=ALU.add), r=[r_blk, r_c], w=[r_BL])
                for hc in range(2):
                    b, rb = banks[hc], r_bank[hc]
                    for kc in range(16):
                        k.op("pe", lambda e: e.matmul(b[:, 0:256], W1[:, kc, hc * 128:(hc + 1) * 128], BL[:, kc, :], start=(kc == 0), stop=(kc == 15)),
                             r=[r_W1, r_BL], w=[rb])
                    k.op("act", lambda e: e.copy(out=xs[:], in_=b[:, 0:256]), r=[rb], w=[r_xs])
                    k.op("dve", lambda e: e.tensor_tensor(out=uu[:], in0=xs[:], in1=xs[:], op=ALU.mult), r=[r_xs], w=[r_uu])
                    k.op("dve", lambda e: e.tensor_scalar(out=uu[:], in0=uu[:], scalar1=0.044715, scalar2=1.0, op0=ALU.mult, op1=ALU.add), r=[r_uu], w=[r_uu])
                    k.op("dve", lambda e: e.tensor_tensor(out=uu[:], in0=uu[:], in1=xs[:], op=ALU.mult), r=[r_uu, r_xs], w=[r_uu])
                    k.op("act", lambda e: e.activation(out=sg[:], in_=uu[:], func=AF.Sigmoid, scale=1.5957691216), r=[r_uu], w=[r_sg])
                    k.op("dve", lambda e: e.tensor_tensor(out=HID[:, hc, :], in0=xs[:], in1=sg[:], op=ALU.mult), r=[r_xs, r_sg], w=[r_HID])
                if which == 0:
                    b, rb = banks[2], r_bank[2]
                    for hc in range(2):
                        k.op("pe", lambda e: e.matmul(b[0:64, 0:256], W2[:, hc, :], HID[:, hc, :], start=(hc == 0), stop=(hc == 1)), r=[r_W2, r_HID], w=[rb])
                    k.op("act", lambda e: e.copy(out=KC[:], in_=b[0:64, 0:256]), r=[rb], w=[r_KC])
                else:
                    for c in range(2):
                        b, rb = banks[3 + c], r_bank[3 + c]
                        for hc in range(2):
                            k.op("pe", lambda e: e.matmul(b[:, 0:64], HID[:, hc, c * 128:(c + 1) * 128], W2[:, hc, :], start=(hc == 0), stop=(hc == 1)),
                                 r=[r_W2, r_HID], w=[rb])
                        k.op("act", lambda e: e.copy(out=VC1[:, c, 0:64], in_=b[:, 0:64]), r=[rb], w=[r_VC])
            k.barrier()

        CMT = [ksb("CMT%d" % i, [128, 256], F32) for i in range(2)]
        r_CMT = [Res(), Res()]
        CMK = [ksb("CMK%d" % i, [128, 512], BF16) for i in range(4)]
        r_CMK = [Res() for i in range(4)]
        sc = ksb("sc", [128, 4, 256], F32)
        r_sc = Res()
        den = ksb("den", [128, 4], F32)
        r_den = Res()
        PP = ksb("PP", [128, 264], F32)
        r_PP = Res()
        imp = ksb("imp", [128, 64], F32)
        imp3 = ksb("imp3", [128, 64], F32)
        m8 = ksb("m8", [128, 8], F32)
        thr = ksb("thr", [128, 1], F32)
        sel = ksb("sel", [128, 64], F32)
        r_imp, r_imp3, r_m8, r_thr, r_sel = [Res() for _ in range(5)]
        SELB = ksb("SELB", [64, 4, 128], BF16)
        r_SELB = Res()
        PT = [ksb("PT%d" % i, [128, 4, 128], BF16) for i in range(3)]
        r_PT = [Res() for i in range(3)]
        cf = ksb("cf", [128, 4], F32)
        tmpo = ksb("tmpo", [128, 4, 64], F32)
        r_cf, r_tmpo = Res(), Res()
        k.op("dve", lambda e: e.memset(PP[:], 0.0), w=[r_PP])
        pt_i = [0]
        st_i = [0]
        cmk_i = [0]

        def attend(qi, kT, kblocks, V1, v_of, Ob, rOb, masks):
            t0 = qi * 128
            nk = len(kblocks)
            for ii, kb in enumerate(kblocks):
                si = 3 + (st_i[0] % 2)
                st_i[0] += 1
                S_, rS_ = banks[si], r_bank[si]
                ml = masks(kb)
                k.op("pe", lambda e: e.matmul(S_[:], kT[:, kb * 128:(kb + 1) * 128], Q[:, :, t0:t0 + 128], start=True, stop=(len(ml) == 0)),
                     r=[r_c, r_Q, r_KC, r_KS, r_KW], w=[rS_])
                for mi, (ml_l, ml_r, ml_res) in enumerate(ml):
                    k.op("pe", lambda e: e.matmul(S_[:], ml_l, ml_r, start=False, stop=(mi == len(ml) - 1)), r=[r_c] + ml_res, w=[rS_])
                pi = pt_i[0] % 3
                pt_i[0] += 1
                k.op("act", lambda e: e.activation(out=PT[pi][:].rearrange("p g t -> p (g t)"), in_=S_[:], func=AF.Exp), r=[rS_], w=[r_PT[pi]])
                for g in range(4):
                    k.op("pe", lambda e: e.matmul(Ob[:, g * 65:(g + 1) * 65], PT[pi][:, g, :], v_of(kb), start=(ii == 0 and g == 0), stop=(ii == nk - 1), skip_group_check=True),
                         r=[r_PT[pi], r_c, r_VC], w=[rOb])

        def combine(qi, Ob, rOb, c, first):
            Ov = Ob[:, 0:260].rearrange("p (g e) -> p g e", g=4)
            k.op("dve", lambda e: e.tensor_scalar(out=cf[:].unsqueeze(2), in0=Ov[:, :, 64:65], scalar1=1e-30, scalar2=None, op0=ALU.max), r=[rOb], w=[r_cf])
            k.op("dve", lambda e: e.reciprocal(out=cf[:], in_=cf[:]), r=[r_cf], w=[r_cf])
            gsl = GA[:, qi, :].rearrange("p (g c) -> p g c", c=3)[:, :, c]
            k.op("dve", lambda e: e.tensor_tensor(out=cf[:], in0=cf[:], in1=gsl, op=ALU.mult), r=[r_cf, r_GA], w=[r_cf])
            ov = OUT[:, qi, :].rearrange("p (g d) -> p g d", g=4)
            if first:
                k.op("dve", lambda e: e.tensor_tensor(out=ov, in0=Ov[:, :, 0:64], in1=cf[:].unsqueeze(2).to_broadcast([128, 4, 64]), op=ALU.mult),
                     r=[rOb, r_cf], w=[r_OUT])
            else:
                k.op("dve", lambda e: e.tensor_tensor(out=tmpo[:], in0=Ov[:, :, 0:64], in1=cf[:].unsqueeze(2).to_broadcast([128, 4, 64]), op=ALU.mult),
                     r=[rOb, r_cf], w=[r_tmpo])
                k.op("dve", lambda e: e.tensor_tensor(out=ov, in0=ov, in1=tmpo[:], op=ALU.add), r=[r_tmpo, r_OUT], w=[r_OUT])

        for qi in range(32):
            t0 = qi * 128
            ci = qi % 2
            k.dma("sp", CMT[ci][:], cmaskT[qi], w=[r_CMT[ci]])
            for h2 in range(2):
                b, rb = banks[h2], r_bank[h2]
                for gg in range(2):
                    g = h2 * 2 + gg
                    k.op("pe", lambda e: e.matmul(b[:, gg * 256:(gg + 1) * 256], Q[:, g, t0:t0 + 128], KC[:], start=True, stop=True), r=[r_Q, r_KC], w=[rb])
                k.op("dve", lambda e: e.tensor_tensor(out=sc[:, h2 * 2:h2 * 2 + 2, :], in0=b[:].rearrange("p (g n) -> p g n", g=2),
                                                      in1=CMT[ci][:].unsqueeze(1).to_broadcast([128, 2, 256]), op=ALU.add), r=[rb, r_CMT[ci]], w=[r_sc])
            k.op("act", lambda e: e.activation(out=sc[:], in_=sc[:], func=AF.Exp), r=[r_sc], w=[r_sc])
            k.op("dve", lambda e: e.tensor_reduce(out=den[:], in_=sc[:], axis=AX.X, op=ALU.add), r=[r_sc], w=[r_den])
            k.op("dve", lambda e: e.tensor_scalar(out=den[:], in0=den[:], scalar1=1e-30, scalar2=None, op0=ALU.max), r=[r_den], w=[r_den])
            k.op("dve", lambda e: e.reciprocal(out=den[:], in_=den[:]), r=[r_den], w=[r_den])
            k.op("dve", lambda e: e.tensor_tensor(out=sc[:], in0=sc[:], in1=den[:].unsqueeze(2).to_broadcast([128, 4, 256]), op=ALU.mult), r=[r_sc, r_den], w=[r_sc])
            k.op("dve", lambda e: e.tensor_reduce(out=PP[:, 4:260], in_=sc[:].rearrange("p g n -> p n g"), axis=AX.X, op=ALU.add), r=[r_sc], w=[r_PP])
            PPv = PP[:].rearrange("p (j f) -> p j f", f=4)
            k.op("dve", lambda e: e.tensor_reduce(out=imp[:], in_=PPv[:, 1:65, :], axis=AX.X, op=ALU.add), r=[r_PP], w=[r_imp])
            k.op("dve", lambda e: e.tensor_tensor(out=imp[:], in0=imp[:], in1=PPv[:, 0:64, 3], op=ALU.add), r=[r_PP, r_imp], w=[r_imp])
            k.op("dve", lambda e: e.tensor_tensor(out=imp[:], in0=imp[:], in1=IMUL[:, qi, :], op=ALU.mult), r=[r_imp, r_c], w=[r_imp])
            k.op("dve", lambda e: e.tensor_tensor(out=imp[:], in0=imp[:], in1=IADD[:, qi, :], op=ALU.add), r=[r_imp, r_c], w=[r_imp])
            k.op("dve", lambda e: e.max(out=m8[:], in_=imp[:]), r=[r_imp], w=[r_m8])
            k.op("dve", lambda e: e.match_replace(out=imp3[:], in_to_replace=m8[:], in_values=imp[:], imm_value=-3.0e38), r=[r_imp, r_m8], w=[r_imp3])
            k.op("dve", lambda e: e.max(out=m8[:], in_=imp3[:]), r=[r_imp3], w=[r_m8])
            k.op("dve", lambda e: e.tensor_reduce(out=thr[:], in_=m8[:], axis=AX.X, op=ALU.min), r=[r_m8], w=[r_thr])
            k.op("dve", lambda e: e.tensor_scalar(out=sel[:], in0=imp[:], scalar1=thr[:], scalar2=None, op0=ALU.is_ge), r=[r_imp, r_thr], w=[r_sel])
            k.op("dve", lambda e: e.tensor_tensor(out=sel[:], in0=sel[:], in1=JOK[:, qi, :], op=ALU.mult), r=[r_sel, r_c], w=[r_sel])
            k.op("dve", lambda e: e.tensor_scalar(out=sel[:], in0=sel[:], scalar1=-1.0, scalar2=-NEG, op0=ALU.add, op1=ALU.mult), r=[r_sel], w=[r_sel])
            bt, rbt = banks[2], r_bank[2]
            k.op("pe", lambda e: e.transpose(bt[0:64, 0:128], sel[:], IDF[:]), r=[r_sel, r_c], w=[rbt])
            for g in range(4):
                k.op("act", lambda e: e.copy(out=SELB[:, g, :], in_=bt[0:64, 0:128]), r=[rbt], w=[r_SELB])
            cchunks = [0] + ([1] if qi >= 16 else [])
            cm_of = {}
            for c in cchunks:
                i = cmk_i[0] % 4
                cmk_i[0] += 1
                k.dma("pool", CMK[i][:], cmaskK[qi, c], w=[r_CMK[i]])
                cm_of[c] = i
            attend(qi, KC, cchunks, VC1, lambda c: VC1[:, c, :], banks[5], r_bank[5],
                   lambda c: [(IDB[:], CMK[cm_of[c]][:], [r_CMK[cm_of[c]]])])
            combine(qi, banks[5], r_bank[5], 0, True)
            attend(qi, KS, list(range(qi + 1)), VS1, lambda kb: VS1[:, kb, :], banks[6], r_bank[6],
                   lambda kb: [(EA[:, kb, :], SELB[:].rearrange("j g t -> j (g t)"), [r_SELB])] + ([(IDB[:], MD[:], [])] if kb == qi else []))
            combine(qi, banks[6], r_bank[6], 1, False)
            attend(qi, KW, list(range(max(0, qi - 4), qi + 1)), VW1, lambda kb: VW1[:, kb, :], banks[7], r_bank[7],
                   lambda kb: ([(IDB[:], MD[:], [])] if kb == qi else []) + ([(IDB[:], MO[:], [])] if kb == qi - 4 else []))
            combine(qi, banks[7], r_bank[7], 2, False)
        ost = [ksb("ost%d" % i, [128, 512], F32) for i in range(2)]
        r_ost = [Res(), Res()]
        oi = 0
        for ct in range(2):
            for t4 in range(8):
                bt_, rbt_ = banks[oi % 2], r_bank[oi % 2]
                for a_ in range(4):
                    tt_ = t4 * 4 + a_
                    k.op("pe", lambda e: e.transpose(bt_[:, a_ * 128:(a_ + 1) * 128], OUT[:, tt_, ct * 128:(ct + 1) * 128], IDF[:]), r=[r_OUT, r_c], w=[rbt_])
                k.op("act", lambda e: e.copy(out=ost[oi % 2][:], in_=bt_[:]), r=[rbt_], w=[r_ost[oi % 2]])
                k.dma("sp", m3[bq * 4 + t4 // 2, 2048 + q * 256 + ct * 128:2048 + q * 256 + (ct + 1) * 128, (t4 % 2) * 512:(t4 % 2 + 1) * 512],
                      ost[oi % 2][:], r=[r_ost[oi % 2]])
                oi += 1
        k.barrier()


_NSA_CONST = {}


def nsa_consts():
    if _NSA_CONST:
        return _NSA_CONST
    s = np.arange(128)[:, None]
    t = np.arange(128)[None, :]
    md = np.where(s <= t, 0.0, NEG).astype(np.float32)
    mo = np.where(s > t, 0.0, NEG).astype(np.float32)
    E = np.zeros((64, 32, 128), np.float32)
    for kb in range(32):
        for sl in range(128):
            E[2 * kb + sl // 64, kb, sl] = 1.0
    cK = np.full((32, 2, 128, 512), NEG, np.float32)
    cT = np.full((32, 128, 256), NEG, np.float32)
    for qi in range(32):
        tt = qi * 128 + np.arange(128)
        n = np.arange(256)
        ok = (16 * n[None, :] + 31 <= tt[:, None]) & (n[None, :] < 255)
        cT[qi] = np.where(ok, 0.0, NEG)
        for c in range(2):
            okc = ok[:, c * 128:(c + 1) * 128].T
            cK[qi, c] = np.tile(np.where(okc, 0.0, NEG), (1, 4))
    tpos = np.arange(T)
    cur = tpos // 64
    j = np.arange(64)[None, :]
    curc = cur[:, None]
    forced = (j == 0) | (j == curc) | (j == curc - 1)
    fut = j > curc
    mul = np.where(forced | fut, 0.0, 1.0).astype(np.float32)
    add = np.zeros((T, 64), np.float32)
    add = np.where(j == curc - 1, 1e30, add)
    add = np.where(j == curc, 2e30, add)
    add = np.where(j == 0, 3e30, add)
    add = np.where(fut, -1e30, add).astype(np.float32)
    okj = (~fut).astype(np.float32)

    def tm(a):
        return np.ascontiguousarray(a.reshape(32, 128, 64).transpose(1, 0, 2))
    _NSA_CONST.update({"ident": np.eye(128, dtype=np.float32), "mdiag": np.tile(md, (1, 4)), "mold": np.tile(mo, (1, 4)), "Eall": E,
                       "cmaskK": cK, "cmaskT": cT, "impmul": tm(mul), "impadd": tm(add), "jok": tm(okj)})
    return _NSA_CONST


class Env:
    pass


class Blk3:
    def __init__(self, lst):
        self.lst = lst

    def __getitem__(self, key):
        return self.lst[key[0]][key[1], key[2]]


NBATCH_PER_CORE = 1
_WNAMES = ["ffn1_w13", "ffn1_w2", "w_in", "w_conv_out", "w_rwkv_out", "w_nsa_out", "w_out", "ffn2_w13", "ffn2_w2",
           "nsa_ck_w1", "nsa_ck_w2", "nsa_cv_w1", "nsa_cv_w2"]
_WSHAPES = {"ffn1_w13": [2, D, 2 * DFF], "ffn1_w2": [2, DFF, D], "w_in": [2, D, DIN], "w_conv_out": [2, 1024, D],
            "w_rwkv_out": [2, 1024, D], "w_nsa_out": [2, 1024, D], "w_out": [2, D, D], "ffn2_w13": [2, D, 2 * DFF],
            "ffn2_w2": [2, DFF, D], "nsa_ck_w1": [2, 2048, 256], "nsa_ck_w2": [2, 256, 64], "nsa_cv_w1": [2, 2048, 256],
            "nsa_cv_w2": [2, 256, 64]}
_CSHAPES = {"gains": [128, 7, 16], "conv_cw": [2, 128, 8, 31], "conv_pb": [2, 128, 3, 8], "rw_mu": [2, 4, 128, 10],
            "rw_prm": [2, 4, 128, 2, 7], "rw_wb": [2, 4, 96, 256], "rw_ab": [2, 4, 96, 256], "rw_gb": [2, 4, 256, 256],
            "ident": [128, 128], "bones": [128, 128], "nsa_pe": [2, 128, 2, 16], "mdiag": [128, 512], "mold": [128, 512],
            "Eall": [64, 32, 128], "cmaskK": [32, 2, 128, 512], "cmaskT": [32, 128, 256], "impmul": [128, 32, 64],
            "impadd": [128, 32, 64], "jok": [128, 32, 64]}


def build_fused(NBATCH):
    nc = bass.Bass("TRN2", target_bir_lowering=False)
    NBLK = 4 * NBATCH

    def din(name, shape):
        return nc.dram_tensor(name, list(shape), F32, kind="ExternalInput").ap()
    E = Env()
    E.nc, E.NBLK = nc, NBLK
    xT = din("xT", [NBLK * D, NT])
    E.outT = nc.dram_tensor("outT", [NBLK * D, NT], F32, kind="ExternalOutput").ap()
    E.w = {nm: din(nm, _WSHAPES[nm]) for nm in _WNAMES}
    cst = {nm: din(nm, sh) for nm, sh in _CSHAPES.items()}
    E.gains_d = cst["gains"]
    E.conv_cw, E.conv_pb = cst["conv_cw"], cst["conv_pb"]
    E.rw_mu, E.rw_prm, E.rw_wb, E.rw_ab, E.rw_gb = cst["rw_mu"], cst["rw_prm"], cst["rw_wb"], cst["rw_ab"], cst["rw_gb"]
    E.ident, E.bones, E.nsa_pe = cst["ident"], cst["bones"], cst["nsa_pe"]
    E.mdiag, E.mold, E.Eall, E.cmaskK, E.cmaskT = cst["mdiag"], cst["mold"], cst["Eall"], cst["cmaskK"], cst["cmaskT"]
    E.impmul, E.impadd, E.jok = cst["impmul"], cst["impadd"], cst["jok"]
    E.uT = [nc.dram_tensor("uT_s%d" % i, [DIN, NT], F32).ap() for i in range(NBLK)]
    E.x1T = nc.dram_tensor("x1T_s", [NBLK * D, NT], F32).ap()
    E.mixT = nc.dram_tensor("mixT_s", [NBLK * 3072, NT], F32).ap()
    E.tokS = nc.dram_tensor("tokS_s", [2, 5, T, 128], F32).ap()
    with ExitStack() as st:
        k = KB(nc, st)
        E.k = k
        E.banks = [k.ps("bank%d" % i, [128, 512], F32) for i in range(8)]
        E.r_bank = [Res() for i in range(8)]
        E.bank_i = [0]
        E.uid = [0]
        emit_tl(E, False, True, False, 0, xT)
        for l in range(2):
            for blk in range(NBLK):
                emit_conv(E, l, blk)
            for b in range(NBATCH):
                for q in range(4):
                    emit_rwkv(E, l, b, q)
                    emit_nsa(E, l, b, q)
            if l == 0:
                emit_tl(E, True, True, False, 0, E.x1T)
            else:
                emit_tl(E, True, False, True, 1, E.x1T)
        k.barrier()
        print("fused instructions:", k.ninst)
    return nc


_PROG = {}


def _g16(v):
    return np.ascontiguousarray(np.asarray(v, np.float32).reshape(16, 128).T)


def kernel(**inp):
    inp = {k_: np.asarray(v_, np.float32) for k_, v_ in inp.items()}
    NB = NBATCH_PER_CORE
    ncores = 2 // NB
    if "nc" not in _PROG:
        _PROG["nc"] = build_fused(NB)
    nc = _PROG["nc"]
    x = inp["x"]
    xT = np.ascontiguousarray(x.reshape(8, 1024, 2048).transpose(0, 2, 1)).reshape(8 * 2048, 1024)
    m = {nm: inp[nm] for nm in _WNAMES}
    gains = np.zeros((128, 7, 16), np.float32)
    for l in range(2):
        gains[:, 3 * l] = _g16(inp["ffn1_norm"][l])
        gains[:, 3 * l + 1] = _g16(inp["mix_norm"][l])
        gains[:, 3 * l + 2] = _g16(inp["ffn2_norm"][l])
    gains[:, 6] = _g16(inp["final_norm"])
    m["gains"] = gains
    cw, pb, mu, prm, wb, ab, gb, pe = [], [], [], [], [], [], [], []
    for l in range(2):
        p = {k_: v_[l] for k_, v_ in inp.items() if k_ not in ("x", "final_norm")}
        c_, p_ = conv_params(p)
        cw.append(c_)
        pb.append(p_)
        rp = [rwkv_params(p, q) for q in range(4)]
        mu.append(np.stack([r_[0] for r_ in rp]))
        prm.append(np.stack([r_[1] for r_ in rp]))
        wb.append(np.stack([r_[2] for r_ in rp]))
        ab.append(np.stack([r_[3] for r_ in rp]))
        gb.append(np.stack([r_[4] for r_ in rp]))
        pe.append(np.stack([p["nsa_pe_k"].reshape(16, 128).T, p["nsa_pe_v"].reshape(16, 128).T], axis=1))
    m.update({"conv_cw": np.stack(cw), "conv_pb": np.stack(pb), "rw_mu": np.stack(mu), "rw_prm": np.stack(prm),
              "rw_wb": np.stack(wb), "rw_ab": np.stack(ab), "rw_gb": np.stack(gb), "nsa_pe": np.stack(pe)})
    bones = np.zeros((128, 128), np.float32)
    bones[:64, :64] = 1
    bones[64:, 64:] = 1
    m["bones"] = bones
    m.update(nsa_consts())
    m = {k_: np.ascontiguousarray(v_, dtype=np.float32) for k_, v_ in m.items()}
    maps = []
    per = 8 // ncores
    for c in range(ncores):
        mc = dict(m)
        mc["xT"] = np.ascontiguousarray(xT[c * per * 2048:(c + 1) * per * 2048])
        maps.append(mc)
    res = run_bass_kernel_spmd(nc, maps, core_ids=list(range(ncores))).results
    outT = np.concatenate([r_["outT"] for r_ in res], axis=0)
    out = np.ascontiguousarray(outT.reshape(8, 2048, 1024).transpose(0, 2, 1)).reshape(2, 4096, 2048)
    return out.astype(np.float32)
```

```python
import numpy as np
from contextlib import ExitStack
import concourse.bass as bass
import concourse.mybir as mybir
from concourse.bass_utils import run_bass_kernel_spmd

F32 = mybir.dt.float32
BF16 = mybir.dt.bfloat16
AF = mybir.ActivationFunctionType
ALU = mybir.AluOpType
AX = mybir.AxisListType


class Res:
    __slots__ = ("name", "lw", "rd")

    def __init__(self, name=""):
        self.name = name
        self.lw = None
        self.rd = {}


class KB:
    NDMA = 6

    def __init__(self, nc, st):
        self.nc = nc
        self.st = st
        self.eng = {"pe": nc.tensor, "act": nc.scalar, "dve": nc.vector,
                    "pool": nc.gpsimd, "sp": nc.sync}
        self.sems = {}
        self.cnt = {}
        for e in self.eng:
            self.sems[e] = st.enter_context(nc.semaphore("c_" + e))
            self.cnt[e] = 0
        self.dq = {}
        for q in ("sp", "pool", "act"):
            lst = []
            for i in range(self.NDMA):
                key = "d_%s%d" % (q, i)
                self.sems[key] = st.enter_context(nc.semaphore(key))
                self.cnt[key] = 0
                lst.append(key)
            self.dq[q] = [lst, 0]
        self.known = {e: {} for e in self.eng}
        self.ninst = 0
        self.noself = set()

    def sb(self, name, shape, dt):
        return self.st.enter_context(self.nc.sbuf_tensor(name, shape, dt))

    def ps(self, name, shape, dt=F32):
        return self.st.enter_context(self.nc.psum_tensor(name, shape, dt))

    def _wait(self, e, deps, keep_last=False):
        best = {}
        for (k, v) in deps:
            if best.get(k, 0) < v:
                best[k] = v
        todo = []
        for k, v in best.items():
            if k == e and (e == "pe" or e in self.noself):
                continue
            if self.known[e].get(k, 0) >= v:
                continue
            todo.append((k, v))
            self.known[e][k] = v
        last = None
        if keep_last and todo:
            last = todo.pop()
        for k, v in todo:
            self.eng[e].wait_ge(self.sems[k], v)
            self.ninst += 1
        return last

    def _deps(self, r, w):
        deps = []
        for x in r:
            if x.lw is not None:
                deps.append(x.lw)
        for x in w:
            if x.lw is not None:
                deps.append(x.lw)
            deps.extend(x.rd.items())
        return deps

    def op(self, e, fn, r=(), w=()):
        last = self._wait(e, self._deps(r, w), keep_last=True)
        ins = fn(self.eng[e])
        if last is not None:
            ins._wait_ge(self.sems[last[0]], last[1])
        self.cnt[e] += 1
        ins.then_inc(self.sems[e], 1)
        tok = (e, self.cnt[e])
        for x in r:
            x.rd[e] = self.cnt[e]
        for x in w:
            x.lw = tok
            x.rd = {}
        self.ninst += 1
        return ins

    def dma(self, q, out, in_, r=(), w=(), **kw):
        lst, i = self.dq[q]
        key = lst[i % len(lst)]
        self.dq[q][1] = i + 1
        deps = self._deps(r, w)
        if self.cnt[key] > 0:
            deps.append((key, self.cnt[key]))
        last = self._wait(q, deps, keep_last=True)
        ins = self.eng[q].dma_start(out=out, in_=in_, **kw)
        if last is not None:
            ins._wait_ge(self.sems[last[0]], last[1])
        self.cnt[key] += 16
        ins.then_inc(self.sems[key], 16)
        tok = (key, self.cnt[key])
        for x in r:
            x.rd[key] = self.cnt[key]
        for x in w:
            x.lw = tok
            x.rd = {}
        self.ninst += 1
        return ins

    def wait_all(self, e, res):
        deps = []
        for x in res:
            if x.lw is not None:
                deps.append(x.lw)
            deps.extend(x.rd.items())
        self._wait(e, deps)

    def barrier(self):
        allk = [(k, v) for k, v in self.cnt.items() if v > 0]
        for e in self.eng:
            self._wait(e, allk)


D = 2048
DFF = 5632
DIN = 14320
NT = 1024
KC = D // 128
EPS = 1e-6


def emit_tl(E, do_C, do_A, do_final, l, xT):
    nc, k, NBLK = E.nc, E.k, E.NBLK
    W = E.w
    uT, x1T, mixT, outT = E.uT, E.x1T, E.mixT, E.outT
    if do_C:
        mixT = E.mixT
        brgT = None
        w_br = [W["w_conv_out"][l], W["w_rwkv_out"][l], W["w_nsa_out"][l]]
        w_out = W["w_out"][l]
        ffn2_w13 = W["ffn2_w13"][l]
        ffn2_w2 = W["ffn2_w2"][l]
    if do_A:
        la = l + 1 if do_C else l
        ffn1_w13 = W["ffn1_w13"][la]
        ffn1_w2 = W["ffn1_w2"][la]
        w_in = W["w_in"][la]
    gsel = {}
    if do_C:
        gsel["ffn2_norm"] = 3 * l + 2
    if do_A:
        gsel["ffn1_norm"] = 3 * la
        gsel["mix_norm"] = 3 * la + 1
    if do_final:
        gsel["final_norm"] = 6
    with ExitStack() as st:
        uid = E.uid

        def ksb(name, shape, dt):
            uid[0] += 1
            return st.enter_context(nc.sbuf_tensor("%s_u%d" % (name, uid[0]), shape, dt))
        X = ksb("X", [128, KC, NT], F32)
        H = ksb("H", [128, KC, NT], BF16)
        rX = [Res("X%d" % c) for c in range(KC)]
        rH = [Res("H%d" % c) for c in range(KC)]
        ones = ksb("ones", [128, 128], BF16)
        r_ones = Res("ones")
        gains = ksb("gains", [128, 7, KC], F32)
        r_gains = Res("gains")
        sq = [ksb("sq%d" % i, [128, NT], BF16) for i in range(2)]
        r_sq = [Res("sq%d" % i) for i in range(2)]
        rstd = ksb("rstd", [128, NT], F32)
        r_rstd = Res("rstd")
        banks, r_bank, bank_i = E.banks, E.r_bank, E.bank_i
        blk = [0]

        def nb():
            i = bank_i[0] % 8
            bank_i[0] += 1
            return banks[i], r_bank[i]

        k.op("dve", lambda e: e.memset(ones[:], 1.0), w=[r_ones])
        xvs = [xT[b_ * D:(b_ + 1) * D, :].rearrange("(c p) t -> p c t", p=128) for b_ in range(NBLK)]
        k.dma("sp", gains[:], E.gains_d, w=[r_gains])
        gidx = gsel

        def rmsnorm(gname, to_x=False):
            g = gidx[gname]
            b0, rb0 = nb()
            b1, rb1 = nb()
            for c in range(KC):
                s, rs = sq[c % 2], r_sq[c % 2]
                k.op("act", lambda e: e.activation(out=s[:], in_=X[:, c, :], func=AF.Square), r=[rX[c]], w=[rs])
                for th, (b, rb) in enumerate(((b0, rb0), (b1, rb1))):
                    k.op("pe", lambda e: e.matmul(b[:], ones[:], s[:, th * 512:(th + 1) * 512],
                                                   start=(c == 0), stop=(c == KC - 1)), r=[rs, r_ones], w=[rb])
            for th, (b, rb) in enumerate(((b0, rb0), (b1, rb1))):
                sl = slice(th * 512, (th + 1) * 512)
                k.op("dve", lambda e: e.tensor_scalar(out=rstd[:, sl], in0=b[:], scalar1=1.0 / D, scalar2=EPS,
                                                      op0=ALU.mult, op1=ALU.add), r=[rb], w=[r_rstd])
            k.op("act", lambda e: e.activation(out=rstd[:], in_=rstd[:], func=AF.Sqrt), r=[r_rstd], w=[r_rstd])
            k.op("dve", lambda e: e.reciprocal(out=rstd[:], in_=rstd[:]), r=[r_rstd], w=[r_rstd])
            for c in range(KC):
                if to_x:
                    k.op("dve", lambda e: e.scalar_tensor_tensor(out=X[:, c, :], in0=X[:, c, :], scalar=gains[:, g, c:c + 1],
                                                                 in1=rstd[:], op0=ALU.mult, op1=ALU.mult),
                         r=[rX[c], r_rstd, r_gains], w=[rX[c]])
                else:
                    k.op("dve", lambda e: e.scalar_tensor_tensor(out=H[:, c, :], in0=X[:, c, :], scalar=gains[:, g, c:c + 1],
                                                                 in1=rstd[:], op0=ALU.mult, op1=ALU.mult),
                         r=[rX[c], r_rstd, r_gains], w=[rH[c]])

        def ffn(w13, w2):
            NG = 4
            GF = 11
            w13v = w13.rearrange("(kc p) n -> p kc n", p=128)
            w2v = w2.rearrange("(f p) n -> p f n", p=128)
            with ExitStack() as st2:
                def sb2(name, shape, dt):
                    uid[0] += 1
                    return st2.enter_context(nc.sbuf_tensor("%s_u%d" % (name, uid[0]), shape, dt))
                G = sb2("G", [128, GF, NT], BF16)
                rG = [Res("G%d" % i) for i in range(GF)]
                W1 = [sb2("W1_%d" % i, [128, KC, 128], BF16) for i in range(2)]
                W3 = [sb2("W3_%d" % i, [128, KC, 128], BF16) for i in range(2)]
                rW1 = [Res() for i in range(2)]
                rW3 = [Res() for i in range(2)]
                W2 = sb2("W2", [128, GF, D], BF16)
                rW2 = [Res() for i in range(GF)]
                sa = [sb2("sa%d" % i, [128, 512], F32) for i in range(2)]
                r_sa = [Res() for i in range(2)]

                def load13(f):
                    i = f % 2
                    k.dma("pool", W1[i][:], w13v[:, :, f * 128:(f + 1) * 128], w=[rW1[i]])
                    k.dma("pool", W3[i][:], w13v[:, :, DFF + f * 128:DFF + (f + 1) * 128], w=[rW3[i]])

                load13(0)
                it = 0
                for gidx_ in range(NG):
                    for fl in range(GF):
                        f = gidx_ * GF + fl
                        if f + 1 < NG * GF:
                            load13(f + 1)
                        if fl == 0:
                            for j in range(GF):
                                k.dma("pool", W2[:, j, :], w2v[:, gidx_ * GF + j, :], w=[rW2[j]])
                        i = f % 2
                        for th in range(2):
                            sl = slice(th * 512, (th + 1) * 512)
                            ba, rba = nb()
                            bb, rbb = nb()
                            for kc in range(KC):
                                k.op("pe", lambda e: e.matmul(ba[:], W1[i][:, kc, :], H[:, kc, sl], start=(kc == 0), stop=(kc == KC - 1)),
                                     r=[rW1[i], rH[kc]], w=[rba])
                            for kc in range(KC):
                                k.op("pe", lambda e: e.matmul(bb[:], W3[i][:, kc, :], H[:, kc, sl], start=(kc == 0), stop=(kc == KC - 1)),
                                     r=[rW3[i], rH[kc]], w=[rbb])
                            s_, rs_ = sa[it % 2], r_sa[it % 2]
                            it += 1
                            k.op("act", lambda e: e.activation(out=s_[:], in_=ba[:], func=AF.Silu), r=[rba], w=[rs_])
                            k.op("dve", lambda e: e.tensor_tensor(out=G[:, fl, sl], in0=s_[:], in1=bb[:], op=ALU.mult),
                                 r=[rs_, rbb], w=[rG[fl]])
                    for m in range(KC):
                        for th in range(2):
                            sl = slice(th * 512, (th + 1) * 512)
                            b, rb = nb()
                            for fl in range(GF):
                                k.op("pe", lambda e: e.matmul(b[:], W2[:, fl, m * 128:(m + 1) * 128], G[:, fl, sl],
                                                               start=(fl == 0), stop=(fl == GF - 1)),
                                     r=[rW2[fl], rG[fl]], w=[rb])
                            k.op("dve", lambda e: e.scalar_tensor_tensor(out=X[:, m, sl], in0=b[:], scalar=0.5, in1=X[:, m, sl],
                                                                         op0=ALU.mult, op1=ALU.add),
                                 r=[rb, rX[m]], w=[rX[m]])
                k.barrier()

        def c_phase():
            mixv = mixT[blk[0] * 3072:(blk[0] + 1) * 3072, :].rearrange("(c p) t -> p c t", p=128)
            brgv = uT[blk[0]][8176:DIN, :].rearrange("(b m p) t -> p b m t", p=128, b=3)
            with ExitStack() as st2:
                def sb2(name, shape, dt):
                    uid[0] += 1
                    return st2.enter_context(nc.sbuf_tensor("%s_u%d" % (name, uid[0]), shape, dt))
                MIX = sb2("MIX", [128, 24, NT], BF16)
                rMIX = [Res() for i in range(24)]
                for c0 in range(0, 24, 4):
                    k.dma("pool", MIX[:, c0:c0 + 4, :], mixv[:, c0:c0 + 4, :], w=rMIX[c0:c0 + 4])
                BRG = [sb2("BRG%d" % i, [128, 3, 512], F32) for i in range(2)]
                rBRG = [Res() for i in range(2)]
                WB = [[sb2("WB%d_%d" % (b, i), [128, 8, 128], BF16) for i in range(2)] for b in range(3)]
                rWB = [[Res() for i in range(2)] for b in range(3)]
                sg = [sb2("sg%d" % i, [128, 512], F32) for i in range(3)]
                r_sg = [Res() for i in range(3)]
                tt = [sb2("tt%d" % i, [128, 512], F32) for i in range(3)]
                r_tt = [Res() for i in range(3)]
                wbv = [w.rearrange("(kc p) n -> p kc n", p=128) for w in w_br]

                def loadm(m):
                    i = m % 2
                    for b in range(3):
                        k.dma("pool", WB[b][i][:], wbv[b][:, :, m * 128:(m + 1) * 128], w=[rWB[b][i]])

                def loadbrg(it_):
                    m_, th_ = it_ // 2, it_ % 2
                    k.dma("sp", BRG[it_ % 2][:], brgv[:, :, m_, th_ * 512:(th_ + 1) * 512], w=[rBRG[it_ % 2]])

                loadm(0)
                loadbrg(0)
                for m in range(KC):
                    if m + 1 < KC:
                        loadm(m + 1)
                    i = m % 2
                    for th in range(2):
                        sl = slice(th * 512, (th + 1) * 512)
                        bi = (m * 2 + th) % 2
                        if m * 2 + th + 1 < 2 * KC:
                            loadbrg(m * 2 + th + 1)
                        pb = []
                        for b in range(3):
                            bk, rbk = nb()
                            pb.append((bk, rbk))
                            for kc in range(8):
                                k.op("pe", lambda e: e.matmul(bk[:], WB[b][i][:, kc, :], MIX[:, b * 8 + kc, sl],
                                                               start=(kc == 0), stop=(kc == 7)),
                                     r=[rWB[b][i], rMIX[b * 8 + kc]], w=[rbk])
                        for b in range(3):
                            k.op("act", lambda e: e.activation(out=sg[b][:], in_=BRG[bi][:, b, :], func=AF.Sigmoid),
                                 r=[rBRG[bi]], w=[r_sg[b]])
                            k.op("dve", lambda e: e.tensor_tensor(out=tt[b][:], in0=sg[b][:], in1=pb[b][0][:], op=ALU.mult),
                                 r=[r_sg[b], pb[b][1]], w=[r_tt[b]])
                        k.op("dve", lambda e: e.tensor_tensor(out=tt[0][:], in0=tt[0][:], in1=tt[1][:], op=ALU.add),
                             r=[r_tt[0], r_tt[1]], w=[r_tt[0]])
                        k.op("dve", lambda e: e.tensor_tensor(out=H[:, m, sl], in0=tt[0][:], in1=tt[2][:], op=ALU.add),
                             r=[r_tt[0], r_tt[2]], w=[rH[m]])
                k.barrier()
            wov = w_out.rearrange("(kc p) n -> p kc n", p=128)
            with ExitStack() as st2:
                WO = [st2.enter_context(nc.sbuf_tensor("WO%d_b%d_%d" % (i, blk[0], l * 10 + do_A), [128, KC, 128], BF16)) for i in range(2)]
                rWO = [Res() for i in range(2)]
                k.dma("pool", WO[0][:], wov[:, :, 0:128], w=[rWO[0]])
                for m in range(KC):
                    if m + 1 < KC:
                        k.dma("pool", WO[(m + 1) % 2][:], wov[:, :, (m + 1) * 128:(m + 2) * 128], w=[rWO[(m + 1) % 2]])
                    i = m % 2
                    for th in range(2):
                        sl = slice(th * 512, (th + 1) * 512)
                        b, rb = nb()
                        for kc in range(KC):
                            k.op("pe", lambda e: e.matmul(b[:], WO[i][:, kc, :], H[:, kc, sl], start=(kc == 0), stop=(kc == KC - 1)),
                                 r=[rWO[i], rH[kc]], w=[rb])
                        k.op("dve", lambda e: e.tensor_tensor(out=X[:, m, sl], in0=b[:], in1=X[:, m, sl], op=ALU.add),
                             r=[rb, rX[m]], w=[rX[m]])
                k.barrier()

        def win_phase():
            wiv = w_in.rearrange("(kc p) n -> p kc n", p=128)
            nch = (DIN + 127) // 128
            with ExitStack() as st2:
                WI = [st2.enter_context(nc.sbuf_tensor("WI%d_b%d_%d" % (i, blk[0], l * 10 + do_C), [128, KC, 128], BF16)) for i in range(2)]
                rWI = [Res() for i in range(2)]
                stg = [st2.enter_context(nc.sbuf_tensor("stg%d_b%d_%d" % (i, blk[0], l * 10 + do_C), [128, 512], F32)) for i in range(4)]
                r_stg = [Res() for i in range(4)]

                def loadj(j):
                    cw = min(128, DIN - j * 128)
                    k.dma("pool", WI[j % 2][:, :, 0:cw], wiv[:, :, j * 128:j * 128 + cw], w=[rWI[j % 2]])

                loadj(0)
                it = 0
                for j in range(nch):
                    if j + 1 < nch:
                        loadj(j + 1)
                    cw = min(128, DIN - j * 128)
                    i = j % 2
                    for th in range(2):
                        sl = slice(th * 512, (th + 1) * 512)
                        b, rb = nb()
                        for kc in range(KC):
                            k.op("pe", lambda e: e.matmul(b[0:cw, :], WI[i][:, kc, 0:cw], H[:, kc, sl], start=(kc == 0), stop=(kc == KC - 1)),
                                 r=[rWI[i], rH[kc]], w=[rb])
                        s_, rs_ = stg[it % 4], r_stg[it % 4]
                        if it % 2 == 0:
                            k.op("act", lambda e: e.copy(out=s_[0:cw, :], in_=b[0:cw, :]), r=[rb], w=[rs_])
                        else:
                            k.op("dve", lambda e: e.tensor_copy(out=s_[0:cw, :], in_=b[0:cw, :]), r=[rb], w=[rs_])
                        it += 1
                        k.dma("sp", uT[blk[0]][j * 128:j * 128 + cw, sl], s_[0:cw, :], r=[rs_])
                k.barrier()

        for b_ in range(NBLK):
            blk[0] = b_
            for c0 in range(0, KC, 4):
                k.dma("sp", X[:, c0:c0 + 4, :], xvs[b_][:, c0:c0 + 4, :], w=rX[c0:c0 + 4])
            if do_C:
                c_phase()
                rmsnorm("ffn2_norm")
                ffn(ffn2_w13, ffn2_w2)
            if do_A:
                rmsnorm("ffn1_norm")
                ffn(ffn1_w13, ffn1_w2)
                x1v = x1T[blk[0] * D:(blk[0] + 1) * D, :].rearrange("(c p) t -> p c t", p=128)
                for c0 in range(0, KC, 4):
                    k.dma("sp", x1v[:, c0:c0 + 4, :], X[:, c0:c0 + 4, :], r=rX[c0:c0 + 4])
                rmsnorm("mix_norm")
                win_phase()
            if do_final:
                rmsnorm("final_norm", to_x=True)
                ov = outT[blk[0] * D:(blk[0] + 1) * D, :].rearrange("(c p) t -> p c t", p=128)
                for c0 in range(0, KC, 4):
                    k.dma("sp", ov[:, c0:c0 + 4, :], X[:, c0:c0 + 4, :], r=rX[c0:c0 + 4])

            k.barrier()
        k.barrier()


T = 4096
LN_EPS = 1e-5
GN_EPS = 64e-5


def emit_conv(E, l, blk):
    nc, k = E.nc, E.k
    NTk = 1024
    PADT = NTk + 30
    q = blk % 4
    u3 = Blk3(E.uT)
    m3 = E.mixT.rearrange("(b r) t -> b r t", r=3072)
    cw, pb = E.conv_cw[l], E.conv_pb[l]
    with ExitStack() as st:
        uid = E.uid

        def ksb(name, shape, dt):
            uid[0] += 1
            return st.enter_context(nc.sbuf_tensor("%s_u%d" % (name, uid[0]), shape, dt))
        CW = ksb("CW", [128, 8, 31], F32)
        PB = ksb("PB", [128, 3, 8], F32)
        r_par = Res()
        k.dma("sp", CW[:], cw, w=[r_par])
        k.dma("sp", PB[:], pb, w=[r_par])
        ones = ksb("ones", [128, 128], BF16)
        r_ones = Res()
        k.op("dve", lambda e: e.memset(ones[:], 1.0), w=[r_ones])
        CO = ksb("CO", [128, 8, NTk], F32)
        rCO = [Res() for c in range(8)]
        A = [ksb("A%d" % i, [128, PADT], F32) for i in range(2)]
        Gt = [ksb("G%d" % i, [128, PADT], F32) for i in range(2)]
        rA = [Res() for i in range(2)]
        rG = [Res() for i in range(2)]
        banks, r_bank = E.banks, E.r_bank
        for c in range(8):
            i = c % 2
            for (dst, rdst, r0) in ((A[i], rA[i], c * 128), (Gt[i], rG[i], 1024 + c * 128)):
                if q == 0:
                    k.op("pool", lambda e: e.memset(dst[:, 0:30], 0.0), w=[rdst])
                else:
                    k.dma("sp", dst[:, 0:30], u3[blk - 1, r0:r0 + 128, NTk - 30:NTk], w=[rdst])
                k.dma("sp", dst[:, 30:PADT], u3[blk, r0:r0 + 128, :], w=[rdst])
            k.op("act", lambda e: e.activation(out=Gt[i][:], in_=Gt[i][:], func=AF.Sigmoid), r=[rG[i]], w=[rG[i]])
            eng = "dve"
            k.op(eng, lambda e: e.tensor_tensor(out=A[i][:], in0=A[i][:], in1=Gt[i][:], op=ALU.mult), r=[rA[i], rG[i]], w=[rA[i]])
            k.op(eng, lambda e: e.tensor_scalar(out=CO[:, c, :], in0=A[i][:, 0:NTk], scalar1=CW[:, c, 0:1], scalar2=PB[:, 0, c:c + 1],
                                                op0=ALU.mult, op1=ALU.add), r=[rA[i], r_par], w=[rCO[c]])
            for j in range(1, 31):
                k.op(eng, lambda e: e.scalar_tensor_tensor(out=CO[:, c, :], in0=A[i][:, j:j + NTk], scalar=CW[:, c, j:j + 1],
                                                           in1=CO[:, c, :], op0=ALU.mult, op1=ALU.add),
                     r=[rA[i], r_par, rCO[c]], w=[rCO[c]])
        xb = [ksb("xb%d" % i, [128, 512], BF16) for i in range(2)]
        x2 = [ksb("x2%d" % i, [128, 512], BF16) for i in range(2)]
        r_xb = [Res() for i in range(2)]
        r_x2 = [Res() for i in range(2)]
        mean = ksb("mean", [128, 512], F32)
        rstd = ksb("rstd", [128, 512], F32)
        msq = ksb("msq", [128, 512], F32)
        r_mean, r_rstd, r_msq = Res(), Res(), Res()
        t1 = [ksb("t1%d" % i, [128, 512], F32) for i in range(2)]
        r_t1 = [Res() for i in range(2)]
        og = [ksb("og%d" % i, [128, 512], F32) for i in range(2)]
        r_og = [Res() for i in range(2)]
        for th in range(2):
            sl = slice(th * 512, (th + 1) * 512)
            b1, rb1 = banks[2 * th], r_bank[2 * th]
            b2, rb2 = banks[2 * th + 1], r_bank[2 * th + 1]
            for c in range(8):
                i = c % 2
                k.op("act", lambda e: e.copy(out=xb[i][:], in_=CO[:, c, sl]), r=[rCO[c]], w=[r_xb[i]])
                k.op("act", lambda e: e.activation(out=x2[i][:], in_=CO[:, c, sl], func=AF.Square), r=[rCO[c]], w=[r_x2[i]])
                k.op("pe", lambda e: e.matmul(b1[:], ones[:], xb[i][:], start=(c == 0), stop=(c == 7)), r=[r_xb[i], r_ones], w=[rb1])
                k.op("pe", lambda e: e.matmul(b2[:], ones[:], x2[i][:], start=(c == 0), stop=(c == 7)), r=[r_x2[i], r_ones], w=[rb2])
            k.op("dve", lambda e: e.tensor_scalar(out=mean[:], in0=b1[:], scalar1=1.0 / 1024, scalar2=None, op0=ALU.mult), r=[rb1], w=[r_mean])
            k.op("dve", lambda e: e.tensor_tensor(out=msq[:], in0=mean[:], in1=mean[:], op=ALU.mult), r=[r_mean], w=[r_msq])
            k.op("dve", lambda e: e.scalar_tensor_tensor(out=rstd[:], in0=b2[:], scalar=1.0 / 1024, in1=msq[:], op0=ALU.mult, op1=ALU.subtract),
                 r=[rb2, r_msq], w=[r_rstd])
            k.op("dve", lambda e: e.tensor_scalar(out=rstd[:], in0=rstd[:], scalar1=LN_EPS, scalar2=None, op0=ALU.add), r=[r_rstd], w=[r_rstd])
            k.op("act", lambda e: e.activation(out=rstd[:], in_=rstd[:], func=AF.Sqrt), r=[r_rstd], w=[r_rstd])
            k.op("dve", lambda e: e.reciprocal(out=rstd[:], in_=rstd[:]), r=[r_rstd], w=[r_rstd])
            for c in range(8):
                i = c % 2
                k.op("dve", lambda e: e.tensor_tensor(out=t1[i][:], in0=CO[:, c, sl], in1=mean[:], op=ALU.subtract), r=[rCO[c], r_mean], w=[r_t1[i]])
                k.op("dve", lambda e: e.tensor_tensor(out=t1[i][:], in0=t1[i][:], in1=rstd[:], op=ALU.mult), r=[r_t1[i], r_rstd], w=[r_t1[i]])
                k.op("act", lambda e: e.activation(out=og[i][:], in_=t1[i][:], func=AF.Silu, scale=PB[:, 1, c:c + 1], bias=PB[:, 2, c:c + 1]),
                     r=[r_t1[i], r_par], w=[r_og[i]])
                k.dma("sp", m3[blk, c * 128:(c + 1) * 128, sl], og[i][:], r=[r_og[i]])
        k.barrier()


def conv_params(p):
    def pc(v): return np.ascontiguousarray(v.reshape(8, 128).T)
    cw = np.ascontiguousarray(p['conv_w'].T.reshape(8, 128, 31).transpose(1, 0, 2))
    pb = np.ascontiguousarray(np.stack([pc(p['conv_b']), pc(p['conv_ln_g']), pc(p['conv_ln_b'])], axis=1))
    return cw, pb


def emit_rwkv(E, l, bq, q):
    nc, k = E.nc, E.k
    u3 = Blk3(E.uT)
    m3 = E.mixT.rearrange("(b r) t -> b r t", r=3072)
    mu, prm = E.rw_mu[l, q], E.rw_prm[l, q]
    w_b, a_b, g_b = E.rw_wb[l, q], E.rw_ab[l, q], E.rw_gb[l, q]
    ident, bones = E.ident, E.bones
    tokS = E.tokS
    r_tok = Res()
    c0_ = q * 256
    rowbase = [2048 + c0_, 2048 + c0_ + 128, 3072 + c0_, 3072 + c0_ + 128, 4096 + c0_, 4096 + c0_ + 128,
               2048 + 3072 + 192, 2048 + 3072 + 192 + 128, 2048 + 3072, 2048 + 3072 + 96]
    with ExitStack() as st:
        uid = E.uid

        def ksb(name, shape, dt):
            uid[0] += 1
            return st.enter_context(nc.sbuf_tensor("%s_u%d" % (name, uid[0]), shape, dt))
        banks, r_bank, bank_i = E.banks, E.r_bank, E.bank_i

        def nb():
            i = bank_i[0] % 8
            bank_i[0] += 1
            return banks[i], r_bank[i]
        MU = ksb("MU", [128, 10], F32)
        PRM = ksb("PRM", [128, 2, 7], F32)
        IDN = ksb("IDN", [128, 128], F32)
        BON = ksb("BON", [128, 128], BF16)
        WB = ksb("WB", [96, 256], BF16)
        AB = ksb("AB", [96, 256], BF16)
        GB = ksb("GB", [128, 2, 256], BF16)
        r_c = Res()
        k.dma("sp", MU[:], mu, w=[r_c])
        k.dma("sp", PRM[:], prm, w=[r_c])
        k.dma("sp", IDN[:], ident, w=[r_c])
        k.dma("pool", BON[:], bones, w=[r_c])
        k.dma("pool", WB[:], w_b, w=[r_c])
        k.dma("pool", AB[:], a_b, w=[r_c])
        k.dma("pool", GB[:], g_b.rearrange("(kc p) n -> p kc n", p=128), w=[r_c])
        V = ksb("V", [128, 2, T], F32)
        rV = [Res(), Res()]
        RKB = ksb("RKB", [128, 2, T], BF16)
        rRKB = [Res(), Res()]
        SGL = ksb("SGL", [128, 2, T], BF16)
        rSGL = [Res(), Res()]

        with ExitStack() as st2:
            def sb2(name, shape, dt):
                uid[0] += 1
                return st2.enter_context(nc.sbuf_tensor("%s_u%d" % (name, uid[0]), shape, dt))
            ld = [sb2("ldc%d" % i, [128, T], F32) for i in range(1)]
            lp = [sb2("ldp%d" % i, [128, T], F32) for i in range(1)]
            r_ld = [Res(), Res()]
            r_lp = [Res(), Res()]
            ldi = [0]

            def shifted(rowtile, nrows, out, r_out_, post=None):
                i = 0
                r0 = rowbase[rowtile]
                k.op("pool", lambda e: e.memset(lp[i][0:nrows, 0:1], 0.0), w=[r_lp[i]])
                for tb in range(4):
                    k.dma("sp", ld[i][0:nrows, tb * 1024:(tb + 1) * 1024], u3[bq * 4 + tb, r0:r0 + nrows, :], w=[r_ld[i]])
                    n_ = 1024 if tb < 3 else 1023
                    k.dma("act", lp[i][0:nrows, tb * 1024 + 1:tb * 1024 + 1 + n_], u3[bq * 4 + tb, r0:r0 + nrows, 0:n_], w=[r_lp[i]])
                k.op("pool", lambda e: e.tensor_tensor(out=lp[i][0:nrows, :], in0=lp[i][0:nrows, :], in1=ld[i][0:nrows, :], op=ALU.subtract),
                     r=[r_ld[i], r_lp[i]], w=[r_lp[i]])
                if post is None:
                    k.op("dve", lambda e: e.scalar_tensor_tensor(out=out, in0=lp[i][0:nrows, :], scalar=MU[0:nrows, rowtile:rowtile + 1],
                                                                 in1=ld[i][0:nrows, :], op0=ALU.mult, op1=ALU.add),
                         r=[r_ld[i], r_lp[i], r_c], w=[r_out_])
                else:
                    k.op("dve", lambda e: e.scalar_tensor_tensor(out=ld[i][0:nrows, :], in0=lp[i][0:nrows, :], scalar=MU[0:nrows, rowtile:rowtile + 1],
                                                                 in1=ld[i][0:nrows, :], op0=ALU.mult, op1=ALU.add),
                         r=[r_ld[i], r_lp[i], r_c], w=[r_ld[i]])
                    k.op("act", lambda e: e.activation(out=out, in_=ld[i][0:nrows, :], func=post), r=[r_ld[i]], w=[r_out_])

            WL = sb2("WL", [96, T], BF16)
            AL = sb2("AL", [96, T], BF16)
            r_WL, r_AL = Res(), Res()
            shifted(8, 96, WL[:], r_WL, post=AF.Tanh)
            shifted(9, 96, AL[:], r_AL, post=AF.Copy)
            for ct in range(2):
                shifted(6 + ct, 128, SGL[:, ct, :], rSGL[ct], post=AF.Sigmoid)
                shifted(4 + ct, 128, V[:, ct, :], rV[ct])
            Rt = sb2("Rt", [128, T], F32)
            Kt = sb2("Kt", [128, T], F32)
            Wt = sb2("Wt", [128, T], F32)
            At = sb2("At", [128, T], F32)
            KKt = sb2("KKt", [128, T], F32)
            SQb = sb2("SQb", [128, 512], BF16)
            r_R, r_K, r_W, r_A, r_KK, r_SQ = Res(), Res(), Res(), Res(), Res(), Res()
            stg = [sb2("stg%d" % i, [128, 4, 128], F32) for i in range(2)]
            r_stg = [Res(), Res()]
            stg_i = [0]

            def to_tok(src, r_src, vec, ct):
                for t0 in range(0, 32, 4):
                    b, rb = nb()
                    for a in range(4):
                        tt = t0 + a
                        k.op("pe", lambda e: e.transpose(b[:, a * 128:(a + 1) * 128], src[:, tt * 128:(tt + 1) * 128], IDN[:]),
                             r=[r_src, r_c], w=[rb])
                    i = stg_i[0] % 2
                    stg_i[0] += 1
                    k.op("act", lambda e: e.copy(out=stg[i][:].rearrange("p a c -> p (a c)"), in_=b[:]), r=[rb], w=[r_stg[i]])
                    for hp in range(2):
                        dst = tokS[hp, vec, t0 * 128:(t0 + 4) * 128, ct * 64:(ct + 1) * 64].rearrange("(a p) c -> p a c", p=128)
                        k.dma("sp", dst, stg[i][:, :, hp * 64:(hp + 1) * 64], r=[r_stg[i]], w=[r_tok])

            for ct in range(2):
                shifted(0 + ct, 128, Rt[:], r_R)
                shifted(2 + ct, 128, Kt[:], r_K)
                for tb in range(8):
                    sl = slice(tb * 512, (tb + 1) * 512)
                    b, rb = nb()
                    k.op("pe", lambda e: e.matmul(b[:], WB[:, ct * 128:(ct + 1) * 128], WL[:, sl], start=True, stop=True), r=[r_WL, r_c], w=[rb])
                    k.op("act", lambda e: e.activation(out=Wt[:, sl], in_=b[:], func=AF.Sigmoid, bias=PRM[:, ct, 0:1]), r=[rb, r_c], w=[r_W])
                    b2, rb2 = nb()
                    k.op("pe", lambda e: e.matmul(b2[:], AB[:, ct * 128:(ct + 1) * 128], AL[:, sl], start=True, stop=True), r=[r_AL, r_c], w=[rb2])
                    k.op("act", lambda e: e.activation(out=At[:, sl], in_=b2[:], func=AF.Sigmoid, bias=PRM[:, ct, 1:2]), r=[rb2, r_c], w=[r_A])
                k.op("act", lambda e: e.activation(out=Wt[:], in_=Wt[:], func=AF.Exp, scale=-0.6065306597), r=[r_W], w=[r_W])
                to_tok(Wt, r_W, 1, ct)
                to_tok(Rt, r_R, 4, ct)
                k.op("dve", lambda e: e.tensor_scalar(out=KKt[:], in0=Kt[:], scalar1=PRM[:, ct, 2:3], scalar2=None, op0=ALU.mult), r=[r_K, r_c], w=[r_KK])
                k.op("dve", lambda e: e.tensor_scalar(out=Wt[:], in0=At[:], scalar1=-1.0, scalar2=PRM[:, ct, 3:4], op0=ALU.add, op1=ALU.mult),
                     r=[r_A, r_c], w=[r_W])
                k.op("dve", lambda e: e.scalar_tensor_tensor(out=Kt[:], in0=Wt[:], scalar=1.0, in1=Kt[:], op0=ALU.add, op1=ALU.mult),
                     r=[r_W, r_K], w=[r_K])
                to_tok(Kt, r_K, 3, ct)
                k.op("dve", lambda e: e.scalar_tensor_tensor(out=RKB[:, ct, :], in0=Rt[:], scalar=PRM[:, ct, 4:5], in1=Kt[:], op0=ALU.mult, op1=ALU.mult),
                     r=[r_R, r_K, r_c], w=[rRKB[ct]])
                for tb in range(8):
                    sl = slice(tb * 512, (tb + 1) * 512)
                    k.op("act", lambda e: e.activation(out=SQb[:], in_=KKt[:, sl], func=AF.Square), r=[r_KK], w=[r_SQ])
                    b, rb = nb()
                    k.op("pe", lambda e: e.matmul(b[:], BON[:], SQb[:], start=True, stop=True), r=[r_SQ, r_c], w=[rb])
                    k.op("dve", lambda e: e.tensor_scalar(out=Rt[:, sl], in0=b[:], scalar1=1e-12, scalar2=None, op0=ALU.add), r=[rb, r_R], w=[r_R])
                k.op("act", lambda e: e.activation(out=Rt[:], in_=Rt[:], func=AF.Sqrt), r=[r_R], w=[r_R])
                k.op("dve", lambda e: e.reciprocal(out=Rt[:], in_=Rt[:]), r=[r_R], w=[r_R])
                k.op("dve", lambda e: e.scalar_tensor_tensor(out=KKt[:], in0=KKt[:], scalar=-1.0, in1=Rt[:], op0=ALU.mult, op1=ALU.mult),
                     r=[r_KK, r_R], w=[r_KK])
                to_tok(KKt, r_KK, 0, ct)
                k.op("dve", lambda e: e.scalar_tensor_tensor(out=At[:], in0=KKt[:], scalar=-1.0, in1=At[:], op0=ALU.mult, op1=ALU.mult),
                     r=[r_KK, r_A], w=[r_A])
                to_tok(At, r_A, 2, ct)
            k.barrier()

        with ExitStack() as st2:
            def sb2(name, shape, dt):
                uid[0] += 1
                return st2.enter_context(nc.sbuf_tensor("%s_u%d" % (name, uid[0]), shape, dt))
            TB = 8
            RY = sb2("RY", [128, T, 2, 2], F32)
            r_Yall = Res()
            BCAR = [sb2("BCAR%d" % i, [128, TB, 2, 128], F32) for i in range(2)]
            rBCAR = [Res(), Res()]
            BC = [None] + [[sb2("BC%d_%d" % (v, i), [128, TB, 128], F32) for i in range(2)] for v in (1, 2, 3)]
            rBC = [None] + [[Res() for i in range(2)] for v in (1, 2, 3)]
            S = sb2("S", [128, 2, 64], F32)
            tmp = sb2("tmp", [128, 2, 64], F32)
            tmp2 = sb2("tmp2", [128, 2, 2, 64], F32)
            r_S, r_tmp, r_tmp2 = Res(), Res(), Res()
            k.op("dve", lambda e: e.memset(S[:], 0.0), w=[r_S])
            k.op("pool", lambda e: e.memset(BCAR[0][:], 0.0), w=[rBCAR[0]])
            k.op("pool", lambda e: e.memset(BCAR[1][:], 0.0), w=[rBCAR[1]])
            nblk = T // TB

            def loadblk(bi):
                i = bi % 2
                t0_ = bi * TB
                na = min(TB, T - 1 - t0_)
                for hp in range(2):
                    ps_ = slice(hp * 64, (hp + 1) * 64)
                    if na > 0:
                        k.dma("sp", BCAR[i][ps_, 0:na, 0, :], tokS[hp, 0, t0_ + 1:t0_ + 1 + na, :].partition_broadcast(64), r=[r_tok], w=[rBCAR[i]])
                    k.dma("act", BCAR[i][ps_, :, 1, :], tokS[hp, 4, t0_:t0_ + TB, :].partition_broadcast(64), r=[r_tok], w=[rBCAR[i]])
                    for v in (1, 2, 3):
                        k.dma("sp" if (v + hp) % 2 == 0 else "act", BC[v][i][ps_, :, :], tokS[hp, v, t0_:t0_ + TB, :].partition_broadcast(64),
                              r=[r_tok], w=[rBC[v][i]])

            loadblk(0)
            k.noself = {"dve"}
            for bi in range(nblk):
                if bi + 1 < nblk:
                    loadblk(bi + 1)
                i = bi % 2
                for ct_ in range(2):
                    kvv = BC[3][i][:, :, ct_ * 64:(ct_ + 1) * 64]
                    k.op("dve", lambda e: e.tensor_tensor(out=kvv, in0=kvv, in1=V[:, ct_, bi * TB:(bi + 1) * TB].unsqueeze(2).to_broadcast([128, TB, 64]),
                                                          op=ALU.mult), r=[rV[0], rV[1], rBC[3][i]], w=[rBC[3][i]])
                for tl in range(TB):
                    t = bi * TB + tl

                    def bc(v):
                        return BC[v][i][:, tl, :].rearrange("p (c j) -> p c j", c=2)
                    k.op("dve", lambda e: e.tensor_tensor(out=S[:], in0=S[:], in1=bc(1), op=ALU.mult), r=[r_S, rBC[1][i]], w=[r_S])
                    if t > 0:
                        k.op("dve", lambda e: e.tensor_tensor(out=tmp[:], in0=bc(2), in1=RY[:, t - 1, 0, :].unsqueeze(2).to_broadcast([128, 2, 64]), op=ALU.mult),
                             r=[r_Yall, rBC[2][i]], w=[r_tmp])
                        k.op("dve", lambda e: e.tensor_tensor(out=S[:], in0=S[:], in1=tmp[:], op=ALU.add), r=[r_S, r_tmp], w=[r_S])
                    k.op("dve", lambda e: e.tensor_tensor(out=S[:], in0=S[:], in1=bc(3), op=ALU.add), r=[r_S, rBC[3][i]], w=[r_S])
                    k.op("dve", lambda e: e.tensor_tensor(out=tmp2[:], in0=BCAR[i][:, tl, :, :].rearrange("p w (c j) -> p w c j", c=2),
                                                          in1=S[:].unsqueeze(1).to_broadcast([128, 2, 2, 64]), op=ALU.mult),
                         r=[r_S, rBCAR[i]], w=[r_tmp2])
                    k.op("dve", lambda e: e.tensor_reduce(out=RY[:, t, :, :], in_=tmp2[:], axis=AX.X, op=ALU.add), r=[r_tmp2], w=[r_Yall])

            k.noself = set()
            yb = sb2("yb", [128, 512], BF16)
            y2 = sb2("y2", [128, 512], BF16)
            mean = sb2("mean", [128, 512], F32)
            msq = sb2("msq", [128, 512], F32)
            rstd = sb2("rstd", [128, 512], F32)
            t1 = sb2("t1", [128, 512], F32)
            t2 = sb2("t2", [128, 512], F32)
            og = [sb2("og%d" % i, [128, 512], F32) for i in range(2)]
            r_yb, r_y2, r_mean, r_msq, r_rstd, r_t1, r_t2 = Res(), Res(), Res(), Res(), Res(), Res(), Res()
            r_og = [Res(), Res()]
            it = 0
            for ct in range(2):
                for tb in range(8):
                    sl = slice(tb * 512, (tb + 1) * 512)
                    k.op("act", lambda e: e.copy(out=yb[:], in_=RY[:, sl, 1, ct]), r=[r_Yall], w=[r_yb])
                    k.op("act", lambda e: e.activation(out=y2[:], in_=RY[:, sl, 1, ct], func=AF.Square), r=[r_Yall], w=[r_y2])
                    b1, rb1 = nb()
                    b2, rb2 = nb()
                    k.op("pe", lambda e: e.matmul(b1[:], BON[:], yb[:], start=True, stop=True), r=[r_yb, r_c], w=[rb1])
                    k.op("pe", lambda e: e.matmul(b2[:], BON[:], y2[:], start=True, stop=True), r=[r_y2, r_c], w=[rb2])
                    k.op("dve", lambda e: e.tensor_scalar(out=mean[:], in0=b1[:], scalar1=1.0 / 64, scalar2=None, op0=ALU.mult), r=[rb1], w=[r_mean])
                    k.op("dve", lambda e: e.tensor_tensor(out=msq[:], in0=mean[:], in1=mean[:], op=ALU.mult), r=[r_mean], w=[r_msq])
                    k.op("dve", lambda e: e.scalar_tensor_tensor(out=rstd[:], in0=b2[:], scalar=1.0 / 64, in1=msq[:], op0=ALU.mult, op1=ALU.subtract),
                         r=[rb2, r_msq], w=[r_rstd])
                    k.op("dve", lambda e: e.tensor_scalar(out=rstd[:], in0=rstd[:], scalar1=GN_EPS, scalar2=None, op0=ALU.add), r=[r_rstd], w=[r_rstd])
                    k.op("act", lambda e: e.activation(out=rstd[:], in_=rstd[:], func=AF.Sqrt), r=[r_rstd], w=[r_rstd])
                    k.op("dve", lambda e: e.reciprocal(out=rstd[:], in_=rstd[:]), r=[r_rstd], w=[r_rstd])
                    k.op("dve", lambda e: e.tensor_tensor(out=t1[:], in0=RY[:, sl, 1, ct], in1=mean[:], op=ALU.subtract), r=[r_Yall, r_mean], w=[r_t1])
                    k.op("dve", lambda e: e.tensor_tensor(out=t1[:], in0=t1[:], in1=rstd[:], op=ALU.mult), r=[r_t1, r_rstd], w=[r_t1])
                    k.op("act", lambda e: e.activation(out=t1[:], in_=t1[:], func=AF.Identity, scale=PRM[:, ct, 5:6], bias=PRM[:, ct, 6:7]),
                         r=[r_t1, r_c], w=[r_t1])
                    b3, rb3 = nb()
                    k.op("pe", lambda e: e.matmul(b3[:], BON[:], RKB[:, ct, sl], start=True, stop=True), r=[rRKB[ct], r_c], w=[rb3])
                    k.op("dve", lambda e: e.tensor_tensor(out=t2[:], in0=b3[:], in1=V[:, ct, sl], op=ALU.mult), r=[rb3, rV[ct]], w=[r_t2])
                    k.op("dve", lambda e: e.tensor_tensor(out=t1[:], in0=t1[:], in1=t2[:], op=ALU.add), r=[r_t1, r_t2], w=[r_t1])
                    b4, rb4 = nb()
                    for kc in range(2):
                        k.op("pe", lambda e: e.matmul(b4[:], GB[:, kc, ct * 128:(ct + 1) * 128], SGL[:, kc, sl], start=(kc == 0), stop=(kc == 1)),
                             r=[rSGL[kc], r_c], w=[rb4])
                    i = it % 2
                    it += 1
                    k.op("dve", lambda e: e.tensor_tensor(out=og[i][:], in0=b4[:], in1=t1[:], op=ALU.mult), r=[rb4, r_t1], w=[r_og[i]])
                    k.dma("sp", m3[bq * 4 + tb // 2, 1024 + q * 256 + ct * 128:1024 + q * 256 + (ct + 1) * 128, (tb % 2) * 512:(tb % 2 + 1) * 512], og[i][:], r=[r_og[i]])
            k.barrier()


def rwkv_params(p, q):
    c0 = q * 256
    cols = np.concatenate([np.arange(c0, c0 + 256), 1024 + np.arange(c0, c0 + 256), 2048 + np.arange(c0, c0 + 256),
                           3072 + 192 + np.arange(256), 3072 + np.arange(96), 3072 + 96 + np.arange(96)])
    mu_sel = p['rwkv_mu'][cols]
    mu = np.zeros((128, 10), np.float32)
    starts = [0, 128, 256, 384, 512, 640, 768, 896, 1024, 1120]
    sizes = [128] * 8 + [96, 96]
    for i, (s, n) in enumerate(zip(starts, sizes)):
        mu[:n, i] = mu_sel[s:s + n]
    prm = np.zeros((128, 2, 7), np.float32)
    for ct in range(2):
        sl = slice(c0 + ct * 128, c0 + (ct + 1) * 128)
        prm[:, ct, 0] = p['rwkv_w0'][sl]
        prm[:, ct, 1] = p['rwkv_a0'][sl]
        prm[:, ct, 2] = p['rwkv_k_k'][sl]
        prm[:, ct, 3] = p['rwkv_k_a'][sl]
        prm[:, ct, 4] = p['rwkv_r_k'].reshape(-1)[sl]
        prm[:, ct, 5] = p['rwkv_lnx_g'][sl]
        prm[:, ct, 6] = p['rwkv_lnx_b'][sl]
    return (mu, prm, np.ascontiguousarray(p['rwkv_w_b'][:, c0:c0 + 256]), np.ascontiguousarray(p['rwkv_a_b'][:, c0:c0 + 256]),
            np.ascontiguousarray(p['rwkv_g_b'][:, c0:c0 + 256]))
NEG = -30000.0


def emit_nsa(E, l, bq, q):
    nc, k = E.nc, E.k
    u3 = Blk3(E.uT)
    m3 = E.mixT.rearrange("(b r) t -> b r t", r=3072)
    pe = E.nsa_pe[l]
    w1k, w1v, w2k, w2v = E.w["nsa_ck_w1"][l], E.w["nsa_cv_w1"][l], E.w["nsa_ck_w2"][l], E.w["nsa_cv_w2"][l]
    ident, mdiag, mold, Eall = E.ident, E.mdiag, E.mold, E.Eall
    cmaskK, cmaskT, impmul, impadd, jok = E.cmaskK, E.cmaskT, E.impmul, E.impadd, E.jok
    qrow = 5568 + q * 256
    kvrow = [6592 + i_ * 256 + q * 64 for i_ in range(6)]
    grow = 8128 + q * 12
    with ExitStack() as st:
        uid = E.uid

        def ksb(name, shape, dt):
            uid[0] += 1
            return st.enter_context(nc.sbuf_tensor("%s_u%d" % (name, uid[0]), shape, dt))
        banks, r_bank = E.banks, E.r_bank
        r_c = Res()
        Q = ksb("Q", [64, 4, T], BF16)
        r_Q = Res()
        KS = ksb("KS", [64, T], BF16)
        KW = ksb("KW", [64, T], BF16)
        KC = ksb("KC", [64, 256], BF16)
        VS1 = ksb("VS1", [128, 32, 65], BF16)
        VW1 = ksb("VW1", [128, 32, 65], BF16)
        VC1 = ksb("VC1", [128, 2, 65], BF16)
        r_KC, r_VC = Res(), Res()
        GA = ksb("GA", [128, 32, 12], F32)
        r_GA = Res()
        IDF = ksb("IDF", [128, 128], F32)
        IDB = ksb("IDB", [128, 128], BF16)
        MD = ksb("MD", [128, 512], BF16)
        MO = ksb("MO", [128, 512], BF16)
        EA = ksb("EA", [64, 32, 128], BF16)
        IMUL = ksb("IMUL", [128, 32, 64], F32)
        IADD = ksb("IADD", [128, 32, 64], F32)
        JOK = ksb("JOK", [128, 32, 64], F32)
        OUT = ksb("OUT", [128, 32, 256], F32)
        r_OUT = Res()
        r_KS, r_KW = Res(), Res()
        for tb in range(4):
            k.dma("pool", KS[:, tb * 1024:(tb + 1) * 1024], u3[bq * 4 + tb, kvrow[2]:kvrow[2] + 64, :], w=[r_KS])
            k.dma("pool", KW[:, tb * 1024:(tb + 1) * 1024], u3[bq * 4 + tb, kvrow[4]:kvrow[4] + 64, :], w=[r_KW])
        k.dma("sp", IDF[:], ident, w=[r_c])
        k.dma("pool", IDB[:], ident, w=[r_c])
        k.dma("pool", MD[:], mdiag, w=[r_c])
        k.dma("pool", MO[:], mold, w=[r_c])
        k.dma("pool", EA[:], Eall, w=[r_c])
        k.dma("sp", IMUL[:], impmul, w=[r_c])
        k.dma("sp", IADD[:], impadd, w=[r_c])
        k.dma("sp", JOK[:], jok, w=[r_c])

        with ExitStack() as st2:
            def sb2(name, shape, dt):
                uid[0] += 1
                return st2.enter_context(nc.sbuf_tensor("%s_u%d" % (name, uid[0]), shape, dt))
            qs = sb2("qs", [64, 4, 1024], F32)
            r_qs = Res()
            for tq in range(4):
                for g in range(4):
                    k.dma("sp", qs[:, g, :], u3[bq * 4 + tq, qrow + g * 64:qrow + (g + 1) * 64, :], w=[r_qs])
                k.op("act", lambda e: e.mul(out=Q[:, :, tq * 1024:(tq + 1) * 1024], in_=qs[:], mul=0.125), r=[r_qs], w=[r_Q])
            vT = sb2("vT", [64, T], F32)
            r_vT = Res()
            k.op("dve", lambda e: e.memset(VS1[:], 1.0), w=[r_c])
            k.op("dve", lambda e: e.memset(VW1[:], 1.0), w=[r_c])
            for (row, V1_) in ((kvrow[3], VS1), (kvrow[5], VW1)):
                for tb in range(4):
                    k.dma("sp", vT[:, tb * 1024:(tb + 1) * 1024], u3[bq * 4 + tb, row:row + 64, :], w=[r_vT])
                for t4 in range(8):
                    bt_, rbt_ = banks[t4 % 2], r_bank[t4 % 2]
                    for a_ in range(4):
                        tt_ = t4 * 4 + a_
                        k.op("pe", lambda e: e.transpose(bt_[:, a_ * 64:(a_ + 1) * 64], vT[:, tt_ * 128:(tt_ + 1) * 128], IDF[0:64, 0:64]), r=[r_vT, r_c], w=[rbt_])
                    k.op("act", lambda e: e.copy(out=V1_[:, t4 * 4:(t4 + 1) * 4, 0:64], in_=bt_[:, 0:256].rearrange("p (a d) -> p a d", a=4)), r=[rbt_], w=[r_c])
            for tb in range(4):
                k.dma("sp", vT[0:12, tb * 1024:(tb + 1) * 1024], u3[bq * 4 + tb, grow:grow + 12, :], w=[r_vT])
            for t4 in range(8):
                bt_, rbt_ = banks[t4 % 2], r_bank[t4 % 2]
                for a_ in range(4):
                    tt_ = t4 * 4 + a_
                    k.op("pe", lambda e: e.transpose(bt_[:, a_ * 12:(a_ + 1) * 12], vT[0:12, tt_ * 128:(tt_ + 1) * 128], IDF[0:12, 0:12]), r=[r_vT, r_c], w=[rbt_])
                k.op("act", lambda e: e.activation(out=GA[:, t4 * 4:(t4 + 1) * 4, :], in_=bt_[:, 0:48].rearrange("p (a d) -> p a d", a=4), func=AF.Sigmoid), r=[rbt_], w=[r_GA])
            k.barrier()
            PE_ = sb2("PE_", [128, 2, 16], F32)
            k.dma("sp", PE_[:], pe, w=[r_c])
            KCT2 = sb2("KCT2", [128, T], F32)
            BL = sb2("BL", [128, 16, 256], BF16)
            W1 = sb2("W1", [128, 16, 256], BF16)
            W2 = sb2("W2", [128, 2, 64], BF16)
            HID = sb2("HID", [128, 2, 256], BF16)
            xs = sb2("xs", [128, 256], F32)
            uu = sb2("uu", [128, 256], F32)
            sg = sb2("sg", [128, 256], F32)
            r_blk, r_BL, r_W1, r_W2, r_HID, r_xs, r_uu, r_sg = [Res() for _ in range(8)]
            k.op("dve", lambda e: e.memset(BL[:], 0.0), w=[r_BL])
            k.op("dve", lambda e: e.memset(VC1[:], 1.0), w=[r_VC])
            for which in range(2):
                srow, w1, w2 = ((kvrow[0], w1k, w2k), (kvrow[1], w1v, w2v))[which]
                k.op("pool", lambda e: e.memset(KCT2[64:128, T - 1:T], 0.0), w=[r_blk])
                for tb in range(4):
                    k.dma("sp", KCT2[0:64, tb * 1024:(tb + 1) * 1024], u3[bq * 4 + tb, srow:srow + 64, :], w=[r_blk])
                    if tb == 0:
                        k.dma("act", KCT2[64:128, 0:1023], u3[bq * 4, srow:srow + 64, 1:1024], w=[r_blk])
                    else:
                        k.dma("act", KCT2[64:128, tb * 1024 - 1:(tb + 1) * 1024 - 1], u3[bq * 4 + tb, srow:srow + 64, :], w=[r_blk])
                KCv = KCT2[:].rearrange("p (n s) -> p n s", s=16)
                k.dma("pool", W1[:], w1.rearrange("(kc p) n -> p kc n", p=128), w=[r_W1])
                k.dma("pool", W2[:], w2.rearrange("(kc p) n -> p kc n", p=128), w=[r_W2])
                for kc in range(16):
                    bsrc = KCv[:, 0:255, 2 * kc] if kc < 8 else KCv[:, 1:256, 2 * kc - 16]
                    k.op("dve", lambda e: e.tensor_scalar(out=BL[:, kc, 0:255], in0=bsrc, scalar1=PE_[:, which, kc:kc + 1], scalar2=None,
                                                          op0=ALU.add), r=[r_blk, r_c], w=[r_BL])
                for hc in range(2):
                    b, rb = banks[hc], r_bank[hc]
                    for kc in range(16):
                        k.op("pe", lambda e: e.matmul(b[:, 0:256], W1[:, kc, hc * 128:(hc + 1) * 128], BL[:, kc, :], start=(kc == 0), stop=(kc == 15)),
                             r=[r_W1, r_BL], w=[rb])
                    k.op("act", lambda e: e.copy(out=xs[:], in_=b[:, 0:256]), r=[rb], w=[r_xs])
                    k.op("dve", lambda e: e.tensor_tensor(out=uu[:], in0=xs[:], in1=xs[:], op=ALU.mult), r=[r_xs], w=[r_uu])
                    k.op("dve", lambda e: e.tensor_scalar(out=uu[:], in0=uu[:], scalar1=0.044715, scalar2=1.0, op0=ALU.mult, op1=ALU.add), r=[r_uu], w=[r_uu])
                    k.op("dve", lambda e: e.tensor_tensor(out=uu[:], in0=uu[:], in1=xs[:], op=ALU.mult), r=[r_uu, r_xs], w=[r_uu])
                    k.op("act", lambda e: e.activation(out=sg[:], in_=uu[:], func=AF.Sigmoid, scale=1.5957691216), r=[r_uu], w=[r_sg])
                    k.op("dve", lambda e: e.tensor_tensor(out=HID[:, hc, :], in0=xs[:], in1=sg[:], op=ALU.mult), r=[r_xs, r_sg], w=[r_HID])
                if which == 0:
                    b, rb = banks[2], r_bank[2]
                    for hc in range(2):
                        k.op("pe", lambda e: e.matmul(b[0:64, 0:256], W2[:, hc, :], HID[:, hc, :], start=(hc == 0), stop=(hc == 1)), r=[r_W2, r_HID], w=[rb])
                    k.op("act", lambda e: e.copy(out=KC[:], in_=b[0:64, 0:256]), r=[rb], w=[r_KC])
                else:
                    for c in range(2):
                        b, rb = banks[3 + c], r_bank[3 + c]
                        for hc in range(2):
                            k.op("pe", lambda e: e.matmul(b[:, 0:64], HID[:, hc, c * 128:(c + 1) * 128], W2[:, hc, :], start=(hc == 0), stop=(hc == 1)),
                                 r=[r_W2, r_HID], w=[rb])
                        k.op("act", lambda e: e.copy(out=VC1[:, c, 0:64], in_=b[:, 0:64]), r=[rb], w=[r_VC])
            k.barrier()

        CMT = [ksb("CMT%d" % i, [128, 256], F32) for i in range(2)]
        r_CMT = [Res(), Res()]
        CMK = [ksb("CMK%d" % i, [128, 512], BF16) for i in range(4)]
        r_CMK = [Res() for i in range(4)]
        sc = ksb("sc", [128, 4, 256], F32)
        r_sc = Res()
        den = ksb("den", [128, 4], F32)
        r_den = Res()
        PP = ksb("PP", [128, 264], F32)
        r_PP = Res()
        imp = ksb("imp", [128, 64], F32)
        imp3 = ksb("imp3", [128, 64], F32)
        m8 = ksb("m8", [128, 8], F32)
        thr = ksb("thr", [128, 1], F32)
        sel = ksb("sel", [128, 64], F32)
        r_imp, r_imp3, r_m8, r_thr, r_sel = [Res() for _ in range(5)]
        SELB = ksb("SELB", [64, 4, 128], BF16)
        r_SELB = Res()
        PT = [ksb("PT%d" % i, [128, 4, 128], BF16) for i in range(3)]
        r_PT = [Res() for i in range(3)]
        cf = ksb("cf", [128, 4], F32)
        tmpo = ksb("tmpo", [128, 4, 64], F32)
        r_cf, r_tmpo = Res(), Res()
        k.op("dve", lambda e: e.memset(PP[:], 0.0), w=[r_PP])
        pt_i = [0]
        st_i = [0]
        cmk_i = [0]

        def attend(qi, kT, kblocks, V1, v_of, Ob, rOb, masks):
            t0 = qi * 128
            nk = len(kblocks)
            for ii, kb in enumerate(kblocks):
                si = 3 + (st_i[0] % 2)
                st_i[0] += 1
                S_, rS_ = banks[si], r_bank[si]
                ml = masks(kb)
                k.op("pe", lambda e: e.matmul(S_[:], kT[:, kb * 128:(kb + 1) * 128], Q[:, :, t0:t0 + 128], start=True, stop=(len(ml) == 0)),
                     r=[r_c, r_Q, r_KC, r_KS, r_KW], w=[rS_])
                for mi, (ml_l, ml_r, ml_res) in enumerate(ml):
                    k.op("pe", lambda e: e.matmul(S_[:], ml_l, ml_r, start=False, stop=(mi == len(ml) - 1)), r=[r_c] + ml_res, w=[rS_])
                pi = pt_i[0] % 3
                pt_i[0] += 1
                k.op("act", lambda e: e.activation(out=PT[pi][:].rearrange("p g t -> p (g t)"), in_=S_[:], func=AF.Exp), r=[rS_], w=[r_PT[pi]])
                for g in range(4):
                    k.op("pe", lambda e: e.matmul(Ob[:, g * 65:(g + 1) * 65], PT[pi][:, g, :], v_of(kb), start=(ii == 0 and g == 0), stop=(ii == nk - 1), skip_group_check=True),
                         r=[r_PT[pi], r_c, r_VC], w=[rOb])

        def combine(qi, Ob, rOb, c, first):
            Ov = Ob[:, 0:260].rearrange("p (g e) -> p g e", g=4)
            k.op("dve", lambda e: e.tensor_scalar(out=cf[:].unsqueeze(2), in0=Ov[:, :, 64:65], scalar1=1e-30, scalar2=None, op0=ALU.max), r=[rOb], w=[r_cf])
            k.op("dve", lambda e: e.reciprocal(out=cf[:], in_=cf[:]), r=[r_cf], w=[r_cf])
            gsl = GA[:, qi, :].rearrange("p (g c) -> p g c", c=3)[:, :, c]
            k.op("dve", lambda e: e.tensor_tensor(out=cf[:], in0=cf[:], in1=gsl, op=ALU.mult), r=[r_cf, r_GA], w=[r_cf])
            ov = OUT[:, qi, :].rearrange("p (g d) -> p g d", g=4)
            if first:
                k.op("dve", lambda e: e.tensor_tensor(out=ov, in0=Ov[:, :, 0:64], in1=cf[:].unsqueeze(2).to_broadcast([128, 4, 64]), op=ALU.mult),
                     r=[rOb, r_cf], w=[r_OUT])
            else:
                k.op("dve", lambda e: e.tensor_tensor(out=tmpo[:], in0=Ov[:, :, 0:64], in1=cf[:].unsqueeze(2).to_broadcast([128, 4, 64]), op=ALU.mult),
                     r=[rOb, r_cf], w=[r_tmpo])
                k.op("dve", lambda e: e.tensor_tensor(out=ov, in0=ov, in1=tmpo[:], op=ALU.add), r=[r_tmpo, r_OUT], w=[r_OUT])

        for qi in range(32):
            t0 = qi * 128
            ci = qi % 2
            k.dma("sp", CMT[ci][:], cmaskT[qi], w=[r_CMT[ci]])
            for h2 in range(2):
                b, rb = banks[h2], r_bank[h2]
                for gg in range(2):
                    g = h2 * 2 + gg
                    k.op("pe", lambda e: e.matmul(b[:, gg * 256:(gg + 1) * 256], Q[:, g, t0:t0 + 128], KC[:], start=True, stop=True), r=[r_Q, r_KC], w=[rb])
                k.op("dve", lambda e: e.tensor_tensor(out=sc[:, h2 * 2:h2 * 2 + 2, :], in0=b[:].rearrange("p (g n) -> p g n", g=2),
                                                      in1=CMT[ci][:].unsqueeze(1).to_broadcast([128, 2, 256]), op=ALU.add), r=[rb, r_CMT[ci]], w=[r_sc])
            k.op("act", lambda e: e.activation(out=sc[:], in_=sc[:], func=AF.Exp), r=[r_sc], w=[r_sc])
            k.op("dve", lambda e: e.tensor_reduce(out=den[:], in_=sc[:], axis=AX.X, op=ALU.add), r=[r_sc], w=[r_den])
            k.op("dve", lambda e: e.tensor_scalar(out=den[:], in0=den[:], scalar1=1e-30, scalar2=None, op0=ALU.max), r=[r_den], w=[r_den])
            k.op("dve", lambda e: e.reciprocal(out=den[:], in_=den[:]), r=[r_den], w=[r_den])
            k.op("dve", lambda e: e.tensor_tensor(out=sc[:], in0=sc[:], in1=den[:].unsqueeze(2).to_broadcast([128, 4, 256]), op=ALU.mult), r=[r_sc, r_den], w=[r_sc])
            k.op("dve", lambda e: e.tensor_reduce(out=PP[:, 4:260], in_=sc[:].rearrange("p g n -> p n g"), axis=AX.X, op=ALU.add), r=[r_sc], w=[r_PP])
            PPv = PP[:].rearrange("p (j f) -> p j f", f=4)
            k.op("dve", lambda e: e.tensor_reduce(out=imp[:], in_=PPv[:, 1:65, :], axis=AX.X, op=ALU.add), r=[r_PP], w=[r_imp])
            k.op("dve", lambda e: e.tensor_tensor(out=imp[:], in0=imp[:], in1=PPv[:, 0:64, 3], op=ALU.add), r=[r_PP, r_imp], w=[r_imp])
            k.op("dve", lambda e: e.tensor_tensor(out=imp[:], in0=imp[:], in1=IMUL[:, qi, :], op=ALU.mult), r=[r_imp, r_c], w=[r_imp])
            k.op("dve", lambda e: e.tensor_tensor(out=imp[:], in0=imp[:], in1=IADD[:, qi, :], op=ALU.add), r=[r_imp, r_c], w=[r_imp])
            k.op("dve", lambda e: e.max(out=m8[:], in_=imp[:]), r=[r_imp], w=[r_m8])
            k.op("dve", lambda e: e.match_replace(out=imp3[:], in_to_replace=m8[:], in_values=imp[:], imm_value=-3.0e38), r=[r_imp, r_m8], w=[r_imp3])
            k.op("dve", lambda e: e.max(out=m8[:], in_=imp3[:]), r=[r_imp3], w=[r_m8])
            k.op("dve", lambda e: e.tensor_reduce(out=thr[:], in_=m8[:], axis=AX.X, op=ALU.min), r=[r_m8], w=[r_thr])
            k.op("dve", lambda e: e.tensor_scalar(out=sel[:], in0=imp[:], scalar1=thr[:], scalar2=None, op0=ALU.is_ge), r=[r_imp, r_thr], w=[r_sel])
            k.op("dve", lambda e: e.tensor_tensor(out=sel[:], in0=sel[:], in1=JOK[:, qi, :], op=ALU.mult), r=[r_sel, r_c], w=[r_sel])
            k.op("dve", lambda e: e.tensor_scalar(out=sel[:], in0=sel[:], scalar1=-1.0, scalar2=-NEG, op0=ALU.add, op1=ALU.mult), r=[r_sel], w=[r_sel])
            bt, rbt = banks[2], r_bank[2]
            k.op("pe", lambda e: e.transpose(bt[0:64, 0:128], sel[:], IDF[:]), r=[r_sel, r_c], w=[rbt])
            for g in range(4):
                k.op("act", lambda e: e.copy(out=SELB[:, g, :], in_=bt[0:64, 0:128]), r=[rbt], w=[r_SELB])
            cchunks = [0] + ([1] if qi >= 16 else [])
            cm_of = {}
            for c in cchunks:
                i = cmk_i[0] % 4
                cmk_i[0] += 1
                k.dma("pool", CMK[i][:], cmaskK[qi, c], w=[r_CMK[i]])
                cm_of[c] = i
            attend(qi, KC, cchunks, VC1, lambda c: VC1[:, c, :], banks[5], r_bank[5],
                   lambda c: [(IDB[:], CMK[cm_of[c]][:], [r_CMK[cm_of[c]]])])
            combine(qi, banks[5], r_bank[5], 0, True)
            attend(qi, KS, list(range(qi + 1)), VS1, lambda kb: VS1[:, kb, :], banks[6], r_bank[6],
                   lambda kb: [(EA[:, kb, :], SELB[:].rearrange("j g t -> j (g t)"), [r_SELB])] + ([(IDB[:], MD[:], [])] if kb == qi else []))
            combine(qi, banks[6], r_bank[6], 1, False)
            attend(qi, KW, list(range(max(0, qi - 4), qi + 1)), VW1, lambda kb: VW1[:, kb, :], banks[7], r_bank[7],
                   lambda kb: ([(IDB[:], MD[:], [])] if kb == qi else []) + ([(IDB[:], MO[:], [])] if kb == qi - 4 else []))
            combine(qi, banks[7], r_bank[7], 2, False)
        ost = [ksb("ost%d" % i, [128, 512], F32) for i in range(2)]
        r_ost = [Res(), Res()]
        oi = 0
        for ct in range(2):
            for t4 in range(8):
                bt_, rbt_ = banks[oi % 2], r_bank[oi % 2]
                for a_ in range(4):
                    tt_ = t4 * 4 + a_
                    k.op("pe", lambda e: e.transpose(bt_[:, a_ * 128:(a_ + 1) * 128], OUT[:, tt_, ct * 128:(ct + 1) * 128], IDF[:]), r=[r_OUT, r_c], w=[rbt_])
                k.op("act", lambda e: e.copy(out=ost[oi % 2][:], in_=bt_[:]), r=[rbt_], w=[r_ost[oi % 2]])
                k.dma("sp", m3[bq * 4 + t4 // 2, 2048 + q * 256 + ct * 128:2048 + q * 256 + (ct + 1) * 128, (t4 % 2) * 512:(t4 % 2 + 1) * 512],
                      ost[oi % 2][:], r=[r_ost[oi % 2]])
                oi += 1
        k.barrier()


_NSA_CONST = {}


def nsa_consts():
    if _NSA_CONST:
        return _NSA_CONST
    s = np.arange(128)[:, None]
    t = np.arange(128)[None, :]
    md = np.where(s <= t, 0.0, NEG).astype(np.float32)
    mo = np.where(s > t, 0.0, NEG).astype(np.float32)
    E = np.zeros((64, 32, 128), np.float32)
    for kb in range(32):
        for sl in range(128):
            E[2 * kb + sl // 64, kb, sl] = 1.0
    cK = np.full((32, 2, 128, 512), NEG, np.float32)
    cT = np.full((32, 128, 256), NEG, np.float32)
    for qi in range(32):
        tt = qi * 128 + np.arange(128)
        n = np.arange(256)
        ok = (16 * n[None, :] + 31 <= tt[:, None]) & (n[None, :] < 255)
        cT[qi] = np.where(ok, 0.0, NEG)
        for c in range(2):
            okc = ok[:, c * 128:(c + 1) * 128].T
            cK[qi, c] = np.tile(np.where(okc, 0.0, NEG), (1, 4))
    tpos = np.arange(T)
    cur = tpos // 64
    j = np.arange(64)[None, :]
    curc = cur[:, None]
    forced = (j == 0) | (j == curc) | (j == curc - 1)
    fut = j > curc
    mul = np.where(forced | fut, 0.0, 1.0).astype(np.float32)
    add = np.zeros((T, 64), np.float32)
    add = np.where(j == curc - 1, 1e30, add)
    add = np.where(j == curc, 2e30, add)
    add = np.where(j == 0, 3e30, add)
    add = np.where(fut, -1e30, add).astype(np.float32)
    okj = (~fut).astype(np.float32)

    def tm(a):
        return np.ascontiguousarray(a.reshape(32, 128, 64).transpose(1, 0, 2))
    _NSA_CONST.update({"ident": np.eye(128, dtype=np.float32), "mdiag": np.tile(md, (1, 4)), "mold": np.tile(mo, (1, 4)), "Eall": E,
                       "cmaskK": cK, "cmaskT": cT, "impmul": tm(mul), "impadd": tm(add), "jok": tm(okj)})
    return _NSA_CONST


class Env:
    pass


class Blk3:
    def __init__(self, lst):
        self.lst = lst

    def __getitem__(self, key):
        return self.lst[key[0]][key[1], key[2]]


NBATCH_PER_CORE = 1
_WNAMES = ["ffn1_w13", "ffn1_w2", "w_in", "w_conv_out", "w_rwkv_out", "w_nsa_out", "w_out", "ffn2_w13", "ffn2_w2",
           "nsa_ck_w1", "nsa_ck_w2", "nsa_cv_w1", "nsa_cv_w2"]
_WSHAPES = {"ffn1_w13": [2, D, 2 * DFF], "ffn1_w2": [2, DFF, D], "w_in": [2, D, DIN], "w_conv_out": [2, 1024, D],
            "w_rwkv_out": [2, 1024, D], "w_nsa_out": [2, 1024, D], "w_out": [2, D, D], "ffn2_w13": [2, D, 2 * DFF],
            "ffn2_w2": [2, DFF, D], "nsa_ck_w1": [2, 2048, 256], "nsa_ck_w2": [2, 256, 64], "nsa_cv_w1": [2, 2048, 256],
            "nsa_cv_w2": [2, 256, 64]}
_CSHAPES = {"gains": [128, 7, 16], "conv_cw": [2, 128, 8, 31], "conv_pb": [2, 128, 3, 8], "rw_mu": [2, 4, 128, 10],
            "rw_prm": [2, 4, 128, 2, 7], "rw_wb": [2, 4, 96, 256], "rw_ab": [2, 4, 96, 256], "rw_gb": [2, 4, 256, 256],
            "ident": [128, 128], "bones": [128, 128], "nsa_pe": [2, 128, 2, 16], "mdiag": [128, 512], "mold": [128, 512],
            "Eall": [64, 32, 128], "cmaskK": [32, 2, 128, 512], "cmaskT": [32, 128, 256], "impmul": [128, 32, 64],
            "impadd": [128, 32, 64], "jok": [128, 32, 64]}


def build_fused(NBATCH):
    nc = bass.Bass("TRN2", target_bir_lowering=False)
    NBLK = 4 * NBATCH

    def din(name, shape):
        return nc.dram_tensor(name, list(shape), F32, kind="ExternalInput").ap()
    E = Env()
    E.nc, E.NBLK = nc, NBLK
    xT = din("xT", [NBLK * D, NT])
    E.outT = nc.dram_tensor("outT", [NBLK * D, NT], F32, kind="ExternalOutput").ap()
    E.w = {nm: din(nm, _WSHAPES[nm]) for nm in _WNAMES}
    cst = {nm: din(nm, sh) for nm, sh in _CSHAPES.items()}
    E.gains_d = cst["gains"]
    E.conv_cw, E.conv_pb = cst["conv_cw"], cst["conv_pb"]
    E.rw_mu, E.rw_prm, E.rw_wb, E.rw_ab, E.rw_gb = cst["rw_mu"], cst["rw_prm"], cst["rw_wb"], cst["rw_ab"], cst["rw_gb"]
    E.ident, E.bones, E.nsa_pe = cst["ident"], cst["bones"], cst["nsa_pe"]
    E.mdiag, E.mold, E.Eall, E.cmaskK, E.cmaskT = cst["mdiag"], cst["mold"], cst["Eall"], cst["cmaskK"], cst["cmaskT"]
    E.impmul, E.impadd, E.jok = cst["impmul"], cst["impadd"], cst["jok"]
    E.uT = [nc.dram_tensor("uT_s%d" % i, [DIN, NT], F32).ap() for i in range(NBLK)]
    E.x1T = nc.dram_tensor("x1T_s", [NBLK * D, NT], F32).ap()
    E.mixT = nc.dram_tensor("mixT_s", [NBLK * 3072, NT], F32).ap()
    E.tokS = nc.dram_tensor("tokS_s", [2, 5, T, 128], F32).ap()
    with ExitStack() as st:
        k = KB(nc, st)
        E.k = k
        E.banks = [k.ps("bank%d" % i, [128, 512], F32) for i in range(8)]
        E.r_bank = [Res() for i in range(8)]
        E.bank_i = [0]
        E.uid = [0]
        emit_tl(E, False, True, False, 0, xT)
        for l in range(2):
            for blk in range(NBLK):
                emit_conv(E, l, blk)
            for b in range(NBATCH):
                for q in range(4):
                    emit_rwkv(E, l, b, q)
                    emit_nsa(E, l, b, q)
            if l == 0:
                emit_tl(E, True, True, False, 0, E.x1T)
            else:
                emit_tl(E, True, False, True, 1, E.x1T)
        k.barrier()
        print("fused instructions:", k.ninst)
    return nc


_PROG = {}


def _g16(v):
    return np.ascontiguousarray(np.asarray(v, np.float32).reshape(16, 128).T)


def kernel(**inp):
    inp = {k_: np.asarray(v_, np.float32) for k_, v_ in inp.items()}
    NB = NBATCH_PER_CORE
    ncores = 2 // NB
    if "nc" not in _PROG:
        _PROG["nc"] = build_fused(NB)
    nc = _PROG["nc"]
    x = inp["x"]
    xT = np.ascontiguousarray(x.reshape(8, 1024, 2048).transpose(0, 2, 1)).reshape(8 * 2048, 1024)
    m = {nm: inp[nm] for nm in _WNAMES}
    gains = np.zeros((128, 7, 16), np.float32)
    for l in range(2):
        gains[:, 3 * l] = _g16(inp["ffn1_norm"][l])
        gains[:, 3 * l + 1] = _g16(inp["mix_norm"][l])
        gains[:, 3 * l + 2] = _g16(inp["ffn2_norm"][l])
    gains[:, 6] = _g16(inp["final_norm"])
    m["gains"] = gains
    cw, pb, mu, prm, wb, ab, gb, pe = [], [], [], [], [], [], [], []
    for l in range(2):
        p = {k_: v_[l] for k_, v_ in inp.items() if k_ not in ("x", "final_norm")}
        c_, p_ = conv_params(p)
        cw.append(c_)
        pb.append(p_)
        rp = [rwkv_params(p, q) for q in range(4)]
        mu.append(np.stack([r_[0] for r_ in rp]))
        prm.append(np.stack([r_[1] for r_ in rp]))
        wb.append(np.stack([r_[2] for r_ in rp]))
        ab.append(np.stack([r_[3] for r_ in rp]))
        gb.append(np.stack([r_[4] for r_ in rp]))
        pe.append(np.stack([p["nsa_pe_k"].reshape(16, 128).T, p["nsa_pe_v"].reshape(16, 128).T], axis=1))
    m.update({"conv_cw": np.stack(cw), "conv_pb": np.stack(pb), "rw_mu": np.stack(mu), "rw_prm": np.stack(prm),
              "rw_wb": np.stack(wb), "rw_ab": np.stack(ab), "rw_gb": np.stack(gb), "nsa_pe": np.stack(pe)})
    bones = np.zeros((128, 128), np.float32)
    bones[:64, :64] = 1
    bones[64:, 64:] = 1
    m["bones"] = bones
    m.update(nsa_consts())
    m = {k_: np.ascontiguousarray(v_, dtype=np.float32) for k_, v_ in m.items()}
    maps = []
    per = 8 // ncores
    for c in range(ncores):
        mc = dict(m)
        mc["xT"] = np.ascontiguousarray(xT[c * per * 2048:(c + 1) * per * 2048])
        maps.append(mc)
    res = run_bass_kernel_spmd(nc, maps, core_ids=list(range(ncores))).results
    outT = np.concatenate([r_["outT"] for r_ in res], axis=0)
    out = np.ascontiguousarray(outT.reshape(8, 2048, 1024).transpose(0, 2, 1)).reshape(2, 4096, 2048)
    return out.astype(np.float32)
```
